# Optimizing a Trainium2 kernel written in Bass

```python
import jax, jax.numpy as jnp
from jax import lax
import numpy as np

D_MODEL = 1024
BATCH = 8
SEQ = 4096
DEPTH = 1
DEC_BATCH = 16
DEC_SEQ = 32
PAST_LEN = 1024

CHUNK = 64
Q_BLOCK = 128
N_ATTN_HEADS = 8
ATTN_HEAD_DIM = 64
D_ATTN = N_ATTN_HEADS * ATTN_HEAD_DIM
ATTN_SCALE = ATTN_HEAD_DIM ** -0.5
D_RNN = 512
N_RNN_BLOCKS = 8
RNN_BLOCK = D_RNN // N_RNN_BLOCKS
CONV_WIDTH = 4
LRU_C = 8.0
N_KEYS = 128
N_EXPERTS = N_KEYS * N_KEYS
PEER_HEADS = 8
PEER_TOPK = 16
PEER_KEY_DIM = 256
PEER_HALF = PEER_KEY_DIM // 2
PEER_TOKEN_BLOCK = 128
DN_ALPHA = (2 * DEPTH) ** 0.25
DN_BETA = (8 * DEPTH) ** -0.25
LN_EPS = 1e-5
IN_WIDTHS = (D_ATTN, D_ATTN, D_ATTN, N_ATTN_HEADS, D_RNN, D_RNN, D_MODEL, D_MODEL)
D_IN = sum(IN_WIDTHS)
IN_SPLITS = tuple(int(s) for s in np.cumsum(IN_WIDTHS)[:-1])

kernel_name = "fox_rglru_peer_streaming_step"


def layer_norm(x, g, b):
    xf = x.astype(jnp.float32)
    mu = jnp.mean(xf, axis=-1, keepdims=True)
    var = jnp.mean(jnp.square(xf - mu), axis=-1, keepdims=True)
    return ((xf - mu) * lax.rsqrt(var + LN_EPS) * g + b).astype(x.dtype)


def fox_block(q, k, v, f_q, f_k, q_pos, k_pos):
    s = jnp.einsum("bqhd,bkhd->bhqk", q, k).astype(jnp.float32) * ATTN_SCALE
    s = s + jnp.transpose(f_q, (0, 2, 1))[..., :, None] - jnp.transpose(f_k, (0, 2, 1))[..., None, :]
    mask = k_pos[None, :] <= q_pos[:, None]
    p = jax.nn.softmax(jnp.where(mask, s, -jnp.inf), axis=-1)
    return jnp.einsum("bhqk,bkhd->bqhd", p.astype(v.dtype), v)


def fox_attention(q, k, v, f_q, f_k, q_pos, k_pos):
    t_q = q.shape[1]
    if t_q <= Q_BLOCK:
        return fox_block(q, k, v, f_q, f_k, q_pos, k_pos)
    nb = t_q // Q_BLOCK
    b, _, h, d = q.shape
    qb = q.reshape(b, nb, Q_BLOCK, h, d).swapaxes(0, 1)
    fb = f_q.reshape(b, nb, Q_BLOCK, h).swapaxes(0, 1)
    pb = q_pos.reshape(nb, Q_BLOCK)
    out = lax.map(lambda a: fox_block(a[0], k, v, a[1], f_k, a[2], k_pos), (qb, fb, pb))
    return out.swapaxes(0, 1).reshape(b, t_q, h, d)


def _combine(c1, c2):
    a1, b1 = c1
    a2, b2 = c2
    return a1 * a2, a2 * b1 + b2


def linear_recurrence(a, b, h0):
    bsz, t, c = a.shape
    nc = -(-t // CHUNK)
    pad = nc * CHUNK - t
    a = jnp.pad(a, ((0, 0), (0, pad), (0, 0)), constant_values=1.0)
    b = jnp.pad(b, ((0, 0), (0, pad), (0, 0)))
    a = a.reshape(bsz, nc, CHUNK, c).swapaxes(0, 1)
    b = b.reshape(bsz, nc, CHUNK, c).swapaxes(0, 1)

    def step(h, ab):
        a_c, b_c = ab
        b_c = b_c.at[:, 0].add(a_c[:, 0] * h)
        _, h_c = lax.associative_scan(_combine, (a_c, b_c), axis=1)
        return h_c[:, -1], h_c

    h_last, hs = lax.scan(step, h0, (a, b))
    hs = hs.swapaxes(0, 1).reshape(bsz, nc * CHUNK, c)[:, :t]
    return hs, h_last


def recurrent_branch(xr, gate, conv_hist, h0, conv_w, conv_b, w_rg_a, b_rg_a, w_rg_x, b_rg_x, lru_lambda):
    bsz, t, _ = xr.shape
    xp = jnp.concatenate([conv_hist.astype(xr.dtype), xr], axis=1)
    win = jnp.stack([xp[:, i:i + t] for i in range(CONV_WIDTH)], axis=2)
    xc = jnp.einsum("btwc,wc->btc", win, conv_w) + conv_b
    new_hist = xp[:, xp.shape[1] - (CONV_WIDTH - 1):]
    xblk = xc.reshape(bsz, t, N_RNN_BLOCKS, RNN_BLOCK)
    r = jax.nn.sigmoid((jnp.einsum("btni,nij->btnj", xblk, w_rg_a).reshape(bsz, t, D_RNN) + b_rg_a).astype(jnp.float32))
    i_g = jax.nn.sigmoid((jnp.einsum("btni,nij->btnj", xblk, w_rg_x).reshape(bsz, t, D_RNN) + b_rg_x).astype(jnp.float32))
    log_a = -LRU_C * r * jax.nn.softplus(-lru_lambda.astype(jnp.float32))
    a = jnp.exp(log_a)
    mult = jnp.sqrt(-jnp.expm1(2.0 * log_a))
    bterm = mult * i_g * xc.astype(jnp.float32)
    hs, h_last = linear_recurrence(a, bterm, h0.astype(jnp.float32))
    out = hs.astype(xr.dtype) * jax.nn.gelu(gate)
    return out, new_hist, h_last


def token_mixer(x, past_k, past_v, past_logf, conv_hist, h0, w_in, b_forget, conv_w, conv_b,
                w_rg_a, b_rg_a, w_rg_x, b_rg_x, lru_lambda, w_attn_up, w_rnn_up, w_out):
    bsz, t, _ = x.shape
    z = x @ w_in
    q, k, v, f_logit, xr, gate, g_attn, g_rnn = jnp.split(z, IN_SPLITS, axis=-1)
    q = q.reshape(bsz, t, N_ATTN_HEADS, ATTN_HEAD_DIM)
    k = k.reshape(bsz, t, N_ATTN_HEADS, ATTN_HEAD_DIM)
    v = v.reshape(bsz, t, N_ATTN_HEADS, ATTN_HEAD_DIM)
    logf = jax.nn.log_sigmoid(f_logit.astype(jnp.float32) + b_forget.astype(jnp.float32))
    if past_k is None:
        k_all, v_all, logf_all, p = k, v, logf, 0
    else:
        k_all = jnp.concatenate([past_k.astype(k.dtype), k], axis=1)
        v_all = jnp.concatenate([past_v.astype(v.dtype), v], axis=1)
        logf_all = jnp.concatenate([past_logf.astype(jnp.float32), logf], axis=1)
        p = past_k.shape[1]
    f_all = jnp.cumsum(logf_all, axis=1)
    o = fox_attention(q, k_all, v_all, f_all[:, p:], f_all, p + jnp.arange(t), jnp.arange(p + t))
    rnn_out, new_hist, h_last = recurrent_branch(xr, gate, conv_hist, h0, conv_w, conv_b,
                                                 w_rg_a, b_rg_a, w_rg_x, b_rg_x, lru_lambda)
    merged = (jax.nn.sigmoid(g_attn) * (o.reshape(bsz, t, D_ATTN) @ w_attn_up)
              + jax.nn.sigmoid(g_rnn) * (rnn_out @ w_rnn_up))
    return merged @ w_out, k, v, logf, new_hist, h_last


def peer_block(xb, w_query, keys_1, keys_2, u_tab, v_tab):
    n = xb.shape[0]
    qy = (xb @ w_query).reshape(n, PEER_HEADS, 2, PEER_HALF)
    s1 = jnp.einsum("nhd,kd->nhk", qy[:, :, 0], keys_1).astype(jnp.float32)
    s2 = jnp.einsum("nhd,kd->nhk", qy[:, :, 1], keys_2).astype(jnp.float32)
    t1, i1 = lax.top_k(s1, PEER_TOPK)
    t2, i2 = lax.top_k(s2, PEER_TOPK)
    cand = (t1[..., :, None] + t2[..., None, :]).reshape(n, PEER_HEADS, PEER_TOPK * PEER_TOPK)
    cidx = (i1[..., :, None] * N_KEYS + i2[..., None, :]).reshape(n, PEER_HEADS, PEER_TOPK * PEER_TOPK)
    top, sel = lax.top_k(cand, PEER_TOPK)
    idx = jnp.take_along_axis(cidx, sel, axis=-1)
    g = jax.nn.softmax(top, axis=-1)
    u = jnp.take(u_tab, idx, axis=0)
    act = jax.nn.gelu(jnp.einsum("nhkd,nd->nhk", u, xb).astype(jnp.float32))
    vv = jnp.take(v_tab, idx, axis=0)
    return jnp.einsum("nhk,nhkd->nd", (g * act).astype(xb.dtype), vv)


def peer_ffn(x, w_query, keys_1, keys_2, u_tab, v_tab):
    bsz, t, d = x.shape
    n = bsz * t
    nb = -(-n // PEER_TOKEN_BLOCK)
    flat = jnp.pad(x.reshape(n, d), ((0, nb * PEER_TOKEN_BLOCK - n), (0, 0)))
    out = lax.map(lambda xb: peer_block(xb, w_query, keys_1, keys_2, u_tab, v_tab),
                  flat.reshape(nb, PEER_TOKEN_BLOCK, d))
    return out.reshape(nb * PEER_TOKEN_BLOCK, d)[:n].reshape(bsz, t, d)


def trunk_layer(x, past_k, past_v, past_logf, conv_hist, h0, w_in, b_forget, conv_w, conv_b,
                w_rg_a, b_rg_a, w_rg_x, b_rg_x, lru_lambda, w_attn_up, w_rnn_up, w_out, ln1_g, ln1_b,
                peer_w_query, peer_keys_1, peer_keys_2, peer_u, peer_v, ln2_g, ln2_b):
    mix, k, v, logf, conv_state, rnn_state = token_mixer(
        x, past_k, past_v, past_logf, conv_hist, h0, w_in, b_forget, conv_w, conv_b,
        w_rg_a, b_rg_a, w_rg_x, b_rg_x, lru_lambda, w_attn_up, w_rnn_up, w_out)
    h = layer_norm(DN_ALPHA * x + mix, ln1_g, ln1_b)
    y = layer_norm(DN_ALPHA * h + peer_ffn(h, peer_w_query, peer_keys_1, peer_keys_2, peer_u, peer_v), ln2_g, ln2_b)
    return y, k, v, logf, conv_state, rnn_state


def setup_inputs(seed: int = 0) -> dict:
    key = jax.random.key(seed)
    ks = jax.random.split(key, 32)
    nrm = jax.random.normal
    f32 = jnp.float32
    a0c = jax.random.uniform(ks[12], (DEPTH, D_RNN), f32, minval=0.9, maxval=0.999)
    s = a0c ** (1.0 / LRU_C)
    return {
        "x_prompt": nrm(ks[0], (BATCH, SEQ, D_MODEL), f32),
        "x_sample": nrm(ks[1], (DEC_BATCH, DEC_SEQ, D_MODEL), f32),
        "cache_k": nrm(ks[2], (DEPTH, DEC_BATCH, PAST_LEN, N_ATTN_HEADS, ATTN_HEAD_DIM), f32),
        "cache_v": nrm(ks[3], (DEPTH, DEC_BATCH, PAST_LEN, N_ATTN_HEADS, ATTN_HEAD_DIM), f32),
        "cache_logf": jax.nn.log_sigmoid(nrm(ks[4], (DEPTH, DEC_BATCH, PAST_LEN, N_ATTN_HEADS), f32) + 3.0),
        "state_conv": nrm(ks[5], (DEPTH, DEC_BATCH, CONV_WIDTH - 1, D_RNN), f32),
        "state_rnn": 0.5 * nrm(ks[6], (DEPTH, DEC_BATCH, D_RNN), f32),
        "w_in": nrm(ks[7], (DEPTH, D_MODEL, D_IN), f32) * D_MODEL ** -0.5,
        "b_forget": 3.0 + 0.5 * nrm(ks[8], (DEPTH, N_ATTN_HEADS), f32),
        "conv_w": nrm(ks[9], (DEPTH, CONV_WIDTH, D_RNN), f32) * CONV_WIDTH ** -0.5,
        "conv_b": 0.01 * nrm(ks[10], (DEPTH, D_RNN), f32),
        "w_rg_a": nrm(ks[11], (DEPTH, N_RNN_BLOCKS, RNN_BLOCK, RNN_BLOCK), f32) * RNN_BLOCK ** -0.5,
        "b_rg_a": 0.01 * nrm(ks[13], (DEPTH, D_RNN), f32),
        "w_rg_x": nrm(ks[14], (DEPTH, N_RNN_BLOCKS, RNN_BLOCK, RNN_BLOCK), f32) * RNN_BLOCK ** -0.5,
        "b_rg_x": 0.01 * nrm(ks[15], (DEPTH, D_RNN), f32),
        "lru_lambda": jnp.log(s) - jnp.log1p(-s),
        "w_attn_up": nrm(ks[16], (DEPTH, D_ATTN, D_MODEL), f32) * (D_ATTN ** -0.5 * DN_BETA),
        "w_rnn_up": nrm(ks[17], (DEPTH, D_RNN, D_MODEL), f32) * (D_RNN ** -0.5 * DN_BETA),
        "w_out": nrm(ks[18], (DEPTH, D_MODEL, D_MODEL), f32) * (D_MODEL ** -0.5 * DN_BETA),
        "ln1_g": 1.0 + 0.02 * nrm(ks[19], (DEPTH, D_MODEL), f32),
        "ln1_b": 0.02 * nrm(ks[20], (DEPTH, D_MODEL), f32),
        "peer_w_query": nrm(ks[21], (DEPTH, D_MODEL, PEER_HEADS * PEER_KEY_DIM), f32) * D_MODEL ** -0.5,
        "peer_keys_1": nrm(ks[22], (DEPTH, N_KEYS, PEER_HALF), f32) * PEER_HALF ** -0.5,
        "peer_keys_2": nrm(ks[23], (DEPTH, N_KEYS, PEER_HALF), f32) * PEER_HALF ** -0.5,
        "peer_u": nrm(ks[24], (DEPTH, N_EXPERTS, D_MODEL), f32) * D_MODEL ** -0.5,
        "peer_v": nrm(ks[25], (DEPTH, N_EXPERTS, D_MODEL), f32) * (DN_BETA * PEER_HEADS ** -0.5),
        "ln2_g": 1.0 + 0.02 * nrm(ks[26], (DEPTH, D_MODEL), f32),
        "ln2_b": 0.02 * nrm(ks[27], (DEPTH, D_MODEL), f32),
    }


def reference(x_prompt, x_sample, cache_k, cache_v, cache_logf, state_conv, state_rnn,
              w_in, b_forget, conv_w, conv_b, w_rg_a, b_rg_a, w_rg_x, b_rg_x, lru_lambda,
              w_attn_up, w_rnn_up, w_out, ln1_g, ln1_b, peer_w_query, peer_keys_1, peer_keys_2,
              peer_u, peer_v, ln2_g, ln2_b):
    hp, hs = x_prompt, x_sample
    kp_l, vp_l, fp_l, cp_l, rp_l = [], [], [], [], []
    ks_l, vs_l, fs_l, cs_l, rs_l = [], [], [], [], []
    for l in range(DEPTH):
        lw = (w_in[l], b_forget[l], conv_w[l], conv_b[l], w_rg_a[l], b_rg_a[l], w_rg_x[l], b_rg_x[l],
              lru_lambda[l], w_attn_up[l], w_rnn_up[l], w_out[l], ln1_g[l], ln1_b[l],
              peer_w_query[l], peer_keys_1[l], peer_keys_2[l], peer_u[l], peer_v[l], ln2_g[l], ln2_b[l])
        zero_hist = jnp.zeros((hp.shape[0], CONV_WIDTH - 1, D_RNN), hp.dtype)
        zero_h = jnp.zeros((hp.shape[0], D_RNN), jnp.float32)
        hp, kp, vp, fp, cp, rp = trunk_layer(hp, None, None, None, zero_hist, zero_h, *lw)
        hs, ks_, vs_, fs_, cs_, rs_ = trunk_layer(hs, cache_k[l], cache_v[l], cache_logf[l],
                                                  state_conv[l], state_rnn[l], *lw)
        kp_l.append(kp); vp_l.append(vp); fp_l.append(fp); cp_l.append(cp); rp_l.append(rp)
        ks_l.append(ks_); vs_l.append(vs_); fs_l.append(fs_); cs_l.append(cs_); rs_l.append(rs_)
    return (hp, hs,
            jnp.stack(kp_l), jnp.stack(vp_l), jnp.stack(fp_l), jnp.stack(cp_l), jnp.stack(rp_l),
            jnp.stack(ks_l), jnp.stack(vs_l), jnp.stack(fs_l), jnp.stack(cs_l), jnp.stack(rs_l))
```

```python
import contextlib
import os
import numpy as np
import concourse.bass as bass
import concourse.mybir as mybir
from concourse.bass_utils import run_bass_kernel_spmd

F32 = mybir.dt.float32
BF16 = mybir.dt.bfloat16
I32 = mybir.dt.int32
U32 = mybir.dt.uint32
AF = mybir.ActivationFunctionType
ALU = mybir.AluOpType
AX = mybir.AxisListType

ENGS = ("sync", "scalar", "vector", "gpsimd", "tensor")
NPOOL = 16

D_MODEL = 1024
D_IN = 4616
N_EXP = 16384
ALPHA = 2.0 ** 0.25
LN_EPS = 1e-5
GELU_C = 1.5957691216057308


class Op:
    __slots__ = ("eng", "fn", "dma", "deps", "needed", "sem", "val", "pre")

    def __init__(self, eng, fn, dma):
        self.eng = eng
        self.fn = fn
        self.dma = dma
        self.deps = ()
        self.needed = False
        self.sem = None
        self.val = 0
        self.pre = None


class Prog:
    def __init__(self, nc):
        self.nc = nc
        self.ops = {e: [] for e in ENGS}
        self.last_writer = {}
        self.readers = {}

    def op(self, eng, fn, reads=(), writes=(), dma=False):
        o = Op(eng, fn, dma)
        deps = set()
        for k in reads:
            w = self.last_writer.get(k)
            if w is not None:
                deps.add(w)
        for k in writes:
            w = self.last_writer.get(k)
            if w is not None:
                deps.add(w)
            for r in self.readers.get(k, ()):
                deps.add(r)
        if eng == "tensor" and not dma:
            deps = {d for d in deps if not (d.eng == "tensor" and not d.dma)}
        o.deps = deps
        for d in deps:
            d.needed = True
        if dma:
            o.needed = True
        for k in reads:
            self.readers.setdefault(k, []).append(o)
        for k in writes:
            self.last_writer[k] = o
            self.readers[k] = []
        self.ops[eng].append(o)
        return o

    def finish(self, semstack, tag):
        nc = self.nc
        esem = {e: semstack.enter_context(nc.semaphore("s%s_%s" % (tag, e))) for e in ENGS}
        pools = {e: [semstack.enter_context(nc.semaphore("d%s_%s_%d" % (tag, e, i))) for i in range(NPOOL)]
                 for e in ("sync", "scalar", "gpsimd") if any(o.dma for o in self.ops[e])}
        final = {}
        for e in ENGS:
            cnt = 0
            ndma = 0
            for o in self.ops[e]:
                if o.dma:
                    o.sem = pools[e][ndma % NPOOL]
                    o.val = 16 * (ndma // NPOOL + 1)
                    if ndma >= NPOOL:
                        o.pre = (o.sem, 16 * (ndma // NPOOL))
                    ndma += 1
                    final[id(o.sem)] = (o.sem, o.val)
                elif o.needed:
                    cnt += 1
                    o.sem = esem[e]
                    o.val = cnt
                    final[id(o.sem)] = (o.sem, o.val)
        for e in ENGS:
            for o in reversed(self.ops[e]):
                if not o.dma:
                    if not o.needed:
                        o.needed = True
                        o.sem = esem[e]
                        o.val = final.get(id(esem[e]), (None, 0))[1] + 1
                        final[id(o.sem)] = (o.sem, o.val)
                    break

        def make(e):
            def body(eng):
                waited = {}
                for o in self.ops[e]:
                    if o.pre is not None:
                        s, v = o.pre
                        if waited.get(id(s), 0) < v:
                            eng.wait_ge(s, v)
                            waited[id(s)] = v
                    for d in o.deps:
                        s, v = d.sem, d.val
                        if waited.get(id(s), 0) < v:
                            eng.wait_ge(s, v)
                            waited[id(s)] = v
                    ins = o.fn(eng)
                    if o.dma:
                        ins.then_inc(o.sem, 16)
                    elif o.needed:
                        ins.then_inc(o.sem, 1)
                for s, v in final.values():
                    if waited.get(id(s), 0) < v:
                        eng.wait_ge(s, v)
            return body

        with nc.Block() as block:
            block.sync(make("sync"))
            block.scalar(make("scalar"))
            block.vector(make("vector"))
            block.gpsimd(make("gpsimd"))
            block.tensor(make("tensor"))


class Em:
    def __init__(self, p):
        self.p = p

    def dma(self, eng, out, in_, r=(), w=(), **kw):
        return self.p.op(eng, lambda e: e.dma_start(out=out, in_=in_, **kw), r, w, dma=True)

    def gather(self, out, table, idx, r=(), w=()):
        return self.p.op("gpsimd", lambda e: e.indirect_dma_start(
            out=out, out_offset=None, in_=table,
            in_offset=bass.IndirectOffsetOnAxis(ap=idx, axis=0)), r, w, dma=True)

    def mm(self, out, lhsT, rhs, start, stop, r=(), w=()):
        return self.p.op("tensor", lambda e: e.matmul(out, lhsT=lhsT, rhs=rhs, start=start, stop=stop), r, w)

    def tr(self, out, in_, ident, r=(), w=()):
        return self.p.op("tensor", lambda e: e.transpose(out=out, in_=in_, identity=ident), r, w)

    def act(self, out, in_, func, r=(), w=(), bias=None, scale=None, accum=None, eng="scalar"):
        kw = {}
        if bias is not None:
            kw["bias"] = bias
        if scale is not None:
            kw["scale"] = scale
        if accum is not None:
            kw["accum_out"] = accum
        return self.p.op(eng, lambda e: e.activation(out=out, in_=in_, func=func, **kw), r, w)

    def copy(self, eng, out, in_, r=(), w=()):
        if eng == "scalar":
            return self.p.op(eng, lambda e: e.copy(out=out, in_=in_), r, w)
        return self.p.op(eng, lambda e: e.tensor_copy(out=out, in_=in_), r, w)

    def tt(self, eng, out, in0, in1, op, r=(), w=()):
        return self.p.op(eng, lambda e: e.tensor_tensor(out=out, in0=in0, in1=in1, op=op), r, w)

    def ts(self, eng, out, in0, s1, s2, op0, op1=None, r=(), w=()):
        if op1 is None:
            return self.p.op(eng, lambda e: e.tensor_scalar(out=out, in0=in0, scalar1=s1, scalar2=None, op0=op0), r, w)
        return self.p.op(eng, lambda e: e.tensor_scalar(out=out, in0=in0, scalar1=s1, scalar2=s2, op0=op0, op1=op1), r, w)

    def tss(self, eng, out, in_, scalar, op, r=(), w=()):
        return self.p.op(eng, lambda e: e.tensor_single_scalar(out=out, in_=in_, scalar=scalar, op=op), r, w)

    def stt(self, eng, out, in0, scalar, in1, op0, op1, r=(), w=(), accum=None):
        if accum is None:
            return self.p.op(eng, lambda e: e.scalar_tensor_tensor(out=out, in0=in0, scalar=scalar, in1=in1, op0=op0, op1=op1), r, w)
        return self.p.op(eng, lambda e: e.scalar_tensor_tensor(out=out, in0=in0, scalar=scalar, in1=in1, op0=op0, op1=op1, accum_out=accum), r, w)

    def memset(self, eng, ap, val, r=(), w=()):
        return self.p.op(eng, lambda e: e.memset(ap, val), r, w)

    def recip(self, out, in_, r=(), w=()):
        return self.p.op("vector", lambda e: e.reciprocal(out=out, in_=in_), r, w)

    def scan(self, out, d0, d1, init, r=(), w=()):
        return self.p.op("vector", lambda e: e.tensor_tensor_scan(out=out, data0=d0, data1=d1, initial=init,
                                                                  op0=ALU.mult, op1=ALU.add), r, w)

    def vmax(self, out, in_, r=(), w=()):
        return self.p.op("vector", lambda e: e.max(out=out, in_=in_), r, w)

    def vmaxidx(self, out, mx, vals, r=(), w=()):
        return self.p.op("vector", lambda e: e.max_index(out=out, in_max=mx, in_values=vals), r, w)

    def vmatchrep(self, out, mx, vals, imm, r=(), w=()):
        return self.p.op("vector", lambda e: e.match_replace(out=out, in_to_replace=mx, in_values=vals, imm_value=imm), r, w)

    def reduce(self, eng, out, in_, op, r=(), w=()):
        return self.p.op(eng, lambda e: e.tensor_reduce(out=out, in_=in_, axis=AX.X, op=op), r, w)


def build(T=4096, NS=2, TS=32, PAST=1024, NB=256, do_peer=True, debug_h=False, stop=99):
    nc = bass.Bass("TRN2", target_bir_lowering=False)

    def din(name, shape, dt=F32):
        return nc.dram_tensor(name, list(shape), dt, kind="ExternalInput").ap()

    def dout(name, shape, dt=F32):
        return nc.dram_tensor(name, list(shape), dt, kind="ExternalOutput").ap()

    def dscr(name, shape, dt):
        return nc.dram_tensor(name, list(shape), dt, kind="Internal").ap()

    NSR = NS * TS
    NROW = T + NSR
    KMAX = max(T, PAST + TS)
    NTK = (KMAX + 127) // 128
    NJ = max(1, NB // 128)

    xp = din("xp", [T, D_MODEL])
    xs = din("xs", [NSR, D_MODEL])
    ckT = din("ckT", [NS, 128, 4, PAST])
    cv = din("cv", [NS, PAST, 512])
    clf = din("clf", [NS, PAST, 8])
    sconvT = din("sconvT", [NS, 128, 4, 3])
    srnnT = din("srnnT", [NS, 128, 4])
    w_in = din("w_in", [D_MODEL, D_IN])
    bF = din("bF", [128, 8])
    cw = din("cw", [128, 4, 4])
    cb = din("cb", [128, 4])
    wa = din("wa", [128, 4, 128])
    ba = din("ba", [128, 4])
    wx = din("wx", [128, 4, 128])
    bx = din("bx", [128, 4])
    lam = din("lam", [128, 4])
    w_au = din("w_au", [512, D_MODEL])
    w_ru = din("w_ru", [512, D_MODEL])
    w_out = din("w_out", [D_MODEL, D_MODEL])
    ln1g = din("ln1g", [128, D_MODEL])
    ln1b = din("ln1b", [128, D_MODEL])
    wq = din("wq", [D_MODEL, 2048])
    k1T = din("k1T", [128, 128])
    k2T = din("k2T", [128, 128])
    pu = din("pu", [N_EXP, D_MODEL])
    pv = din("pv", [N_EXP, D_MODEL])
    ln2g = din("ln2g", [128, D_MODEL])
    ln2b = din("ln2b", [128, D_MODEL])
    c_ident = din("c_ident", [128, 128])
    c_tri = din("c_tri", [128, 128])
    c_mask = din("c_mask", [128, NJ, NB])
    c_iota = din("c_iota", [128, 16])

    y_p = dout("y_p", [T, D_MODEL])
    y_s = dout("y_s", [NSR, D_MODEL])
    k_p = dout("k_p", [T, 512])
    v_p = dout("v_p", [T, 512])
    lf_p = dout("lf_p", [T, 8])
    conv_p = dout("conv_p", [3, 512])
    rnn_p = dout("rnn_p", [1, 512])
    k_s = dout("k_s", [NSR, 512])
    v_s = dout("v_s", [NSR, 512])
    lf_s = dout("lf_s", [NSR, 8])
    conv_s = dout("conv_s", [NS, 3, 512])
    rnn_s = dout("rnn_s", [NS, 512])

    w_in_b = dscr("w_in_b", [D_MODEL, 4672], BF16)
    w_au_b = dscr("w_au_b", [512, D_MODEL], BF16)
    w_ru_b = dscr("w_ru_b", [512, D_MODEL], BF16)
    w_out_b = dscr("w_out_b", [D_MODEL, D_MODEL], BF16)
    wq_b = dscr("wq_b", [D_MODEL, 2048], BF16)
    h_scr = dout("h_scr", [NROW, D_MODEL]) if debug_h else dscr("h_scr", [NROW, D_MODEL], F32)
    uv_b = dscr("uv_b", [N_EXP, 2048], BF16)

    semstack = contextlib.ExitStack()
    with semstack:
        with contextlib.ExitStack() as st:
            def sb(name, shape, dt=F32):
                return st.enter_context(nc.sbuf_tensor(name, list(shape), dt))

            def ps(name, shape, dt=F32):
                return st.enter_context(nc.psum_tensor(name, list(shape), dt))

            p = Prog(nc)
            em = Em(p)

            ID = sb("ID", [128, 128])
            TRI = sb("TRI", [128, 128])
            ONESF = sb("ONESF", [128, 128])
            MASKF = sb("MASKF", [128, NJ, NB])
            MASK = sb("MASK", [128, NJ, NB], BF16)
            BFt = sb("BFt", [128, 8])
            CW = sb("CW", [128, 4, 4])
            CB = sb("CB", [128, 4])
            WA = sb("WA", [128, 4, 128])
            WX = sb("WX", [128, 4, 128])
            BA = sb("BA", [128, 4])
            BX = sb("BX", [128, 4])
            LAM = sb("LAM", [128, 4])
            C8 = sb("C8", [128, 4])
            C16 = sb("C16", [128, 4])
            LNG = sb("LNG", [128, D_MODEL])
            LNB = sb("LNB", [128, D_MODEL])
            WF = sb("WF", [128, 8, 8], BF16)

            KT = sb("KT", [128, 4, KMAX], BF16)
            VA = sb("VA", [128, NTK, 4, 192], BF16)
            LF = sb("LF", [128, NTK, 8])
            FF = sb("FF", [128, NTK, 8])
            BIAS = sb("BIAS", [128, NTK, 8])
            RT = sb("RT", [128, 8])
            FREF = sb("FREF", [128, 8])
            HIST = sb("HIST", [128, 4, 3])
            HST = sb("HST", [128, 4])

            NWS = 5
            WS = [sb("WS%d" % i, [128, 8, 512], BF16) for i in range(NWS)]
            XS = [sb("XS%d" % i, [128, D_MODEL]) for i in range(2)]
            XT = sb("XT", [128, 8, NB], BF16)
            QTP = sb("QTP", [128, 8, NB], BF16)
            KTOK = [sb("KTOK%d" % i, [128, 512]) for i in range(2)]
            VTOK = [sb("VTOK%d" % i, [128, 512]) for i in range(2)]
            LT = sb("LT", [128, 8])
            XRs = [sb("XR%d" % i, [128, NB + 3]) for i in range(2)]
            GGs = [sb("GG%d" % i, [128, NB]) for i in range(2)]
            XCs = [sb("XC%d" % i, [128, NB]) for i in range(2)]
            RR = sb("RR", [128, NB])
            II = sb("II", [128, NB])
            AA = sb("AA", [128, NB])
            BB = sb("BB", [128, NB])
            HS = sb("HS", [128, NB])
            T1 = sb("T1", [128, NB])
            T2 = sb("T2", [128, NB])
            ROT = sb("ROT", [128, 4, NB], BF16)
            NPB = 3
            PB = [sb("PB%d" % i, [128, NB], BF16) for i in range(NPB)]
            RECS = [sb("RECS%d" % i, [128, NB]) for i in range(1)]
            OT = sb("OT", [128, 4, NB], BF16)
            SA = sb("SA", [128, NB])
            SR = sb("SR", [128, NB])
            M1 = sb("M1", [128, NB])
            M2 = sb("M2", [128, NB])
            MT = sb("MT", [128, 8, NB], BF16)
            PRE = [sb("PRE%d" % i, [128, D_MODEL]) for i in range(2)]
            HO = [sb("HO%d" % i, [128, D_MODEL]) for i in range(2)]
            STAT = [sb("STAT%d" % i, [128, 8]) for i in range(2)]

            PZ = [ps("PZ%d" % i, [128, 512]) for i in range(2)]
            PSS = [ps("PSS%d" % i, [128, 512]) for i in range(3)]
            PO = [ps("PO%d" % i, [128, 512]) for i in range(2)]
            PX = ps("PX", [128, 512])
            if os.environ.get("KDEBUG"):
                print("phase1 sbuf remaining", nc.sbuf_bytes_remaining)

            for r0 in range(0, D_MODEL, 128):
                em.dma("gpsimd", w_in_b[r0:r0 + 128, 0:1536], w_in[r0:r0 + 128, 0:1536], w=[("w_in_b", r0 // 128, 0)])
                em.dma("gpsimd", w_in_b[r0:r0 + 128, 1536:4608], w_in[r0:r0 + 128, 1544:4616], w=[("w_in_b", r0 // 128, 1)])
                em.dma("gpsimd", w_in_b[r0:r0 + 128, 4608:4616], w_in[r0:r0 + 128, 1536:1544], w=[("w_in_b", r0 // 128, 2)])
            for r0 in range(0, 512, 128):
                em.dma("gpsimd", w_au_b[r0:r0 + 128, :], w_au[r0:r0 + 128, :], w=[("w_au_b", r0 // 128)])
                em.dma("gpsimd", w_ru_b[r0:r0 + 128, :], w_ru[r0:r0 + 128, :], w=[("w_ru_b", r0 // 128)])
            for r0 in range(0, D_MODEL, 128):
                em.dma("gpsimd", w_out_b[r0:r0 + 128, :], w_out[r0:r0 + 128, :], w=[("w_out_b", r0 // 128)])
            for r0 in range(0, D_MODEL, 128):
                em.dma("gpsimd", wq_b[r0:r0 + 128, :], wq[r0:r0 + 128, :], w=[("wq_b", r0 // 128)])
            W_IN_KEYS = [("w_in_b", i, j) for i in range(8) for j in range(3)]
            W_AU_KEYS = [("w_au_b", i) for i in range(4)]
            W_RU_KEYS = [("w_ru_b", i) for i in range(4)]
            W_OUT_KEYS = [("w_out_b", i) for i in range(8)]

            for (t_, d_, k_) in ((ID, c_ident, "ID"), (TRI, c_tri, "TRI"), (MASKF, c_mask, "MASKF"), (BFt, bF, "BFt"),
                                 (CW, cw, "CW"), (CB, cb, "CB"), (WA, wa, "WA"), (WX, wx, "WX"), (BA, ba, "BA"),
                                 (BX, bx, "BX"), (LAM, lam, "LAM"), (LNG, ln1g, "LNG"), (LNB, ln1b, "LNB")):
                em.dma("sync", t_[:], d_, w=[k_])
            em.memset("vector", ONESF[:], 1.0, w=["ONESF"])
            em.ts("vector", BA[:], BA[:], -1.0, None, ALU.mult, r=["BA"], w=["BA"])
            em.ts("vector", BX[:], BX[:], -1.0, None, ALU.mult, r=["BX"], w=["BX"])
            em.memset("vector", FF[:], 0.0, w=[("FF", kt) for kt in range(NTK)])
            em.memset("vector", LF[:], 0.0, w=[("LF", kt) for kt in range(NTK)])
            em.copy("vector", MASK[:], MASKF[:], r=["MASKF"], w=["MASK"])
            em.memset("vector", VA[:, :, :, 64:128], 1.0, w=["VA1"])
            em.memset("vector", QTP[:], 0.0, w=[("QTP", h_) for h_ in range(8)])
            em.act(C8[:], LAM[:], AF.Exp, scale=-1.0, r=["LAM"], w=["C8"])
            em.act(C8[:], C8[:], AF.Ln, bias=1.0, r=["C8"], w=["C8"])
            em.ts("vector", C16[:], C8[:], -16.0, None, ALU.mult, r=["C8"], w=["C16"])
            em.ts("vector", C8[:], C8[:], -8.0, None, ALU.mult, r=["C8", "C16"], w=["C8"])
            em.dma("sync", WF[:], w_in_b.rearrange("(kc p) c -> p kc c", p=128)[:, :, 4608:4616],
                   r=W_IN_KEYS, w=["WF"])

            ws_rr = [0]

            def load_w(src_ap, shape_part, key_reads):
                i = ws_rr[0] % NWS
                ws_rr[0] += 1
                dst = WS[i]
                npart, nk, ncol = shape_part
                em.dma("sync", dst[0:npart, 0:nk, 0:ncol], src_ap, r=key_reads, w=[("WS", i)])
                return dst, ("WS", i)

            w_in_v = w_in_b.rearrange("(kc p) c -> p kc c", p=128)
            w_out_v = w_out_b.rearrange("(kc p) c -> p kc c", p=128)
            w_ru_v = w_ru_b.rearrange("(kc p) c -> p kc c", p=128)
            w_au_v = w_au_b.rearrange("(kc p) c -> p kc c", p=128)

            rr = {"xs": 0, "pz": 0, "kt": 0, "pss": 0, "po": 0, "pb": 0, "bc": 0, "pre": 0, "rg": 0}

            def nxt(name, n):
                v = rr[name] % n
                rr[name] += 1
                return v

            def cumsum_tile(kt, rows):
                if os.environ.get("SKIP_CUMSUM"):
                    return
                em.mm(PX[0:rows, 8:16], TRI[0:rows, 0:rows], LF[0:rows, kt, :], True, True,
                      r=["TRI", ("LF", kt)], w=["PX"])
                em.mm(PX[:, 16:24], ONESF[0:rows, :], LF[0:rows, kt, :], True, True,
                      r=["ONESF", ("LF", kt)], w=["PX"])
                em.tt("vector", FF[0:rows, kt, :], PX[0:rows, 8:16], RT[0:rows, :], ALU.add,
                      r=["PX", "RT"], w=[("FF", kt)])
                em.tt("vector", RT[:], PX[:, 16:24], RT[:], ALU.add, r=["PX", "RT"], w=["RT"])

            def process_block(xsrc, t0, NBs, kbase, outs, is_last):
                k_out, v_out, lf_out, h_row0 = outs
                TP = min(128, NBs)
                ntile = NBs // TP
                kpos0 = kbase + t0
                kt_first = kpos0 // 128
                n_kt = (kpos0 + NBs + 127) // 128
                new_kts = list(range(kt_first, n_kt))

                for i in range(ntile):
                    s = nxt("xs", 2)
                    em.dma("sync", XS[s][0:TP, :], xsrc[t0 + i * TP: t0 + (i + 1) * TP, :], w=[("XS", s)])
                    for half in range(2):
                        z = nxt("pz", 2)
                        for j in range(4):
                            kc = half * 4 + j
                            em.tr(PZ[z][:, j * TP:(j + 1) * TP], XS[s][0:TP, kc * 128:(kc + 1) * 128], ID[0:TP, 0:TP],
                                  r=[("XS", s), "ID"], w=[("PZ", z)])
                        em.copy("scalar", XT[:, half * 4:(half + 1) * 4, i * TP:(i + 1) * TP],
                                PZ[z][:, 0:4 * TP].rearrange("p (j t) -> p j t", j=4), r=[("PZ", z)], w=["XT"])
                yield "A"
                if stop <= 1:
                    return

                wq_t, wq_k = load_w(w_in_v[:, :, 0:512], (128, 8, 512), W_IN_KEYS)
                for pc in range(4):
                    z = nxt("pz", 2)
                    for kc in range(8):
                        em.mm(PZ[z][:, 0:NBs], wq_t[:, kc, pc * 128:(pc + 1) * 128], XT[:, kc, 0:NBs], kc == 0, kc == 7,
                              r=[wq_k, "XT"], w=[("PZ", z)])
                    em.copy("scalar", QTP[0:64, 2 * pc, 0:NBs], PZ[z][0:64, 0:NBs], r=[("PZ", z)], w=[("QTP", 2 * pc)])
                    em.copy("scalar", QTP[64:128, 2 * pc + 1, 0:NBs], PZ[z][64:128, 0:NBs], r=[("PZ", z)], w=[("QTP", 2 * pc + 1)])
                wk_t, wk_k = load_w(w_in_v[:, :, 512:1024], (128, 8, 512), W_IN_KEYS)
                for pc in range(4):
                    z = nxt("pz", 2)
                    for kc in range(8):
                        em.mm(PZ[z][:, 0:NBs], wk_t[:, kc, pc * 128:(pc + 1) * 128], XT[:, kc, 0:NBs], kc == 0, kc == 7,
                              r=[wk_k, "XT"], w=[("PZ", z)])
                    em.copy("scalar", KT[:, pc, kpos0:kpos0 + NBs], PZ[z][:, 0:NBs], r=[("PZ", z)],
                            w=[("KT", kt) for kt in new_kts])
                wv_t, wv_k = load_w(w_in_v[:, :, 1024:1536], (128, 8, 512), W_IN_KEYS)
                tile_kts = []
                for i in range(ntile):
                    kt = (kpos0 + i * TP) // 128
                    r0 = (kpos0 + i * TP) % 128
                    assert r0 == 0
                    tile_kts.append(kt)
                    tok = slice(i * TP, (i + 1) * TP)
                    s = nxt("kt", 2)
                    z = nxt("pz", 2)
                    for kc in range(8):
                        em.mm(PZ[z][0:TP, :], XT[:, kc, tok], wk_t[:, kc, :], kc == 0, kc == 7,
                              r=[wk_k, "XT"], w=[("PZ", z)])
                    em.copy("vector", KTOK[s][0:TP, :], PZ[z][0:TP, :], r=[("PZ", z)], w=[("KTOK", s)])
                    em.dma("gpsimd", k_out[t0 + i * TP: t0 + (i + 1) * TP, :], KTOK[s][0:TP, :], r=[("KTOK", s)])
                    z = nxt("pz", 2)
                    for kc in range(8):
                        em.mm(PZ[z][0:TP, :], XT[:, kc, tok], wv_t[:, kc, :], kc == 0, kc == 7,
                              r=[wv_k, "XT"], w=[("PZ", z)])
                    em.copy("scalar", VTOK[s][0:TP, :], PZ[z][0:TP, :], r=[("PZ", z)], w=[("VTOK", s)])
                    vt4 = VTOK[s][0:TP, :].rearrange("p (c t d) -> p c t d", c=4, t=2)
                    em.copy("gpsimd", VA[0:TP, kt, :, 0:64], vt4[:, :, 0, :], r=[("VTOK", s)], w=[("VA", kt)])
                    em.copy("gpsimd", VA[0:TP, kt, :, 128:192], vt4[:, :, 1, :], r=[("VTOK", s)], w=[("VA", kt)])
                    em.dma("gpsimd", v_out[t0 + i * TP: t0 + (i + 1) * TP, :], VTOK[s][0:TP, :], r=[("VTOK", s)])
                    for kc in range(8):
                        em.mm(PX[0:TP, 0:8], XT[:, kc, tok], WF[:, kc, :], kc == 0, kc == 7, r=["WF", "XT"], w=["PX"])
                    em.tt("vector", LT[0:TP, :], PX[0:TP, 0:8], BFt[0:TP, :], ALU.add, r=["PX", "BFt"], w=["LT"])
                    em.act(LT[0:TP, :], LT[0:TP, :], AF.Exp, scale=-1.0, r=["LT"], w=["LT"])
                    em.act(LT[0:TP, :], LT[0:TP, :], AF.Ln, bias=1.0, r=["LT"], w=["LT"])
                    em.ts("vector", LF[0:TP, kt, :], LT[0:TP, :], -1.0, None, ALU.mult, r=["LT"], w=[("LF", kt)])
                    em.dma("gpsimd", lf_out[t0 + i * TP: t0 + (i + 1) * TP, :], LF[0:TP, kt, :], r=[("LF", kt)])
                if stop <= 2:
                    return

                rgbuf = {}
                rgw = {}

                def rg_inproj(cc):
                    b_ = nxt("rg", 2)
                    rgbuf[cc] = b_
                    XR, GG, XC = XRs[b_], GGs[b_], XCs[b_]
                    xck = ("XC", b_)
                    if "xr" not in rgw:
                        rgw["xr"] = load_w(w_in_v[:, :, 1536:2048], (128, 8, 512), W_IN_KEYS)
                        rgw["gate"] = load_w(w_in_v[:, :, 2048:2560], (128, 8, 512), W_IN_KEYS)
                    wr_t, wr_k = rgw["xr"]
                    wg_t, wg_k = rgw["gate"]
                    csl = slice(cc * 128, (cc + 1) * 128)
                    za = nxt("pz", 2)
                    for kc in range(8):
                        em.mm(PZ[za][:, 0:NBs], wr_t[:, kc, csl], XT[:, kc, 0:NBs], kc == 0, kc == 7,
                              r=[wr_k, "XT"], w=[("PZ", za)])
                    em.copy("vector", XR[:, 0:3], HIST[:, cc, :], r=[("HIST", cc)], w=[("XRh", b_)])
                    em.copy("scalar", XR[:, 3:3 + NBs], PZ[za][:, 0:NBs], r=[("PZ", za)], w=[("XRb", b_)])
                    zb = nxt("pz", 2)
                    for kc in range(8):
                        em.mm(PZ[zb][:, 0:NBs], wg_t[:, kc, csl], XT[:, kc, 0:NBs], kc == 0, kc == 7,
                              r=[wg_k, "XT"], w=[("PZ", zb)])
                    em.copy("scalar", GG[:, 0:NBs], PZ[zb][:, 0:NBs], r=[("PZ", zb)], w=[("GG", b_)])
                    xk = [("XRh", b_), ("XRb", b_)]
                    em.copy("vector", HIST[:, cc, :], XR[:, NBs:NBs + 3], r=xk, w=[("HIST", cc)])
                    em.ts("vector", XC[:, 0:NBs], XR[:, 0:NBs], CW[:, cc, 0:1], CB[:, cc:cc + 1], ALU.mult, ALU.add,
                          r=xk + ["CW", "CB"], w=[xck])
                    for wtap in range(1, 4):
                        em.stt("vector", XC[:, 0:NBs], XR[:, wtap:wtap + NBs], CW[:, cc, wtap:wtap + 1], XC[:, 0:NBs],
                               ALU.mult, ALU.add, r=xk + ["CW", xck], w=[xck])

                def rg_chain(cc):
                    b_ = rgbuf[cc]
                    GG, XC = GGs[b_], XCs[b_]
                    gk, xck = ("GG", b_), ("XC", b_)
                    N = slice(0, NBs)
                    z1 = nxt("pz", 2)
                    em.mm(PZ[z1][:, N], WA[:, cc, :], XC[:, N], True, True, r=["WA", xck], w=[("PZ", z1)])
                    z2 = nxt("pz", 2)
                    em.mm(PZ[z2][:, N], WX[:, cc, :], XC[:, N], True, True, r=["WX", xck], w=[("PZ", z2)])
                    em.tt("gpsimd", T2[:, N], GG[:, N], GG[:, N], ALU.mult, r=[gk], w=["T2"])
                    em.ts("gpsimd", T2[:, N], T2[:, N], 0.044715, 1.0, ALU.mult, ALU.add, r=["T2"], w=["T2"])
                    em.tt("gpsimd", T2[:, N], T2[:, N], GG[:, N], ALU.mult, r=["T2", gk], w=["T2"])
                    yield
                    em.act(RR[:, N], PZ[z1][:, N], AF.Exp, bias=BA[:, cc:cc + 1], scale=-1.0, r=[("PZ", z1), "BA"], w=["RR"])
                    em.act(II[:, N], PZ[z2][:, N], AF.Exp, bias=BX[:, cc:cc + 1], scale=-1.0, r=[("PZ", z2), "BX"], w=["II"])
                    em.act(T2[:, N], T2[:, N], AF.Exp, scale=-GELU_C, r=["T2"], w=["T2"])
                    yield
                    em.ts("vector", RR[:, N], RR[:, N], 1.0, None, ALU.add, r=["RR"], w=["RR"])
                    em.recip(RR[:, N], RR[:, N], r=["RR"], w=["RR"])
                    yield
                    em.act(AA[:, N], RR[:, N], AF.Exp, scale=C8[:, cc:cc + 1], r=["RR", "C8"], w=["AA"])
                    em.act(T1[:, N], RR[:, N], AF.Exp, scale=C16[:, cc:cc + 1], r=["RR", "C16"], w=["T1"])
                    em.ts("vector", II[:, N], II[:, N], 1.0, None, ALU.add, r=["II"], w=["II"])
                    em.recip(II[:, N], II[:, N], r=["II"], w=["II"])
                    yield
                    em.ts("vector", T1[:, N], T1[:, N], -1.0, 1.0, ALU.mult, ALU.add, r=["T1"], w=["T1"])
                    em.tt("vector", BB[:, N], II[:, N], XC[:, N], ALU.mult, r=["II", xck], w=["BB"])
                    em.ts("gpsimd", T2[:, N], T2[:, N], 1.0, None, ALU.add, r=["T2"], w=["T2"])
                    yield
                    em.act(T1[:, N], T1[:, N], AF.Ln, r=["T1"], w=["T1"])
                    em.recip(T2[:, N], T2[:, N], r=["T2"], w=["T2"])
                    yield
                    em.act(T1[:, N], T1[:, N], AF.Exp, scale=0.5, r=["T1"], w=["T1"])
                    em.tt("gpsimd", T2[:, N], T2[:, N], GG[:, N], ALU.mult, r=["T2", gk], w=["T2"])
                    yield
                    em.tt("vector", BB[:, N], BB[:, N], T1[:, N], ALU.mult, r=["BB", "T1"], w=["BB"])
                    yield
                    em.scan(HS[:, N], AA[:, N], BB[:, N], HST[:, cc:cc + 1], r=["AA", "BB", ("HST", cc)], w=["HS"])
                    yield
                    em.copy("vector", HST[:, cc:cc + 1], HS[:, NBs - 1:NBs], r=["HS"], w=[("HST", cc)])
                    em.tt("vector", ROT[:, cc, N], HS[:, N], T2[:, N], ALU.mult, r=["HS", "T2"], w=["ROT"])

                rg_state = {"next_cc": 0, "gen": None}

                def rg_step(flush=False):
                    while True:
                        if rg_state["gen"] is None:
                            cc = rg_state["next_cc"]
                            if cc >= 4:
                                return
                            rg_state["gen"] = rg_chain(cc)
                        try:
                            next(rg_state["gen"])
                        except StopIteration:
                            rg_state["gen"] = None
                            rg_state["next_cc"] += 1
                            if rg_state["next_cc"] < 4:
                                rg_inproj(rg_state["next_cc"])
                        if not flush:
                            return

                rg_inproj(0)
                if len(tile_kts) == 1:
                    em.copy("vector", FREF[:], RT[:], r=["RT"], w=["FREF"])
                for ti_, kt in enumerate(tile_kts):
                    cumsum_tile(kt, TP)
                    if len(tile_kts) > 1 and ti_ == len(tile_kts) // 2 - 1:
                        em.copy("vector", FREF[:], RT[:], r=["RT"], w=["FREF"])
                if stop <= 3:
                    rg_step(flush=True)
                    return

                em.tt("vector", BIAS[:, 0:n_kt, :], FREF[:, :].unsqueeze(1).to_broadcast([128, n_kt, 8]),
                      FF[:, 0:n_kt, :], ALU.subtract, r=["FREF"] + [("FF", kt) for kt in range(n_kt)], w=["BIAS"])
                items = [(h, kt) for h in range(8) for kt in range(n_kt)]
                po_of = {}

                def kp_of(kt):
                    return min(128, kpos0 + NBs - kt * 128)

                def emit_qk(i):
                    h, kt = items[i]
                    kp = kp_of(kt)
                    s = i % 3
                    em.mm(PSS[s][0:kp, 0:NBs], KT[:, h // 2, kt * 128:kt * 128 + kp], QTP[:, h, 0:NBs],
                          True, True, r=[("KT", kt), ("QTP", h)], w=[("PSS", s)])

                def emit_exp(i):
                    h, kt = items[i]
                    kp = kp_of(kt)
                    s = i % 3
                    b = i % NPB
                    em.act(PB[b][0:kp, 0:NBs], PSS[s][0:kp, 0:NBs], AF.Exp, bias=BIAS[0:kp, kt, h:h + 1], scale=0.125,
                           r=[("PSS", s), "BIAS"], w=[("PB", b)])
                    if kt >= kt_first:
                        j = kt - kt_first
                        em.tt("gpsimd", PB[b][0:kp, 0:NBs], PB[b][0:kp, 0:NBs], MASK[0:kp, j, 0:NBs], ALU.mult,
                              r=[("PB", b), "MASK"], w=[("PB", b)])

                def emit_pv(i):
                    h, kt = items[i]
                    kp = kp_of(kt)
                    b = i % NPB
                    if kt == 0:
                        po_of[h] = nxt("po", 2)
                    o = po_of[h]
                    pc, odd = h // 2, h % 2
                    lhsT = VA[0:kp, kt, pc, 64:192] if odd else VA[0:kp, kt, pc, 0:128]
                    em.mm(PO[o][:, 0:NBs], lhsT, PB[b][0:kp, 0:NBs], kt == 0, kt == n_kt - 1,
                          r=[("VA", kt), "VA1", ("PB", b)], w=[("PO", o)])
                    if kt == n_kt - 1:
                        c = 0
                        osl = slice(64, 128) if odd else slice(0, 64)
                        dsl = slice(0, 64) if odd else slice(64, 128)
                        em.recip(RECS[c][osl, 0:NBs], PO[o][dsl, 0:NBs], r=[("PO", o)], w=[("RECS", c)])
                        em.tt("vector", OT[osl, pc, 0:NBs], PO[o][osl, 0:NBs], RECS[c][osl, 0:NBs], ALU.mult,
                              r=[("PO", o), ("RECS", c)], w=["OT"])

                emit_qk(0)
                if len(items) > 1:
                    emit_qk(1)
                for i in range(len(items)):
                    if i + 2 < len(items):
                        emit_qk(i + 2)
                    emit_exp(i)
                    emit_pv(i)
                    rg_step()
                rg_step(flush=True)
                if stop <= 4:
                    return

                for hf in range(2):
                    cs = slice(hf * 512, (hf + 1) * 512)
                    ga_t, ga_k = load_w(w_in_v[:, :, 2560 + hf * 512:2560 + (hf + 1) * 512], (128, 8, 512), W_IN_KEYS)
                    gr_t, gr_k = load_w(w_in_v[:, :, 3584 + hf * 512:3584 + (hf + 1) * 512], (128, 8, 512), W_IN_KEYS)
                    i_ = ws_rr[0] % NWS
                    ws_rr[0] += 1
                    em.dma("sync", WS[i_][:, 0:4, :], w_au_v[:, :, cs], r=W_AU_KEYS, w=[("WS", i_)])
                    em.dma("sync", WS[i_][:, 4:8, :], w_ru_v[:, :, cs], r=W_RU_KEYS, w=[("WS", i_)])
                    au_t, au_k = WS[i_], ("WS", i_)
                    ru_k = au_k
                    for fc in range(4):
                        fs = slice(fc * 128, (fc + 1) * 128)
                        for kc in range(8):
                            em.mm(PZ[0][:, 0:NBs], ga_t[:, kc, fs], XT[:, kc, 0:NBs], kc == 0, kc == 7,
                                  r=[ga_k, "XT"], w=[("PZ", 0)])
                        em.act(SA[:, 0:NBs], PZ[0][:, 0:NBs], AF.Sigmoid, r=[("PZ", 0)], w=["SA"])
                        for kc in range(8):
                            em.mm(PZ[1][:, 0:NBs], gr_t[:, kc, fs], XT[:, kc, 0:NBs], kc == 0, kc == 7,
                                  r=[gr_k, "XT"], w=[("PZ", 1)])
                        em.act(SR[:, 0:NBs], PZ[1][:, 0:NBs], AF.Sigmoid, r=[("PZ", 1)], w=["SR"])
                        for kc in range(4):
                            em.mm(PSS[0][:, 0:NBs], au_t[:, kc, fs], OT[:, kc, 0:NBs], kc == 0, kc == 3,
                                  r=[au_k, "OT"], w=[("PSS", 0)])
                        em.tt("vector", M1[:, 0:NBs], PSS[0][:, 0:NBs], SA[:, 0:NBs], ALU.mult, r=[("PSS", 0), "SA"], w=["M1"])
                        for kc in range(4):
                            em.mm(PSS[1][:, 0:NBs], au_t[:, 4 + kc, fs], ROT[:, kc, 0:NBs], kc == 0, kc == 3,
                                  r=[ru_k, "ROT"], w=[("PSS", 1)])
                        em.tt("vector", M2[:, 0:NBs], PSS[1][:, 0:NBs], SR[:, 0:NBs], ALU.mult, r=[("PSS", 1), "SR"], w=["M2"])
                        em.tt("gpsimd", MT[:, hf * 4 + fc, 0:NBs], M1[:, 0:NBs], M2[:, 0:NBs], ALU.add,
                              r=["M1", "M2"], w=["MT"])
                yield "E"
                if stop <= 5:
                    return

                wo0_t, wo0_k = load_w(w_out_v[:, :, 0:512], (128, 8, 512), W_OUT_KEYS)
                wo1_t, wo1_k = load_w(w_out_v[:, :, 512:1024], (128, 8, 512), W_OUT_KEYS)
                for i in range(ntile):
                    tok = slice(i * TP, (i + 1) * TP)
                    s = nxt("xs", 2)
                    em.dma("sync", XS[s][0:TP, :], xsrc[t0 + i * TP: t0 + (i + 1) * TP, :], w=[("XS", s)])
                    q = nxt("pre", 2)
                    for hf, (wt, wk) in enumerate(((wo0_t, wo0_k), (wo1_t, wo1_k))):
                        for kc in range(8):
                            em.mm(PO[hf][0:TP, :], MT[:, kc, tok], wt[:, kc, :], kc == 0, kc == 7,
                                  r=[wk, "MT"], w=[("PO", hf)])
                        em.stt("vector", PRE[q][0:TP, hf * 512:(hf + 1) * 512], XS[s][0:TP, hf * 512:(hf + 1) * 512],
                               ALPHA, PO[hf][0:TP, :], ALU.mult, ALU.add, r=[("XS", s), ("PO", hf)], w=[("PRE", q, hf)])
                    layer_norm(em, PRE[q], HO[q], None, STAT[q], LNG, LNB, TP,
                               [("PRE", q, 0), ("PRE", q, 1)], ("HO", q), ("STAT", q), "LNG", "LNB")
                    em.dma("gpsimd", h_scr[h_row0 + t0 + i * TP: h_row0 + t0 + (i + 1) * TP, :], HO[q][0:TP, :],
                           r=[("HO", q)], w=[("h_scr", (h_row0 + t0 + i * TP) // 32)])

            em.memset("vector", RT[:], 0.0, w=["RT"])
            em.memset("vector", HIST[:], 0.0, w=[("HIST", c) for c in range(4)])
            em.memset("vector", HST[:], 0.0, w=[("HST", c) for c in range(4)])
            UVCH = 512
            uv_jobs = []
            for r0 in range(0, N_EXP, UVCH):
                uv_jobs.append((uv_b[r0:r0 + UVCH, 0:1024], pu[r0:r0 + UVCH, :]))
                uv_jobs.append((uv_b[r0:r0 + UVCH, 1024:2048], pv[r0:r0 + UVCH, :]))
            nblk = T // NB if stop >= 1 else 0
            per_blk = (len(uv_jobs) + max(nblk, 1) - 1) // max(nblk, 1)
            def run_to(gen, tag):
                for t_ in gen:
                    if t_ == tag:
                        return
            gens_p = [process_block(xp, blk * NB, NB, 0, (k_p, v_p, lf_p, 0), blk == T // NB - 1) for blk in range(nblk)]
            if nblk:
                run_to(gens_p[0], "A")
            for blk in range(nblk):
                if do_peer:
                    for (o_, i_) in uv_jobs[blk * per_blk:(blk + 1) * per_blk]:
                        em.dma("gpsimd", o_, i_)
                run_to(gens_p[blk], "E")
                if blk + 1 < nblk:
                    run_to(gens_p[blk + 1], "A")
                run_to(gens_p[blk], None)
            if do_peer and nblk == 0:
                for (o_, i_) in uv_jobs:
                    em.dma("gpsimd", o_, i_)
            for c_ in range(4):
                em.dma("sync", conv_p.rearrange("w (cc p) -> p cc w", p=128)[:, c_, :], HIST[:, c_, :],
                       r=[("HIST", c_)], allow_slow_non_contiguous=True)
            em.dma("sync", rnn_p.rearrange("o (cc p) -> p (o cc)", p=128), HST[:],
                   r=[("HST", c) for c in range(4)], allow_slow_non_contiguous=True)

            NPT = PAST // 128
            for s_ in range(NS if stop >= 7 else 0):
                em.dma("gpsimd", KT[:, :, 0:PAST], ckT[s_], w=[("KT", kt) for kt in range(NPT)])
                for kt in range(NPT):
                    s = nxt("kt", 2)
                    em.dma("sync", VTOK[s][:, :], cv[s_, kt * 128:(kt + 1) * 128, :], w=[("VTOK", s)])
                    vt4 = VTOK[s][:, :].rearrange("p (c t d) -> p c t d", c=4, t=2)
                    em.copy("vector", VA[:, kt, :, 0:64], vt4[:, :, 0, :], r=[("VTOK", s)], w=[("VA", kt)])
                    em.copy("vector", VA[:, kt, :, 128:192], vt4[:, :, 1, :], r=[("VTOK", s)], w=[("VA", kt)])
                em.dma("sync", LF[:, 0:NPT, :], clf[s_].rearrange("(kt p) h -> p kt h", p=128),
                       w=[("LF", kt) for kt in range(NPT)])
                em.memset("vector", RT[:], 0.0, w=["RT"])
                for kt in range(NPT):
                    cumsum_tile(kt, 128)
                em.dma("sync", HIST[:], sconvT[s_], w=[("HIST", c) for c in range(4)])
                em.dma("sync", HST[:], srnnT[s_], w=[("HST", c) for c in range(4)])
                for _ in process_block(xs[s_ * TS:(s_ + 1) * TS, :], 0, TS, PAST,
                                       (k_s[s_ * TS:(s_ + 1) * TS, :], v_s[s_ * TS:(s_ + 1) * TS, :],
                                        lf_s[s_ * TS:(s_ + 1) * TS, :], T + s_ * TS), True):
                    pass
                for c_ in range(4):
                    em.dma("sync", conv_s[s_].rearrange("w (cc p) -> p cc w", p=128)[:, c_, :], HIST[:, c_, :],
                           r=[("HIST", c_)], allow_slow_non_contiguous=True)
                em.dma("sync", rnn_s[s_:s_ + 1, :].rearrange("o (cc p) -> p (o cc)", p=128), HST[:],
                       r=[("HST", c) for c in range(4)], allow_slow_non_contiguous=True)

            p.finish(semstack, "a")

        with contextlib.ExitStack() as st:
            def sb(name, shape, dt=F32):
                return st.enter_context(nc.sbuf_tensor(name, list(shape), dt))

            def ps(name, shape, dt=F32):
                return st.enter_context(nc.psum_tensor(name, list(shape), dt))

            p = Prog(nc)
            em = Em(p)
            ID = sb("ID2", [128, 128])
            IDB = sb("IDB", [128, 128], BF16)
            IOTA = sb("IOTA", [128, 16])
            LNG = sb("LNG2", [128, D_MODEL])
            LNB = sb("LNB2", [128, D_MODEL])
            WQ = sb("WQ", [128, 8, 2048], BF16)
            KEYF = sb("KEYF", [128, 2, 128])
            KEYB = sb("KEYB", [128, 2, 128], BF16)
            H = [sb("H%d" % i, [128, D_MODEL]) for i in range(2)]
            HB = [sb("HB%d" % i, [128, D_MODEL], BF16) for i in range(2)]
            HT = sb("HT", [128, 8, 128], BF16)
            QYT = sb("QYT", [128, 16, 128], BF16)
            SC = sb("SC", [128, 16, 128])
            SC2r = [sb("SC2r%d" % i, [128, 128]) for i in range(2)]
            MX = sb("MX", [128, 16, 16])
            MI = sb("MI", [128, 16, 16], U32)
            MIF = sb("MIF", [128, 16, 16])
            CAND = sb("CAND", [128, 8, 256])
            CAND2r = [sb("CAND2r%d" % i, [128, 256]) for i in range(2)]
            TOPV = sb("TOPV", [128, 8, 16])
            SEL = sb("SEL", [128, 8, 16], U32)
            SELA = sb("SELA", [128, 8, 16], U32)
            SELB = sb("SELB", [128, 8, 16], U32)
            AF_ = sb("AFl", [128, 8, 16])
            BF_ = sb("BFl", [128, 8, 16])
            EQ = sb("EQ", [128, 8, 16, 16])
            I1S = sb("I1S", [128, 8, 16])
            I2S = sb("I2S", [128, 8, 16])
            IDXF = sb("IDXF", [128, 128])
            IDX = [sb("IDX%d" % i, [128, 128], I32) for i in range(2)]
            GW = [sb("GW%d" % i, [128, 8, 16]) for i in range(2)]
            GS = sb("GS", [128, 8])
            NGRP = 32
            NGA = 4
            GSZ = 128 // NGRP
            ACT = [sb("ACT%d" % i, [128, GSZ]) for i in range(NGA)]
            ACT2 = [sb("ACTb%d" % i, [128, GSZ]) for i in range(NGA)]
            WGT = [sb("WGT%d" % i, [128, GSZ]) for i in range(NGA)]
            NUV = 20
            UV = [sb("UV%d" % i, [128, 2048], BF16) for i in range(NUV)]
            NDG = 4
            DG = [sb("DG%d" % i, [128, 128], BF16) for i in range(NDG)]
            JUNKB = sb("JUNKB", [128, D_MODEL], BF16)
            JUNKC = sb("JUNKC", [128, D_MODEL], BF16)
            NPRD = 4
            PRD = [sb("PRD%d" % i, [128, D_MODEL], BF16) for i in range(NPRD)]
            PRE = [sb("PREb%d" % i, [128, D_MODEL]) for i in range(2)]
            YO = [sb("YO%d" % i, [128, D_MODEL]) for i in range(2)]
            STAT = [sb("STATb%d" % i, [128, 8]) for i in range(2)]

            PG = [ps("PG%d" % i, [128, 512]) for i in range(4)]
            POUT = [ps("POUT%d" % i, [128, 512]) for i in range(4)]
            if os.environ.get("KDEBUG"):
                print("phase2 sbuf remaining", nc.sbuf_bytes_remaining)

            em.dma("sync", ID[:], c_ident, w=["ID"])
            em.copy("vector", IDB[:], ID[:], r=["ID"], w=["IDB"])
            em.dma("sync", IOTA[:], c_iota, w=["IOTA"])
            em.dma("sync", LNG[:], ln2g, w=["LNG"])
            em.dma("sync", LNB[:], ln2b, w=["LNB"])
            em.dma("sync", KEYF[:, 0, :], k1T, w=["KEYF0"])
            em.dma("sync", KEYF[:, 1, :], k2T, w=["KEYF1"])
            em.copy("vector", KEYB[:], KEYF[:], r=["KEYF0", "KEYF1"], w=["KEYB"])
            wq_v = wq_b.rearrange("(kc p) c -> p kc c", p=128)
            for kc in range(8):
                em.dma("sync", WQ[:, kc, :], wq_v[:, kc, :], w=[("WQ", kc)])
            WQK = [("WQ", kc) for kc in range(8)]

            rr2 = {"h": 0, "pq": 0, "pg": 0, "uv": 0, "dg": 0, "pre": 0, "idx": 0, "grp": 0, "prd": 0}

            def nxt2(name, n):
                v = rr2[name] % n
                rr2[name] += 1
                return v

            def prologue(row0, n, ctx):
                hs_ = nxt2("h", 2)
                Ht = H[hs_]
                Hb = HB[hs_]
                em.dma("sync", Ht[0:n, :], h_scr[row0:row0 + n, :], w=[("H", hs_)])
                em.copy("scalar", Hb[0:n, :], Ht[0:n, :], r=[("H", hs_)], w=[("HB", hs_)])
                for half in range(2):
                    z = nxt2("pg", 4)
                    for j in range(4):
                        kc = half * 4 + j
                        em.tr(PG[z][:, j * n:(j + 1) * n], Ht[0:n, kc * 128:(kc + 1) * 128], ID[0:n, 0:n],
                              r=[("H", hs_), "ID"], w=[("PG", z)])
                    em.copy("scalar", HT[:, half * 4:(half + 1) * 4, 0:n],
                            PG[z][:, 0:4 * n].rearrange("p (j t) -> p j t", j=4), r=[("PG", z)], w=["HT"])
                yield
                for c4 in range(4):
                    if c4 > 0:
                        yield
                    z = nxt2("pg", 4)
                    for cj in range(4):
                        c = c4 * 4 + cj
                        for kc in range(8):
                            em.mm(PG[z][:, cj * n:(cj + 1) * n], WQ[:, kc, c * 128:(c + 1) * 128], HT[:, kc, 0:n],
                                  kc == 0, kc == 7, r=WQK + ["HT"], w=[("PG", z)])
                    em.copy("scalar", QYT[:, c4 * 4:(c4 + 1) * 4, 0:n],
                            PG[z][:, 0:4 * n].rearrange("p (j t) -> p j t", j=4), r=[("PG", z)], w=[("QYT", c4)])
                yield
                for c4 in range(4):
                    z = nxt2("pg", 4)
                    for cj in range(4):
                        c = c4 * 4 + cj
                        em.mm(PG[z][0:n, cj * 128:(cj + 1) * 128], QYT[:, c, 0:n], KEYB[:, c % 2, :], True, True,
                              r=[("QYT", c4), "KEYB"], w=[("PG", z)])
                    em.copy("vector", SC[0:n, c4 * 4:(c4 + 1) * 4, :],
                            PG[z][0:n, :].rearrange("p (j k) -> p j k", j=4), r=[("PG", z)], w=[("SC", c4)])
                for c in range(16):
                    if c % 4 == 0:
                        yield
                    ck = ("SC", c // 4)
                    em.vmax(MX[0:n, c, 0:8], SC[0:n, c, :], r=[ck], w=[("MX", c)])
                    em.vmaxidx(MI[0:n, c, 0:8], MX[0:n, c, 0:8], SC[0:n, c, :], r=[ck, ("MX", c)], w=[("MI", c)])
                    em.vmatchrep(SC2r[c % 2][0:n, :], MX[0:n, c, 0:8], SC[0:n, c, :], -1e30, r=[ck, ("MX", c)], w=[("SC2", c % 2)])
                    em.vmax(MX[0:n, c, 8:16], SC2r[c % 2][0:n, :], r=[("SC2", c % 2)], w=[("MXb", c)])
                    em.vmaxidx(MI[0:n, c, 8:16], MX[0:n, c, 8:16], SC2r[c % 2][0:n, :], r=[("SC2", c % 2), ("MXb", c)], w=[("MIb", c)])
                MXK = [("MX", c) for c in range(16)] + [("MXb", c) for c in range(16)]
                MIK = [("MI", c) for c in range(16)] + [("MIb", c) for c in range(16)]
                em.copy("vector", MIF[0:n], MI[0:n], r=MIK, w=["MIF"])
                yield
                MXv = MX[0:n].rearrange("p (h t) k -> p h t k", t=2)
                em.tt("vector", CAND[0:n].rearrange("p h (a b) -> p h a b", a=16),
                      MXv[:, :, 0, :].unsqueeze(3).to_broadcast([n, 8, 16, 16]),
                      MXv[:, :, 1, :].unsqueeze(2).to_broadcast([n, 8, 16, 16]), ALU.add, r=MXK, w=["CAND"])
                for h in range(8):
                    if h % 2 == 0:
                        yield
                    em.vmax(TOPV[0:n, h, 0:8], CAND[0:n, h, :], r=["CAND"], w=[("TV", h)])
                    em.vmaxidx(SEL[0:n, h, 0:8], TOPV[0:n, h, 0:8], CAND[0:n, h, :], r=["CAND", ("TV", h)], w=[("SEL", h)])
                    em.vmatchrep(CAND2r[h % 2][0:n, :], TOPV[0:n, h, 0:8], CAND[0:n, h, :], -1e30, r=["CAND", ("TV", h)], w=[("C2", h % 2)])
                    em.vmax(TOPV[0:n, h, 8:16], CAND2r[h % 2][0:n, :], r=[("C2", h % 2)], w=[("TVb", h)])
                    em.vmaxidx(SEL[0:n, h, 8:16], TOPV[0:n, h, 8:16], CAND2r[h % 2][0:n, :], r=[("C2", h % 2), ("TVb", h)], w=[("SELb", h)])
                TVK = [("TV", h) for h in range(8)] + [("TVb", h) for h in range(8)]
                SELK = [("SEL", h) for h in range(8)] + [("SELb", h) for h in range(8)]
                yield
                em.tss("vector", SELB[0:n], SEL[0:n], 15, ALU.bitwise_and, r=SELK, w=["SELB"])
                em.tss("vector", SELA[0:n], SEL[0:n], 4, ALU.logical_shift_right, r=SELK, w=["SELA"])
                em.copy("vector", AF_[0:n], SELA[0:n], r=["SELA"], w=["AFl"])
                em.copy("vector", BF_[0:n], SELB[0:n], r=["SELB"], w=["BFl"])
                MIFv = MIF[0:n].rearrange("p (h t) k -> p h t k", t=2)
                iota_b = IOTA[0:n, :].unsqueeze(1).unsqueeze(1).to_broadcast([n, 8, 16, 16])
                for (sel_f, tsel, dst, dk) in ((AF_, 0, I1S, "I1S"), (BF_, 1, I2S, "I2S")):
                    em.tt("vector", EQ[0:n], iota_b, sel_f[0:n].unsqueeze(3).to_broadcast([n, 8, 16, 16]), ALU.is_equal,
                          r=["IOTA", "AFl", "BFl"], w=["EQ"])
                    em.tt("vector", EQ[0:n], EQ[0:n], MIFv[:, :, tsel, :].unsqueeze(2).to_broadcast([n, 8, 16, 16]), ALU.mult,
                          r=["EQ", "MIF"], w=["EQ"])
                    em.reduce("vector", dst[0:n], EQ[0:n], ALU.add, r=["EQ"], w=[dk])
                ix = nxt2("idx", 2)
                em.stt("vector", IDXF[0:n, :], I1S[0:n].rearrange("p h k -> p (h k)"), 128.0,
                       I2S[0:n].rearrange("p h k -> p (h k)"), ALU.mult, ALU.add, r=["I1S", "I2S"], w=["IDXF"])
                em.copy("vector", IDX[ix][0:n, :], IDXF[0:n, :], r=["IDXF"], w=[("IDX", ix)])
                yield
                GWt = GW[ix]
                gwk = ("GW", ix)
                em.tt("vector", GWt[0:n], TOPV[0:n], TOPV[0:n, :, 0:1].to_broadcast([n, 8, 16]), ALU.subtract, r=TVK, w=[gwk])
                em.act(GWt[0:n], GWt[0:n], AF.Exp, r=[gwk], w=[gwk])
                em.reduce("vector", GS[0:n], GWt[0:n], ALU.add, r=[gwk], w=["GS"])
                em.recip(GS[0:n], GS[0:n], r=["GS"], w=["GS"])
                em.tt("vector", GWt[0:n], GWt[0:n], GS[0:n].unsqueeze(2).to_broadcast([n, 8, 16]), ALU.mult, r=[gwk, "GS"], w=[gwk])
                ctx.update(hs_=hs_, ix=ix, gwk=gwk, GWt=GWt)

            grp_bufs = {}

            def emit_dots_dve(t, n, ctx, grp):
                hs_, ix = ctx["hs_"], ctx["ix"]
                Hb = HB[hs_]
                ga = grp % NGA
                A_ = ACT[ga]
                bufs = []
                pend = []
                for k in range(GSZ):
                    s = grp * GSZ + k
                    g = nxt2("uv", NUV)
                    bufs.append(g)
                    em.gather(UV[g][0:n, :], uv_b, IDX[ix][0:n, s:s + 1], r=[("IDX", ix)], w=[("UV", g)])
                    if k % 4 == 0:
                        em.stt("vector", JUNKB[0:n, :], UV[g][0:n, 0:1024], 1.0, Hb[0:n, :], ALU.mult, ALU.mult,
                               r=[("UV", g), ("HB", hs_)], w=["JUNKB", ("ACT", ga, k)], accum=A_[0:n, k:k + 1])
                    else:
                        pj = nxt2("prd", NPRD)
                        em.tt("vector", PRD[pj][0:n, :], UV[g][0:n, 0:1024], Hb[0:n, :], ALU.mult,
                              r=[("UV", g), ("HB", hs_)], w=[("PRD", pj)])
                        pend.append((pj, k))
                grp_bufs[(t, grp)] = (bufs, pend)

            def emit_dots_act(t, n, ctx, grp):
                ga = grp % NGA
                A_ = ACT[ga]
                for (pj, k) in grp_bufs[(t, grp)][1]:
                    em.act(JUNKC[0:n, :], PRD[pj][0:n, :], AF.Identity, accum=A_[0:n, k:k + 1],
                           r=[("PRD", pj)], w=["JUNKC", ("ACT", ga, k)])

            def emit_tail_a(t, n, ctx, grp):
                gwk, GWt = ctx["gwk"], ctx["GWt"]
                GWf = GWt[0:n].rearrange("p h k -> p (h k)")
                ga = grp % NGA
                A_, A2_, W_ = ACT[ga], ACT2[ga], WGT[ga]
                AK = [("ACT", ga, k) for k in range(GSZ)]
                em.act(A2_[0:n], A_[0:n], AF.Gelu_apprx_tanh, r=AK, w=[("ACT2", ga)])
                em.tt("vector", W_[0:n], A2_[0:n], GWf[:, grp * GSZ:(grp + 1) * GSZ], ALU.mult,
                      r=[("ACT2", ga), gwk], w=[("WGT", ga)])

            def emit_tail_b(t, n, ctx, grp):
                ga = grp % NGA
                W_ = WGT[ga]
                bufs = grp_bufs.pop((t, grp))[0]
                po = (t % 2) * 2
                for k in range(GSZ):
                    s = grp * GSZ + k
                    g = bufs[k]
                    d = nxt2("dg", NDG)
                    em.act(DG[d][0:n, 0:n], IDB[0:n, 0:n], AF.Copy, scale=W_[0:n, k:k + 1],
                           r=["IDB", ("WGT", ga)], w=[("DG", d)])
                    for hf in range(2):
                        em.mm(POUT[po + hf][0:n, :], DG[d][0:n, 0:n], UV[g][0:n, 1024 + hf * 512:1024 + (hf + 1) * 512],
                              s == 0, s == 127, r=[("DG", d), ("UV", g)], w=[("POUT", po + hf)])

            def epilogue(t, n, ctx, y_dst):
                po = (t % 2) * 2
                hs_ = ctx["hs_"]
                Ht = H[hs_]
                q = nxt2("pre", 2)
                for hf in range(2):
                    em.stt("vector", PRE[q][0:n, hf * 512:(hf + 1) * 512], Ht[0:n, hf * 512:(hf + 1) * 512],
                           ALPHA, POUT[po + hf][0:n, :], ALU.mult, ALU.add, r=[("H", hs_), ("POUT", po + hf)], w=[("PRE", q, hf)])
                layer_norm(em, PRE[q], YO[q], None, STAT[q], LNG, LNB, n,
                           [("PRE", q, 0), ("PRE", q, 1)], ("YO", q), ("STAT", q), "LNG", "LNB", gb_eng="vector")
                em.dma("sync", y_dst, YO[q][0:n, :], r=[("YO", q)])

            if do_peer:
                tiles = [(ti * 128, 128, y_p[ti * 128:(ti + 1) * 128, :]) for ti in range(T // 128)]
                if NSR > 0:
                    tiles.append((T, NSR, y_s[:, :]))
                ctxs = [dict() for _ in tiles]
                for _ in prologue(tiles[0][0], tiles[0][1], ctxs[0]):
                    pass
                items = [(t, grp) for t in range(len(tiles)) for grp in range(NGRP)]
                gens = {}
                emit_dots_dve(0, tiles[0][1], ctxs[0], 0)
                emit_dots_act(0, tiles[0][1], ctxs[0], 0)
                for i, (t, grp) in enumerate(items):
                    row0, n, y_dst = tiles[t]
                    if grp == 0 and t + 1 < len(tiles):
                        gens[t + 1] = prologue(tiles[t + 1][0], tiles[t + 1][1], ctxs[t + 1])
                    nx = items[i + 1] if i + 1 < len(items) else None
                    if nx is not None and nx[0] != t and (nx[0] in gens):
                        for _ in gens.pop(nx[0]):
                            pass
                    emit_tail_a(t, n, ctxs[t], grp)
                    if nx is not None:
                        emit_dots_dve(nx[0], tiles[nx[0]][1], ctxs[nx[0]], nx[1])
                    emit_tail_b(t, n, ctxs[t], grp)
                    if nx is not None:
                        emit_dots_act(nx[0], tiles[nx[0]][1], ctxs[nx[0]], nx[1])
                    if (t + 1) in gens:
                        try:
                            next(gens[t + 1])
                        except StopIteration:
                            gens.pop(t + 1)
                    if grp == NGRP - 1:
                        epilogue(t, n, ctxs[t], y_dst)
            p.finish(semstack, "b")
    return nc


def layer_norm(em, X, Y, JUNK, ST, G, B, n, xkeys, ykey, stkey, gk, bk, gb_eng="gpsimd"):
    jk = "JUNK"
    if JUNK is None:
        JUNK, jk = Y, ykey
    em.act(JUNK[0:n, :], X[0:n, :], AF.Identity, accum=ST[0:n, 0:1], r=xkeys, w=[jk, (stkey, 0)])
    em.act(JUNK[0:n, :], X[0:n, :], AF.Square, accum=ST[0:n, 1:2], r=xkeys, w=[jk, (stkey, 1)])
    em.ts("vector", ST[0:n, 2:3], ST[0:n, 0:1], 1.0 / D_MODEL, None, ALU.mult, r=[(stkey, 0)], w=[(stkey, 2)])
    em.ts("vector", ST[0:n, 3:4], ST[0:n, 1:2], 1.0 / D_MODEL, None, ALU.mult, r=[(stkey, 1)], w=[(stkey, 3)])
    em.tt("vector", ST[0:n, 4:5], ST[0:n, 2:3], ST[0:n, 2:3], ALU.mult, r=[(stkey, 2)], w=[(stkey, 4)])
    em.tt("vector", ST[0:n, 5:6], ST[0:n, 3:4], ST[0:n, 4:5], ALU.subtract, r=[(stkey, 3), (stkey, 4)], w=[(stkey, 5)])
    em.ts("vector", ST[0:n, 6:7], ST[0:n, 5:6], LN_EPS, None, ALU.add, r=[(stkey, 5)], w=[(stkey, 6)])
    em.act(ST[0:n, 6:7], ST[0:n, 6:7], AF.Ln, r=[(stkey, 6)], w=[(stkey, 6)])
    em.act(ST[0:n, 7:8], ST[0:n, 6:7], AF.Exp, scale=-0.5, r=[(stkey, 6)], w=[(stkey, 7)])
    em.ts("vector", Y[0:n, :], X[0:n, :], ST[0:n, 2:3], ST[0:n, 7:8], ALU.subtract, ALU.mult,
          r=xkeys + [(stkey, 2), (stkey, 7)], w=[ykey])
    em.tt(gb_eng, Y[0:n, :], Y[0:n, :], G[0:n, :], ALU.mult, r=[ykey, gk], w=[ykey])
    em.tt(gb_eng, Y[0:n, :], Y[0:n, :], B[0:n, :], ALU.add, r=[ykey, bk], w=[ykey])


def _fm(v):
    return np.ascontiguousarray(np.asarray(v, np.float32).reshape(4, 128).T)


def _blockdiag(w):
    out = np.zeros((128, 4, 128), np.float32)
    for cc in range(4):
        out[0:64, cc, 0:64] = w[2 * cc]
        out[64:128, cc, 64:128] = w[2 * cc + 1]
    return out


def make_consts(NB):
    NJ = max(1, NB // 128)
    ident = np.eye(128, dtype=np.float32)
    tri = np.triu(np.ones((128, 128), np.float32))
    k = np.arange(128)[:, None]
    q = np.arange(NB)[None, :]
    mask = np.stack([(q >= j * 128 + k).astype(np.float32) for j in range(NJ)], axis=1)
    iota = np.broadcast_to(np.arange(16, dtype=np.float32), (128, 16)).copy()
    return dict(c_ident=ident, c_tri=tri, c_mask=np.ascontiguousarray(mask), c_iota=iota)


def shared_maps(inp, NB):
    f = lambda a: np.ascontiguousarray(np.asarray(a, np.float32))
    rep = lambda v: np.ascontiguousarray(np.broadcast_to(np.asarray(v, np.float32).reshape(1, -1), (128, np.asarray(v).size)))
    m = dict(
        w_in=f(inp["w_in"][0]), bF=rep(inp["b_forget"][0]),
        cw=np.ascontiguousarray(np.asarray(inp["conv_w"][0], np.float32).reshape(4, 4, 128).transpose(2, 1, 0)),
        cb=_fm(inp["conv_b"][0]), wa=_blockdiag(np.asarray(inp["w_rg_a"][0], np.float32)), ba=_fm(inp["b_rg_a"][0]),
        wx=_blockdiag(np.asarray(inp["w_rg_x"][0], np.float32)), bx=_fm(inp["b_rg_x"][0]), lam=_fm(inp["lru_lambda"][0]),
        w_au=f(inp["w_attn_up"][0]), w_ru=f(inp["w_rnn_up"][0]), w_out=f(inp["w_out"][0]),
        ln1g=rep(inp["ln1_g"][0]), ln1b=rep(inp["ln1_b"][0]), wq=f(inp["peer_w_query"][0]),
        k1T=f(np.asarray(inp["peer_keys_1"][0]).T), k2T=f(np.asarray(inp["peer_keys_2"][0]).T),
        pu=f(inp["peer_u"][0]), pv=f(inp["peer_v"][0]), ln2g=rep(inp["ln2_g"][0]), ln2b=rep(inp["ln2_b"][0]),
    )
    m.update(make_consts(NB))
    return m


def core_map(inp, c, NS, shared):
    f = lambda a: np.ascontiguousarray(np.asarray(a, np.float32))
    sl = slice(c * NS, (c + 1) * NS)
    ck = np.asarray(inp["cache_k"][0][sl], np.float32)
    ns, past = ck.shape[0], ck.shape[1]
    ckT = ck.reshape(ns, past, 4, 2, 64).transpose(0, 3, 4, 2, 1).reshape(ns, 128, 4, past)
    sconv = np.asarray(inp["state_conv"][0][sl], np.float32)
    sconvT = sconv.reshape(ns, 3, 4, 128).transpose(0, 3, 2, 1)
    srnn = np.asarray(inp["state_rnn"][0][sl], np.float32)
    srnnT = srnn.reshape(ns, 4, 128).transpose(0, 2, 1)
    xs = np.asarray(inp["x_sample"][sl], np.float32)
    m = dict(shared)
    m.update(
        xp=f(inp["x_prompt"][c]), xs=f(xs.reshape(-1, D_MODEL)), ckT=f(ckT),
        cv=f(np.asarray(inp["cache_v"][0][sl]).reshape(ns, past, 512)),
        clf=f(inp["cache_logf"][0][sl]), sconvT=f(sconvT), srnnT=f(srnnT),
    )
    return m


def kernel(**inp):
    NCORES = 8
    T = inp["x_prompt"].shape[1]
    NS = inp["x_sample"].shape[0] // NCORES
    TS = inp["x_sample"].shape[1]
    PAST = inp["cache_k"].shape[2]
    NB = 256
    nc = build(T=T, NS=NS, TS=TS, PAST=PAST, NB=NB)
    shared = shared_maps(inp, NB)
    in_maps = [core_map(inp, c, NS, shared) for c in range(NCORES)]
    res = run_bass_kernel_spmd(nc, in_maps, core_ids=list(range(NCORES)))
    R = res.results
    B = NCORES
    cat = lambda k: np.stack([np.asarray(R[c][k], np.float32) for c in range(B)], 0)
    y_p = cat("y_p")
    y_s = cat("y_s").reshape(B * NS, TS, D_MODEL)
    k_p = cat("k_p").reshape(1, B, T, 8, 64)
    v_p = cat("v_p").reshape(1, B, T, 8, 64)
    lf_p = cat("lf_p").reshape(1, B, T, 8)
    conv_p = cat("conv_p").reshape(1, B, 3, 512)
    rnn_p = cat("rnn_p").reshape(1, B, 512)
    k_s = cat("k_s").reshape(1, B * NS, TS, 8, 64)
    v_s = cat("v_s").reshape(1, B * NS, TS, 8, 64)
    lf_s = cat("lf_s").reshape(1, B * NS, TS, 8)
    conv_s = cat("conv_s").reshape(1, B * NS, 3, 512)
    rnn_s = cat("rnn_s").reshape(1, B * NS, 512)
    return (y_p, y_s, k_p, v_p, lf_p, conv_p, rnn_p, k_s, v_s, lf_s, conv_s, rnn_s)
```

```python
import contextlib
import os
import numpy as np
import concourse.bass as bass
import concourse.mybir as mybir
from concourse.bass_utils import run_bass_kernel_spmd

F32 = mybir.dt.float32
BF16 = mybir.dt.bfloat16
I32 = mybir.dt.int32
U32 = mybir.dt.uint32
AF = mybir.ActivationFunctionType
ALU = mybir.AluOpType
AX = mybir.AxisListType

ENGS = ("sync", "scalar", "vector", "gpsimd", "tensor")
NPOOL = 16

D_MODEL = 1024
D_IN = 4616
N_EXP = 16384
ALPHA = 2.0 ** 0.25
LN_EPS = 1e-5
GELU_C = 1.5957691216057308


class Op:
    __slots__ = ("eng", "fn", "dma", "deps", "needed", "sem", "val", "pre")

    def __init__(self, eng, fn, dma):
        self.eng = eng
        self.fn = fn
        self.dma = dma
        self.deps = ()
        self.needed = False
        self.sem = None
        self.val = 0
        self.pre = None


class Prog:
    def __init__(self, nc):
        self.nc = nc
        self.ops = {e: [] for e in ENGS}
        self.last_writer = {}
        self.readers = {}

    def op(self, eng, fn, reads=(), writes=(), dma=False):
        o = Op(eng, fn, dma)
        deps = set()
        for k in reads:
            w = self.last_writer.get(k)
            if w is not None:
                deps.add(w)
        for k in writes:
            w = self.last_writer.get(k)
            if w is not None:
                deps.add(w)
            for r in self.readers.get(k, ()):
                deps.add(r)
        if eng == "tensor" and not dma:
            deps = {d for d in deps if not (d.eng == "tensor" and not d.dma)}
        o.deps = deps
        for d in deps:
            d.needed = True
        if dma:
            o.needed = True
        for k in reads:
            self.readers.setdefault(k, []).append(o)
        for k in writes:
            self.last_writer[k] = o
            self.readers[k] = []
        self.ops[eng].append(o)
        return o

    def finish(self, semstack, tag):
        nc = self.nc
        esem = {e: semstack.enter_context(nc.semaphore("s%s_%s" % (tag, e))) for e in ENGS}
        pools = {e: [semstack.enter_context(nc.semaphore("d%s_%s_%d" % (tag, e, i))) for i in range(NPOOL)]
                 for e in ("sync", "scalar", "gpsimd") if any(o.dma for o in self.ops[e])}
        final = {}
        for e in ENGS:
            cnt = 0
            ndma = 0
            for o in self.ops[e]:
                if o.dma:
                    o.sem = pools[e][ndma % NPOOL]
                    o.val = 16 * (ndma // NPOOL + 1)
                    if ndma >= NPOOL:
                        o.pre = (o.sem, 16 * (ndma // NPOOL))
                    ndma += 1
                    final[id(o.sem)] = (o.sem, o.val)
                elif o.needed:
                    cnt += 1
                    o.sem = esem[e]
                    o.val = cnt
                    final[id(o.sem)] = (o.sem, o.val)
        for e in ENGS:
            for o in reversed(self.ops[e]):
                if not o.dma:
                    if not o.needed:
                        o.needed = True
                        o.sem = esem[e]
                        o.val = final.get(id(esem[e]), (None, 0))[1] + 1
                        final[id(o.sem)] = (o.sem, o.val)
                    break

        def make(e):
            def body(eng):
                waited = {}
                for o in self.ops[e]:
                    if o.pre is not None:
                        s, v = o.pre
                        if waited.get(id(s), 0) < v:
                            eng.wait_ge(s, v)
                            waited[id(s)] = v
                    for d in o.deps:
                        s, v = d.sem, d.val
                        if waited.get(id(s), 0) < v:
                            eng.wait_ge(s, v)
                            waited[id(s)] = v
                    ins = o.fn(eng)
                    if o.dma:
                        ins.then_inc(o.sem, 16)
                    elif o.needed:
                        ins.then_inc(o.sem, 1)
                for s, v in final.values():
                    if waited.get(id(s), 0) < v:
                        eng.wait_ge(s, v)
            return body

        with nc.Block() as block:
            block.sync(make("sync"))
            block.scalar(make("scalar"))
            block.vector(make("vector"))
            block.gpsimd(make("gpsimd"))
            block.tensor(make("tensor"))


class Em:
    def __init__(self, p):
        self.p = p

    def dma(self, eng, out, in_, r=(), w=(), **kw):
        return self.p.op(eng, lambda e: e.dma_start(out=out, in_=in_, **kw), r, w, dma=True)

    def gather(self, out, table, idx, r=(), w=()):
        return self.p.op("gpsimd", lambda e: e.indirect_dma_start(
            out=out, out_offset=None, in_=table,
            in_offset=bass.IndirectOffsetOnAxis(ap=idx, axis=0)), r, w, dma=True)

    def mm(self, out, lhsT, rhs, start, stop, r=(), w=()):
        return self.p.op("tensor", lambda e: e.matmul(out, lhsT=lhsT, rhs=rhs, start=start, stop=stop), r, w)

    def tr(self, out, in_, ident, r=(), w=()):
        return self.p.op("tensor", lambda e: e.transpose(out=out, in_=in_, identity=ident), r, w)

    def act(self, out, in_, func, r=(), w=(), bias=None, scale=None, accum=None, eng="scalar"):
        kw = {}
        if bias is not None:
            kw["bias"] = bias
        if scale is not None:
            kw["scale"] = scale
        if accum is not None:
            kw["accum_out"] = accum
        return self.p.op(eng, lambda e: e.activation(out=out, in_=in_, func=func, **kw), r, w)

    def copy(self, eng, out, in_, r=(), w=()):
        if eng == "scalar":
            return self.p.op(eng, lambda e: e.copy(out=out, in_=in_), r, w)
        return self.p.op(eng, lambda e: e.tensor_copy(out=out, in_=in_), r, w)

    def tt(self, eng, out, in0, in1, op, r=(), w=()):
        return self.p.op(eng, lambda e: e.tensor_tensor(out=out, in0=in0, in1=in1, op=op), r, w)

    def ts(self, eng, out, in0, s1, s2, op0, op1=None, r=(), w=()):
        if op1 is None:
            return self.p.op(eng, lambda e: e.tensor_scalar(out=out, in0=in0, scalar1=s1, scalar2=None, op0=op0), r, w)
        return self.p.op(eng, lambda e: e.tensor_scalar(out=out, in0=in0, scalar1=s1, scalar2=s2, op0=op0, op1=op1), r, w)

    def tss(self, eng, out, in_, scalar, op, r=(), w=()):
        return self.p.op(eng, lambda e: e.tensor_single_scalar(out=out, in_=in_, scalar=scalar, op=op), r, w)

    def stt(self, eng, out, in0, scalar, in1, op0, op1, r=(), w=(), accum=None):
        if accum is None:
            return self.p.op(eng, lambda e: e.scalar_tensor_tensor(out=out, in0=in0, scalar=scalar, in1=in1, op0=op0, op1=op1), r, w)
        return self.p.op(eng, lambda e: e.scalar_tensor_tensor(out=out, in0=in0, scalar=scalar, in1=in1, op0=op0, op1=op1, accum_out=accum), r, w)

    def memset(self, eng, ap, val, r=(), w=()):
        return self.p.op(eng, lambda e: e.memset(ap, val), r, w)

    def recip(self, out, in_, r=(), w=()):
        return self.p.op("vector", lambda e: e.reciprocal(out=out, in_=in_), r, w)

    def scan(self, out, d0, d1, init, r=(), w=()):
        return self.p.op("vector", lambda e: e.tensor_tensor_scan(out=out, data0=d0, data1=d1, initial=init,
                                                                  op0=ALU.mult, op1=ALU.add), r, w)

    def vmax(self, out, in_, r=(), w=()):
        return self.p.op("vector", lambda e: e.max(out=out, in_=in_), r, w)

    def vmaxidx(self, out, mx, vals, r=(), w=()):
        return self.p.op("vector", lambda e: e.max_index(out=out, in_max=mx, in_values=vals), r, w)

    def vmatchrep(self, out, mx, vals, imm, r=(), w=()):
        return self.p.op("vector", lambda e: e.match_replace(out=out, in_to_replace=mx, in_values=vals, imm_value=imm), r, w)

    def reduce(self, eng, out, in_, op, r=(), w=()):
        return self.p.op(eng, lambda e: e.tensor_reduce(out=out, in_=in_, axis=AX.X, op=op), r, w)


def build(T=4096, NS=2, TS=32, PAST=1024, NB=256, do_peer=True, debug_h=False, stop=99):
    nc = bass.Bass("TRN2", target_bir_lowering=False)

    def din(name, shape, dt=F32):
        return nc.dram_tensor(name, list(shape), dt, kind="ExternalInput").ap()

    def dout(name, shape, dt=F32):
        return nc.dram_tensor(name, list(shape), dt, kind="ExternalOutput").ap()

    def dscr(name, shape, dt):
        return nc.dram_tensor(name, list(shape), dt, kind="Internal").ap()

    NSR = NS * TS
    NROW = T + NSR
    KMAX = max(T, PAST + TS)
    NTK = (KMAX + 127) // 128
    NJ = max(1, NB // 128)

    xp = din("xp", [T, D_MODEL])
    xs = din("xs", [NSR, D_MODEL])
    ckT = din("ckT", [NS, 128, 4, PAST])
    cv = din("cv", [NS, PAST, 512])
    clf = din("clf", [NS, PAST, 8])
    sconvT = din("sconvT", [NS, 128, 4, 3])
    srnnT = din("srnnT", [NS, 128, 4])
    w_in = din("w_in", [D_MODEL, D_IN])
    bF = din("bF", [128, 8])
    cw = din("cw", [128, 4, 4])
    cb = din("cb", [128, 4])
    wa = din("wa", [128, 4, 128])
    ba = din("ba", [128, 4])
    wx = din("wx", [128, 4, 128])
    bx = din("bx", [128, 4])
    lam = din("lam", [128, 4])
    w_au = din("w_au", [512, D_MODEL])
    w_ru = din("w_ru", [512, D_MODEL])
    w_out = din("w_out", [D_MODEL, D_MODEL])
    ln1g = din("ln1g", [128, D_MODEL])
    ln1b = din("ln1b", [128, D_MODEL])
    wq = din("wq", [D_MODEL, 2048])
    k1T = din("k1T", [128, 128])
    k2T = din("k2T", [128, 128])
    pu = din("pu", [N_EXP, D_MODEL])
    pv = din("pv", [N_EXP, D_MODEL])
    ln2g = din("ln2g", [128, D_MODEL])
    ln2b = din("ln2b", [128, D_MODEL])
    c_ident = din("c_ident", [128, 128])
    c_tri = din("c_tri", [128, 128])
    c_mask = din("c_mask", [128, NJ, NB])
    c_iota = din("c_iota", [128, 16])

    y_p = dout("y_p", [T, D_MODEL])
    y_s = dout("y_s", [NSR, D_MODEL])
    k_p = dout("k_p", [T, 512])
    v_p = dout("v_p", [T, 512])
    lf_p = dout("lf_p", [T, 8])
    conv_p = dout("conv_p", [3, 512])
    rnn_p = dout("rnn_p", [1, 512])
    k_s = dout("k_s", [NSR, 512])
    v_s = dout("v_s", [NSR, 512])
    lf_s = dout("lf_s", [NSR, 8])
    conv_s = dout("conv_s", [NS, 3, 512])
    rnn_s = dout("rnn_s", [NS, 512])

    w_in_b = dscr("w_in_b", [D_MODEL, 4672], BF16)
    w_au_b = dscr("w_au_b", [512, D_MODEL], BF16)
    w_ru_b = dscr("w_ru_b", [512, D_MODEL], BF16)
    w_out_b = dscr("w_out_b", [D_MODEL, D_MODEL], BF16)
    wq_b = dscr("wq_b", [D_MODEL, 2048], BF16)
    h_scr = dout("h_scr", [NROW, D_MODEL]) if debug_h else dscr("h_scr", [NROW, D_MODEL], F32)
    uv_b = dscr("uv_b", [N_EXP, 2048], BF16)

    semstack = contextlib.ExitStack()
    with semstack:
        with contextlib.ExitStack() as st:
            def sb(name, shape, dt=F32):
                return st.enter_context(nc.sbuf_tensor(name, list(shape), dt))

            def ps(name, shape, dt=F32):
                return st.enter_context(nc.psum_tensor(name, list(shape), dt))

            p = Prog(nc)
            em = Em(p)

            ID = sb("ID", [128, 128])
            TRI = sb("TRI", [128, 128])
            ONESF = sb("ONESF", [128, 128])
            MASKF = sb("MASKF", [128, NJ, NB])
            MASK = sb("MASK", [128, NJ, NB], BF16)
            BFt = sb("BFt", [128, 8])
            CW = sb("CW", [128, 4, 4])
            CB = sb("CB", [128, 4])
            WA = sb("WA", [128, 4, 128])
            WX = sb("WX", [128, 4, 128])
            BA = sb("BA", [128, 4])
            BX = sb("BX", [128, 4])
            LAM = sb("LAM", [128, 4])
            C8 = sb("C8", [128, 4])
            C16 = sb("C16", [128, 4])
            LNG = sb("LNG", [128, D_MODEL])
            LNB = sb("LNB", [128, D_MODEL])
            WF = sb("WF", [128, 8, 8], BF16)

            KT = sb("KT", [128, 4, KMAX], BF16)
            VA = sb("VA", [128, NTK, 4, 192], BF16)
            LF = sb("LF", [128, NTK, 8])
            FF = sb("FF", [128, NTK, 8])
            BIAS = sb("BIAS", [128, NTK, 8])
            RT = sb("RT", [128, 8])
            FREF = sb("FREF", [128, 8])
            HIST = sb("HIST", [128, 4, 3])
            HST = sb("HST", [128, 4])

            NWS = 5
            WS = [sb("WS%d" % i, [128, 8, 512], BF16) for i in range(NWS)]
            XS = [sb("XS%d" % i, [128, D_MODEL]) for i in range(2)]
            XT = sb("XT", [128, 8, NB], BF16)
            QTP = sb("QTP", [128, 8, NB], BF16)
            KTOK = [sb("KTOK%d" % i, [128, 512]) for i in range(2)]
            VTOK = [sb("VTOK%d" % i, [128, 512]) for i in range(2)]
            LT = sb("LT", [128, 8])
            XRs = [sb("XR%d" % i, [128, NB + 3]) for i in range(2)]
            GGs = [sb("GG%d" % i, [128, NB]) for i in range(2)]
            XCs = [sb("XC%d" % i, [128, NB]) for i in range(2)]
            RR = sb("RR", [128, NB])
            II = sb("II", [128, NB])
            AA = sb("AA", [128, NB])
            BB = sb("BB", [128, NB])
            HS = sb("HS", [128, NB])
            T1 = sb("T1", [128, NB])
            T2 = sb("T2", [128, NB])
            ROT = sb("ROT", [128, 4, NB], BF16)
            NPB = 3
            PB = [sb("PB%d" % i, [128, NB], BF16) for i in range(NPB)]
            RECS = [sb("RECS%d" % i, [128, NB]) for i in range(1)]
            OT = sb("OT", [128, 4, NB], BF16)
            SA = sb("SA", [128, NB])
            SR = sb("SR", [128, NB])
            M1 = sb("M1", [128, NB])
            M2 = sb("M2", [128, NB])
            MT = sb("MT", [128, 8, NB], BF16)
            PRE = [sb("PRE%d" % i, [128, D_MODEL]) for i in range(2)]
            HO = [sb("HO%d" % i, [128, D_MODEL]) for i in range(2)]
            STAT = [sb("STAT%d" % i, [128, 8]) for i in range(2)]

            PZ = [ps("PZ%d" % i, [128, 512]) for i in range(2)]
            PSS = [ps("PSS%d" % i, [128, 512]) for i in range(3)]
            PO = [ps("PO%d" % i, [128, 512]) for i in range(2)]
            PX = ps("PX", [128, 512])
            if os.environ.get("KDEBUG"):
                print("phase1 sbuf remaining", nc.sbuf_bytes_remaining)

            for r0 in range(0, D_MODEL, 128):
                em.dma("gpsimd", w_in_b[r0:r0 + 128, 0:1536], w_in[r0:r0 + 128, 0:1536], w=[("w_in_b", r0 // 128, 0)])
                em.dma("gpsimd", w_in_b[r0:r0 + 128, 1536:4608], w_in[r0:r0 + 128, 1544:4616], w=[("w_in_b", r0 // 128, 1)])
                em.dma("gpsimd", w_in_b[r0:r0 + 128, 4608:4616], w_in[r0:r0 + 128, 1536:1544], w=[("w_in_b", r0 // 128, 2)])
            for r0 in range(0, 512, 128):
                em.dma("gpsimd", w_au_b[r0:r0 + 128, :], w_au[r0:r0 + 128, :], w=[("w_au_b", r0 // 128)])
                em.dma("gpsimd", w_ru_b[r0:r0 + 128, :], w_ru[r0:r0 + 128, :], w=[("w_ru_b", r0 // 128)])
            for r0 in range(0, D_MODEL, 128):
                em.dma("gpsimd", w_out_b[r0:r0 + 128, :], w_out[r0:r0 + 128, :], w=[("w_out_b", r0 // 128)])
            for r0 in range(0, D_MODEL, 128):
                em.dma("gpsimd", wq_b[r0:r0 + 128, :], wq[r0:r0 + 128, :], w=[("wq_b", r0 // 128)])
            W_IN_KEYS = [("w_in_b", i, j) for i in range(8) for j in range(3)]
            W_AU_KEYS = [("w_au_b", i) for i in range(4)]
            W_RU_KEYS = [("w_ru_b", i) for i in range(4)]
            W_OUT_KEYS = [("w_out_b", i) for i in range(8)]

            for (t_, d_, k_) in ((ID, c_ident, "ID"), (TRI, c_tri, "TRI"), (MASKF, c_mask, "MASKF"), (BFt, bF, "BFt"),
                                 (CW, cw, "CW"), (CB, cb, "CB"), (WA, wa, "WA"), (WX, wx, "WX"), (BA, ba, "BA"),
                                 (BX, bx, "BX"), (LAM, lam, "LAM"), (LNG, ln1g, "LNG"), (LNB, ln1b, "LNB")):
                em.dma("sync", t_[:], d_, w=[k_])
            em.memset("vector", ONESF[:], 1.0, w=["ONESF"])
            em.ts("vector", BA[:], BA[:], -1.0, None, ALU.mult, r=["BA"], w=["BA"])
            em.ts("vector", BX[:], BX[:], -1.0, None, ALU.mult, r=["BX"], w=["BX"])
            em.memset("vector", FF[:], 0.0, w=[("FF", kt) for kt in range(NTK)])
            em.memset("vector", LF[:], 0.0, w=[("LF", kt) for kt in range(NTK)])
            em.copy("vector", MASK[:], MASKF[:], r=["MASKF"], w=["MASK"])
            em.memset("vector", VA[:, :, :, 64:128], 1.0, w=["VA1"])
            em.memset("vector", QTP[:], 0.0, w=[("QTP", h_) for h_ in range(8)])
            em.act(C8[:], LAM[:], AF.Exp, scale=-1.0, r=["LAM"], w=["C8"])
            em.act(C8[:], C8[:], AF.Ln, bias=1.0, r=["C8"], w=["C8"])
            em.ts("vector", C16[:], C8[:], -16.0, None, ALU.mult, r=["C8"], w=["C16"])
            em.ts("vector", C8[:], C8[:], -8.0, None, ALU.mult, r=["C8", "C16"], w=["C8"])
            em.dma("sync", WF[:], w_in_b.rearrange("(kc p) c -> p kc c", p=128)[:, :, 4608:4616],
                   r=W_IN_KEYS, w=["WF"])

            ws_rr = [0]

            def load_w(src_ap, shape_part, key_reads):
                i = ws_rr[0] % NWS
                ws_rr[0] += 1
                dst = WS[i]
                npart, nk, ncol = shape_part
                em.dma("sync", dst[0:npart, 0:nk, 0:ncol], src_ap, r=key_reads, w=[("WS", i)])
                return dst, ("WS", i)

            w_in_v = w_in_b.rearrange("(kc p) c -> p kc c", p=128)
            w_out_v = w_out_b.rearrange("(kc p) c -> p kc c", p=128)
            w_ru_v = w_ru_b.rearrange("(kc p) c -> p kc c", p=128)
            w_au_v = w_au_b.rearrange("(kc p) c -> p kc c", p=128)

            rr = {"xs": 0, "pz": 0, "kt": 0, "pss": 0, "po": 0, "pb": 0, "bc": 0, "pre": 0, "rg": 0}

            def nxt(name, n):
                v = rr[name] % n
                rr[name] += 1
                return v

            def cumsum_tile(kt, rows):
                if os.environ.get("SKIP_CUMSUM"):
                    return
                em.mm(PX[0:rows, 8:16], TRI[0:rows, 0:rows], LF[0:rows, kt, :], True, True,
                      r=["TRI", ("LF", kt)], w=["PX"])
                em.mm(PX[:, 16:24], ONESF[0:rows, :], LF[0:rows, kt, :], True, True,
                      r=["ONESF", ("LF", kt)], w=["PX"])
                em.tt("vector", FF[0:rows, kt, :], PX[0:rows, 8:16], RT[0:rows, :], ALU.add,
                      r=["PX", "RT"], w=[("FF", kt)])
                em.tt("vector", RT[:], PX[:, 16:24], RT[:], ALU.add, r=["PX", "RT"], w=["RT"])

            def process_block(xsrc, t0, NBs, kbase, outs, is_last):
                k_out, v_out, lf_out, h_row0 = outs
                TP = min(128, NBs)
                ntile = NBs // TP
                kpos0 = kbase + t0
                kt_first = kpos0 // 128
                n_kt = (kpos0 + NBs + 127) // 128
                new_kts = list(range(kt_first, n_kt))

                for i in range(ntile):
                    s = nxt("xs", 2)
                    em.dma("sync", XS[s][0:TP, :], xsrc[t0 + i * TP: t0 + (i + 1) * TP, :], w=[("XS", s)])
                    for half in range(2):
                        z = nxt("pz", 2)
                        for j in range(4):
                            kc = half * 4 + j
                            em.tr(PZ[z][:, j * TP:(j + 1) * TP], XS[s][0:TP, kc * 128:(kc + 1) * 128], ID[0:TP, 0:TP],
                                  r=[("XS", s), "ID"], w=[("PZ", z)])
                        em.copy("scalar", XT[:, half * 4:(half + 1) * 4, i * TP:(i + 1) * TP],
                                PZ[z][:, 0:4 * TP].rearrange("p (j t) -> p j t", j=4), r=[("PZ", z)], w=["XT"])
                yield "A"
                if stop <= 1:
                    return

                wq_t, wq_k = load_w(w_in_v[:, :, 0:512], (128, 8, 512), W_IN_KEYS)
                for pc in range(4):
                    z = nxt("pz", 2)
                    for kc in range(8):
                        em.mm(PZ[z][:, 0:NBs], wq_t[:, kc, pc * 128:(pc + 1) * 128], XT[:, kc, 0:NBs], kc == 0, kc == 7,
                              r=[wq_k, "XT"], w=[("PZ", z)])
                    em.copy("scalar", QTP[0:64, 2 * pc, 0:NBs], PZ[z][0:64, 0:NBs], r=[("PZ", z)], w=[("QTP", 2 * pc)])
                    em.copy("scalar", QTP[64:128, 2 * pc + 1, 0:NBs], PZ[z][64:128, 0:NBs], r=[("PZ", z)], w=[("QTP", 2 * pc + 1)])
                wk_t, wk_k = load_w(w_in_v[:, :, 512:1024], (128, 8, 512), W_IN_KEYS)
                for pc in range(4):
                    z = nxt("pz", 2)
                    for kc in range(8):
                        em.mm(PZ[z][:, 0:NBs], wk_t[:, kc, pc * 128:(pc + 1) * 128], XT[:, kc, 0:NBs], kc == 0, kc == 7,
                              r=[wk_k, "XT"], w=[("PZ", z)])
                    em.copy("scalar", KT[:, pc, kpos0:kpos0 + NBs], PZ[z][:, 0:NBs], r=[("PZ", z)],
                            w=[("KT", kt) for kt in new_kts])
                wv_t, wv_k = load_w(w_in_v[:, :, 1024:1536], (128, 8, 512), W_IN_KEYS)
                tile_kts = []
                for i in range(ntile):
                    kt = (kpos0 + i * TP) // 128
                    r0 = (kpos0 + i * TP) % 128
                    assert r0 == 0
                    tile_kts.append(kt)
                    tok = slice(i * TP, (i + 1) * TP)
                    s = nxt("kt", 2)
                    z = nxt("pz", 2)
                    for kc in range(8):
                        em.mm(PZ[z][0:TP, :], XT[:, kc, tok], wk_t[:, kc, :], kc == 0, kc == 7,
                              r=[wk_k, "XT"], w=[("PZ", z)])
                    em.copy("vector", KTOK[s][0:TP, :], PZ[z][0:TP, :], r=[("PZ", z)], w=[("KTOK", s)])
                    em.dma("gpsimd", k_out[t0 + i * TP: t0 + (i + 1) * TP, :], KTOK[s][0:TP, :], r=[("KTOK", s)])
                    z = nxt("pz", 2)
                    for kc in range(8):
                        em.mm(PZ[z][0:TP, :], XT[:, kc, tok], wv_t[:, kc, :], kc == 0, kc == 7,
                              r=[wv_k, "XT"], w=[("PZ", z)])
                    em.copy("scalar", VTOK[s][0:TP, :], PZ[z][0:TP, :], r=[("PZ", z)], w=[("VTOK", s)])
                    vt4 = VTOK[s][0:TP, :].rearrange("p (c t d) -> p c t d", c=4, t=2)
                    em.copy("gpsimd", VA[0:TP, kt, :, 0:64], vt4[:, :, 0, :], r=[("VTOK", s)], w=[("VA", kt)])
                    em.copy("gpsimd", VA[0:TP, kt, :, 128:192], vt4[:, :, 1, :], r=[("VTOK", s)], w=[("VA", kt)])
                    em.dma("gpsimd", v_out[t0 + i * TP: t0 + (i + 1) * TP, :], VTOK[s][0:TP, :], r=[("VTOK", s)])
                    for kc in range(8):
                        em.mm(PX[0:TP, 0:8], XT[:, kc, tok], WF[:, kc, :], kc == 0, kc == 7, r=["WF", "XT"], w=["PX"])
                    em.tt("vector", LT[0:TP, :], PX[0:TP, 0:8], BFt[0:TP, :], ALU.add, r=["PX", "BFt"], w=["LT"])
                    em.act(LT[0:TP, :], LT[0:TP, :], AF.Exp, scale=-1.0, r=["LT"], w=["LT"])
                    em.act(LT[0:TP, :], LT[0:TP, :], AF.Ln, bias=1.0, r=["LT"], w=["LT"])
                    em.ts("vector", LF[0:TP, kt, :], LT[0:TP, :], -1.0, None, ALU.mult, r=["LT"], w=[("LF", kt)])
                    em.dma("gpsimd", lf_out[t0 + i * TP: t0 + (i + 1) * TP, :], LF[0:TP, kt, :], r=[("LF", kt)])
                if stop <= 2:
                    return

                rgbuf = {}
                rgw = {}

                def rg_inproj(cc):
                    b_ = nxt("rg", 2)
                    rgbuf[cc] = b_
                    XR, GG, XC = XRs[b_], GGs[b_], XCs[b_]
                    xck = ("XC", b_)
                    if "xr" not in rgw:
                        rgw["xr"] = load_w(w_in_v[:, :, 1536:2048], (128, 8, 512), W_IN_KEYS)
                        rgw["gate"] = load_w(w_in_v[:, :, 2048:2560], (128, 8, 512), W_IN_KEYS)
                    wr_t, wr_k = rgw["xr"]
                    wg_t, wg_k = rgw["gate"]
                    csl = slice(cc * 128, (cc + 1) * 128)
                    za = nxt("pz", 2)
                    for kc in range(8):
                        em.mm(PZ[za][:, 0:NBs], wr_t[:, kc, csl], XT[:, kc, 0:NBs], kc == 0, kc == 7,
                              r=[wr_k, "XT"], w=[("PZ", za)])
                    em.copy("vector", XR[:, 0:3], HIST[:, cc, :], r=[("HIST", cc)], w=[("XRh", b_)])
                    em.copy("scalar", XR[:, 3:3 + NBs], PZ[za][:, 0:NBs], r=[("PZ", za)], w=[("XRb", b_)])
                    zb = nxt("pz", 2)
                    for kc in range(8):
                        em.mm(PZ[zb][:, 0:NBs], wg_t[:, kc, csl], XT[:, kc, 0:NBs], kc == 0, kc == 7,
                              r=[wg_k, "XT"], w=[("PZ", zb)])
                    em.copy("scalar", GG[:, 0:NBs], PZ[zb][:, 0:NBs], r=[("PZ", zb)], w=[("GG", b_)])
                    xk = [("XRh", b_), ("XRb", b_)]
                    em.copy("vector", HIST[:, cc, :], XR[:, NBs:NBs + 3], r=xk, w=[("HIST", cc)])
                    em.ts("vector", XC[:, 0:NBs], XR[:, 0:NBs], CW[:, cc, 0:1], CB[:, cc:cc + 1], ALU.mult, ALU.add,
                          r=xk + ["CW", "CB"], w=[xck])
                    for wtap in range(1, 4):
                        em.stt("vector", XC[:, 0:NBs], XR[:, wtap:wtap + NBs], CW[:, cc, wtap:wtap + 1], XC[:, 0:NBs],
                               ALU.mult, ALU.add, r=xk + ["CW", xck], w=[xck])

                def rg_chain(cc):
                    b_ = rgbuf[cc]
                    GG, XC = GGs[b_], XCs[b_]
                    gk, xck = ("GG", b_), ("XC", b_)
                    N = slice(0, NBs)
                    z1 = nxt("pz", 2)
                    em.mm(PZ[z1][:, N], WA[:, cc, :], XC[:, N], True, True, r=["WA", xck], w=[("PZ", z1)])
                    z2 = nxt("pz", 2)
                    em.mm(PZ[z2][:, N], WX[:, cc, :], XC[:, N], True, True, r=["WX", xck], w=[("PZ", z2)])
                    em.tt("gpsimd", T2[:, N], GG[:, N], GG[:, N], ALU.mult, r=[gk], w=["T2"])
                    em.ts("gpsimd", T2[:, N], T2[:, N], 0.044715, 1.0, ALU.mult, ALU.add, r=["T2"], w=["T2"])
                    em.tt("gpsimd", T2[:, N], T2[:, N], GG[:, N], ALU.mult, r=["T2", gk], w=["T2"])
                    yield
                    em.act(RR[:, N], PZ[z1][:, N], AF.Exp, bias=BA[:, cc:cc + 1], scale=-1.0, r=[("PZ", z1), "BA"], w=["RR"])
                    em.act(II[:, N], PZ[z2][:, N], AF.Exp, bias=BX[:, cc:cc + 1], scale=-1.0, r=[("PZ", z2), "BX"], w=["II"])
                    em.act(T2[:, N], T2[:, N], AF.Exp, scale=-GELU_C, r=["T2"], w=["T2"])
                    yield
                    em.ts("vector", RR[:, N], RR[:, N], 1.0, None, ALU.add, r=["RR"], w=["RR"])
                    em.recip(RR[:, N], RR[:, N], r=["RR"], w=["RR"])
                    yield
                    em.act(AA[:, N], RR[:, N], AF.Exp, scale=C8[:, cc:cc + 1], r=["RR", "C8"], w=["AA"])
                    em.act(T1[:, N], RR[:, N], AF.Exp, scale=C16[:, cc:cc + 1], r=["RR", "C16"], w=["T1"])
                    em.ts("vector", II[:, N], II[:, N], 1.0, None, ALU.add, r=["II"], w=["II"])
                    em.recip(II[:, N], II[:, N], r=["II"], w=["II"])
                    yield
                    em.ts("vector", T1[:, N], T1[:, N], -1.0, 1.0, ALU.mult, ALU.add, r=["T1"], w=["T1"])
                    em.tt("vector", BB[:, N], II[:, N], XC[:, N], ALU.mult, r=["II", xck], w=["BB"])
                    em.ts("gpsimd", T2[:, N], T2[:, N], 1.0, None, ALU.add, r=["T2"], w=["T2"])
                    yield
                    em.act(T1[:, N], T1[:, N], AF.Ln, r=["T1"], w=["T1"])
                    em.recip(T2[:, N], T2[:, N], r=["T2"], w=["T2"])
                    yield
                    em.act(T1[:, N], T1[:, N], AF.Exp, scale=0.5, r=["T1"], w=["T1"])
                    em.tt("gpsimd", T2[:, N], T2[:, N], GG[:, N], ALU.mult, r=["T2", gk], w=["T2"])
                    yield
                    em.tt("vector", BB[:, N], BB[:, N], T1[:, N], ALU.mult, r=["BB", "T1"], w=["BB"])
                    yield
                    em.scan(HS[:, N], AA[:, N], BB[:, N], HST[:, cc:cc + 1], r=["AA", "BB", ("HST", cc)], w=["HS"])
                    yield
                    em.copy("vector", HST[:, cc:cc + 1], HS[:, NBs - 1:NBs], r=["HS"], w=[("HST", cc)])
                    em.tt("vector", ROT[:, cc, N], HS[:, N], T2[:, N], ALU.mult, r=["HS", "T2"], w=["ROT"])

                rg_state = {"next_cc": 0, "gen": None}

                def rg_step(flush=False):
                    while True:
                        if rg_state["gen"] is None:
                            cc = rg_state["next_cc"]
                            if cc >= 4:
                                return
                            rg_state["gen"] = rg_chain(cc)
                        try:
                            next(rg_state["gen"])
                        except StopIteration:
                            rg_state["gen"] = None
                            rg_state["next_cc"] += 1
                            if rg_state["next_cc"] < 4:
                                rg_inproj(rg_state["next_cc"])
                        if not flush:
                            return

                rg_inproj(0)
                if len(tile_kts) == 1:
                    em.copy("vector", FREF[:], RT[:], r=["RT"], w=["FREF"])
                for ti_, kt in enumerate(tile_kts):
                    cumsum_tile(kt, TP)
                    if len(tile_kts) > 1 and ti_ == len(tile_kts) // 2 - 1:
                        em.copy("vector", FREF[:], RT[:], r=["RT"], w=["FREF"])
                if stop <= 3:
                    rg_step(flush=True)
                    return

                em.tt("vector", BIAS[:, 0:n_kt, :], FREF[:, :].unsqueeze(1).to_broadcast([128, n_kt, 8]),
                      FF[:, 0:n_kt, :], ALU.subtract, r=["FREF"] + [("FF", kt) for kt in range(n_kt)], w=["BIAS"])
                items = [(h, kt) for h in range(8) for kt in range(n_kt)]
                po_of = {}

                def kp_of(kt):
                    return min(128, kpos0 + NBs - kt * 128)

                def emit_qk(i):
                    h, kt = items[i]
                    kp = kp_of(kt)
                    s = i % 3
                    em.mm(PSS[s][0:kp, 0:NBs], KT[:, h // 2, kt * 128:kt * 128 + kp], QTP[:, h, 0:NBs],
                          True, True, r=[("KT", kt), ("QTP", h)], w=[("PSS", s)])

                def emit_exp(i):
                    h, kt = items[i]
                    kp = kp_of(kt)
                    s = i % 3
                    b = i % NPB
                    em.act(PB[b][0:kp, 0:NBs], PSS[s][0:kp, 0:NBs], AF.Exp, bias=BIAS[0:kp, kt, h:h + 1], scale=0.125,
                           r=[("PSS", s), "BIAS"], w=[("PB", b)])
                    if kt >= kt_first:
                        j = kt - kt_first
                        em.tt("gpsimd", PB[b][0:kp, 0:NBs], PB[b][0:kp, 0:NBs], MASK[0:kp, j, 0:NBs], ALU.mult,
                              r=[("PB", b), "MASK"], w=[("PB", b)])

                def emit_pv(i):
                    h, kt = items[i]
                    kp = kp_of(kt)
                    b = i % NPB
                    if kt == 0:
                        po_of[h] = nxt("po", 2)
                    o = po_of[h]
                    pc, odd = h // 2, h % 2
                    lhsT = VA[0:kp, kt, pc, 64:192] if odd else VA[0:kp, kt, pc, 0:128]
                    em.mm(PO[o][:, 0:NBs], lhsT, PB[b][0:kp, 0:NBs], kt == 0, kt == n_kt - 1,
                          r=[("VA", kt), "VA1", ("PB", b)], w=[("PO", o)])
                    if kt == n_kt - 1:
                        c = 0
                        osl = slice(64, 128) if odd else slice(0, 64)
                        dsl = slice(0, 64) if odd else slice(64, 128)
                        em.recip(RECS[c][osl, 0:NBs], PO[o][dsl, 0:NBs], r=[("PO", o)], w=[("RECS", c)])
                        em.tt("vector", OT[osl, pc, 0:NBs], PO[o][osl, 0:NBs], RECS[c][osl, 0:NBs], ALU.mult,
                              r=[("PO", o), ("RECS", c)], w=["OT"])

                emit_qk(0)
                if len(items) > 1:
                    emit_qk(1)
                for i in range(len(items)):
                    if i + 2 < len(items):
                        emit_qk(i + 2)
                    emit_exp(i)
                    emit_pv(i)
                    rg_step()
                rg_step(flush=True)
                if stop <= 4:
                    return

                for hf in range(2):
                    cs = slice(hf * 512, (hf + 1) * 512)
                    ga_t, ga_k = load_w(w_in_v[:, :, 2560 + hf * 512:2560 + (hf + 1) * 512], (128, 8, 512), W_IN_KEYS)
                    gr_t, gr_k = load_w(w_in_v[:, :, 3584 + hf * 512:3584 + (hf + 1) * 512], (128, 8, 512), W_IN_KEYS)
                    i_ = ws_rr[0] % NWS
                    ws_rr[0] += 1
                    em.dma("sync", WS[i_][:, 0:4, :], w_au_v[:, :, cs], r=W_AU_KEYS, w=[("WS", i_)])
                    em.dma("sync", WS[i_][:, 4:8, :], w_ru_v[:, :, cs], r=W_RU_KEYS, w=[("WS", i_)])
                    au_t, au_k = WS[i_], ("WS", i_)
                    ru_k = au_k
                    for fc in range(4):
                        fs = slice(fc * 128, (fc + 1) * 128)
                        for kc in range(8):
                            em.mm(PZ[0][:, 0:NBs], ga_t[:, kc, fs], XT[:, kc, 0:NBs], kc == 0, kc == 7,
                                  r=[ga_k, "XT"], w=[("PZ", 0)])
                        em.act(SA[:, 0:NBs], PZ[0][:, 0:NBs], AF.Sigmoid, r=[("PZ", 0)], w=["SA"])
                        for kc in range(8):
                            em.mm(PZ[1][:, 0:NBs], gr_t[:, kc, fs], XT[:, kc, 0:NBs], kc == 0, kc == 7,
                                  r=[gr_k, "XT"], w=[("PZ", 1)])
                        em.act(SR[:, 0:NBs], PZ[1][:, 0:NBs], AF.Sigmoid, r=[("PZ", 1)], w=["SR"])
                        for kc in range(4):
                            em.mm(PSS[0][:, 0:NBs], au_t[:, kc, fs], OT[:, kc, 0:NBs], kc == 0, kc == 3,
                                  r=[au_k, "OT"], w=[("PSS", 0)])
                        em.tt("vector", M1[:, 0:NBs], PSS[0][:, 0:NBs], SA[:, 0:NBs], ALU.mult, r=[("PSS", 0), "SA"], w=["M1"])
                        for kc in range(4):
                            em.mm(PSS[1][:, 0:NBs], au_t[:, 4 + kc, fs], ROT[:, kc, 0:NBs], kc == 0, kc == 3,
                                  r=[ru_k, "ROT"], w=[("PSS", 1)])
                        em.tt("vector", M2[:, 0:NBs], PSS[1][:, 0:NBs], SR[:, 0:NBs], ALU.mult, r=[("PSS", 1), "SR"], w=["M2"])
                        em.tt("gpsimd", MT[:, hf * 4 + fc, 0:NBs], M1[:, 0:NBs], M2[:, 0:NBs], ALU.add,
                              r=["M1", "M2"], w=["MT"])
                yield "E"
                if stop <= 5:
                    return

                wo0_t, wo0_k = load_w(w_out_v[:, :, 0:512], (128, 8, 512), W_OUT_KEYS)
                wo1_t, wo1_k = load_w(w_out_v[:, :, 512:1024], (128, 8, 512), W_OUT_KEYS)
                for i in range(ntile):
                    tok = slice(i * TP, (i + 1) * TP)
                    s = nxt("xs", 2)
                    em.dma("sync", XS[s][0:TP, :], xsrc[t0 + i * TP: t0 + (i + 1) * TP, :], w=[("XS", s)])
                    q = nxt("pre", 2)
                    for hf, (wt, wk) in enumerate(((wo0_t, wo0_k), (wo1_t, wo1_k))):
                        for kc in range(8):
                            em.mm(PO[hf][0:TP, :], MT[:, kc, tok], wt[:, kc, :], kc == 0, kc == 7,
                                  r=[wk, "MT"], w=[("PO", hf)])
                        em.stt("vector", PRE[q][0:TP, hf * 512:(hf + 1) * 512], XS[s][0:TP, hf * 512:(hf + 1) * 512],
                               ALPHA, PO[hf][0:TP, :], ALU.mult, ALU.add, r=[("XS", s), ("PO", hf)], w=[("PRE", q, hf)])
                    layer_norm(em, PRE[q], HO[q], None, STAT[q], LNG, LNB, TP,
                               [("PRE", q, 0), ("PRE", q, 1)], ("HO", q), ("STAT", q), "LNG", "LNB")
                    em.dma("gpsimd", h_scr[h_row0 + t0 + i * TP: h_row0 + t0 + (i + 1) * TP, :], HO[q][0:TP, :],
                           r=[("HO", q)], w=[("h_scr", (h_row0 + t0 + i * TP) // 32)])

            em.memset("vector", RT[:], 0.0, w=["RT"])
            em.memset("vector", HIST[:], 0.0, w=[("HIST", c) for c in range(4)])
            em.memset("vector", HST[:], 0.0, w=[("HST", c) for c in range(4)])
            UVCH = 512
            uv_jobs = []
            for r0 in range(0, N_EXP, UVCH):
                uv_jobs.append((uv_b[r0:r0 + UVCH, 0:1024], pu[r0:r0 + UVCH, :]))
                uv_jobs.append((uv_b[r0:r0 + UVCH, 1024:2048], pv[r0:r0 + UVCH, :]))
            nblk = T // NB if stop >= 1 else 0
            per_blk = (len(uv_jobs) + max(nblk, 1) - 1) // max(nblk, 1)
            def run_to(gen, tag):
                for t_ in gen:
                    if t_ == tag:
                        return
            gens_p = [process_block(xp, blk * NB, NB, 0, (k_p, v_p, lf_p, 0), blk == T // NB - 1) for blk in range(nblk)]
            if nblk:
                run_to(gens_p[0], "A")
            for blk in range(nblk):
                if do_peer:
                    for (o_, i_) in uv_jobs[blk * per_blk:(blk + 1) * per_blk]:
                        em.dma("gpsimd", o_, i_)
                run_to(gens_p[blk], "E")
                if blk + 1 < nblk:
                    run_to(gens_p[blk + 1], "A")
                run_to(gens_p[blk], None)
            if do_peer and nblk == 0:
                for (o_, i_) in uv_jobs:
                    em.dma("gpsimd", o_, i_)
            for c_ in range(4):
                em.dma("sync", conv_p.rearrange("w (cc p) -> p cc w", p=128)[:, c_, :], HIST[:, c_, :],
                       r=[("HIST", c_)], allow_slow_non_contiguous=True)
            em.dma("sync", rnn_p.rearrange("o (cc p) -> p (o cc)", p=128), HST[:],
                   r=[("HST", c) for c in range(4)], allow_slow_non_contiguous=True)

            NPT = PAST // 128
            for s_ in range(NS if stop >= 7 else 0):
                em.dma("gpsimd", KT[:, :, 0:PAST], ckT[s_], w=[("KT", kt) for kt in range(NPT)])
                for kt in range(NPT):
                    s = nxt("kt", 2)
                    em.dma("sync", VTOK[s][:, :], cv[s_, kt * 128:(kt + 1) * 128, :], w=[("VTOK", s)])
                    vt4 = VTOK[s][:, :].rearrange("p (c t d) -> p c t d", c=4, t=2)
                    em.copy("vector", VA[:, kt, :, 0:64], vt4[:, :, 0, :], r=[("VTOK", s)], w=[("VA", kt)])
                    em.copy("vector", VA[:, kt, :, 128:192], vt4[:, :, 1, :], r=[("VTOK", s)], w=[("VA", kt)])
                em.dma("sync", LF[:, 0:NPT, :], clf[s_].rearrange("(kt p) h -> p kt h", p=128),
                       w=[("LF", kt) for kt in range(NPT)])
                em.memset("vector", RT[:], 0.0, w=["RT"])
                for kt in range(NPT):
                    cumsum_tile(kt, 128)
                em.dma("sync", HIST[:], sconvT[s_], w=[("HIST", c) for c in range(4)])
                em.dma("sync", HST[:], srnnT[s_], w=[("HST", c) for c in range(4)])
                for _ in process_block(xs[s_ * TS:(s_ + 1) * TS, :], 0, TS, PAST,
                                       (k_s[s_ * TS:(s_ + 1) * TS, :], v_s[s_ * TS:(s_ + 1) * TS, :],
                                        lf_s[s_ * TS:(s_ + 1) * TS, :], T + s_ * TS), True):
                    pass
                for c_ in range(4):
                    em.dma("sync", conv_s[s_].rearrange("w (cc p) -> p cc w", p=128)[:, c_, :], HIST[:, c_, :],
                           r=[("HIST", c_)], allow_slow_non_contiguous=True)
                em.dma("sync", rnn_s[s_:s_ + 1, :].rearrange("o (cc p) -> p (o cc)", p=128), HST[:],
                       r=[("HST", c) for c in range(4)], allow_slow_non_contiguous=True)

            p.finish(semstack, "a")

        with contextlib.ExitStack() as st:
            def sb(name, shape, dt=F32):
                return st.enter_context(nc.sbuf_tensor(name, list(shape), dt))

            def ps(name, shape, dt=F32):
                return st.enter_context(nc.psum_tensor(name, list(shape), dt))

            p = Prog(nc)
            em = Em(p)
            ID = sb("ID2", [128, 128])
            IDB = sb("IDB", [128, 128], BF16)
            IOTA = sb("IOTA", [128, 16])
            LNG = sb("LNG2", [128, D_MODEL])
            LNB = sb("LNB2", [128, D_MODEL])
            WQ = sb("WQ", [128, 8, 2048], BF16)
            KEYF = sb("KEYF", [128, 2, 128])
            KEYB = sb("KEYB", [128, 2, 128], BF16)
            H = [sb("H%d" % i, [128, D_MODEL]) for i in range(2)]
            HB = [sb("HB%d" % i, [128, D_MODEL], BF16) for i in range(2)]
            HT = sb("HT", [128, 8, 128], BF16)
            QYT = sb("QYT", [128, 16, 128], BF16)
            SC = sb("SC", [128, 16, 128])
            SC2 = sb("SC2", [128, 16, 128])
            MX = sb("MX", [128, 16, 16])
            MI = sb("MI", [128, 16, 16], U32)
            MIF = sb("MIF", [128, 16, 16])
            CAND = sb("CAND", [128, 8, 256])
            CAND2 = sb("CAND2", [128, 8, 256])
            TOPV = sb("TOPV", [128, 8, 16])
            SEL = sb("SEL", [128, 8, 16], U32)
            SELA = sb("SELA", [128, 8, 16], U32)
            SELB = sb("SELB", [128, 8, 16], U32)
            AF_ = sb("AFl", [128, 8, 16])
            BF_ = sb("BFl", [128, 8, 16])
            EQ = sb("EQ", [128, 8, 16, 16])
            I1S = sb("I1S", [128, 8, 16])
            I2S = sb("I2S", [128, 8, 16])
            IDXF = sb("IDXF", [128, 128])
            IDX = [sb("IDX%d" % i, [128, 128], I32) for i in range(2)]
            GW = [sb("GW%d" % i, [128, 8, 16]) for i in range(2)]
            GS = sb("GS", [128, 8])
            NGRP = 32
            NGA = 4
            GSZ = 128 // NGRP
            ACT = [sb("ACT%d" % i, [128, GSZ]) for i in range(NGA)]
            ACT2 = [sb("ACTb%d" % i, [128, GSZ]) for i in range(NGA)]
            WGT = [sb("WGT%d" % i, [128, GSZ]) for i in range(NGA)]
            NUV = 16
            UV = [sb("UV%d" % i, [128, 2048], BF16) for i in range(NUV)]
            NDG = 4
            DG = [sb("DG%d" % i, [128, 128], BF16) for i in range(NDG)]
            JUNKB = sb("JUNKB", [128, D_MODEL], BF16)
            JUNKC = sb("JUNKC", [128, D_MODEL], BF16)
            NPRD = 4
            PRD = [sb("PRD%d" % i, [128, D_MODEL], BF16) for i in range(NPRD)]
            PRE = [sb("PREb%d" % i, [128, D_MODEL]) for i in range(2)]
            YO = [sb("YO%d" % i, [128, D_MODEL]) for i in range(2)]
            STAT = [sb("STATb%d" % i, [128, 8]) for i in range(2)]

            PG = [ps("PG%d" % i, [128, 512]) for i in range(4)]
            POUT = [ps("POUT%d" % i, [128, 512]) for i in range(4)]
            if os.environ.get("KDEBUG"):
                print("phase2 sbuf remaining", nc.sbuf_bytes_remaining)

            em.dma("sync", ID[:], c_ident, w=["ID"])
            em.copy("vector", IDB[:], ID[:], r=["ID"], w=["IDB"])
            em.dma("sync", IOTA[:], c_iota, w=["IOTA"])
            em.dma("sync", LNG[:], ln2g, w=["LNG"])
            em.dma("sync", LNB[:], ln2b, w=["LNB"])
            em.dma("sync", KEYF[:, 0, :], k1T, w=["KEYF0"])
            em.dma("sync", KEYF[:, 1, :], k2T, w=["KEYF1"])
            em.copy("vector", KEYB[:], KEYF[:], r=["KEYF0", "KEYF1"], w=["KEYB"])
            wq_v = wq_b.rearrange("(kc p) c -> p kc c", p=128)
            for kc in range(8):
                em.dma("sync", WQ[:, kc, :], wq_v[:, kc, :], w=[("WQ", kc)])
            WQK = [("WQ", kc) for kc in range(8)]

            rr2 = {"h": 0, "pq": 0, "pg": 0, "uv": 0, "dg": 0, "pre": 0, "idx": 0, "grp": 0, "prd": 0}

            def nxt2(name, n):
                v = rr2[name] % n
                rr2[name] += 1
                return v

            def prologue(row0, n, ctx):
                hs_ = nxt2("h", 2)
                Ht = H[hs_]
                Hb = HB[hs_]
                em.dma("sync", Ht[0:n, :], h_scr[row0:row0 + n, :], w=[("H", hs_)])
                em.copy("scalar", Hb[0:n, :], Ht[0:n, :], r=[("H", hs_)], w=[("HB", hs_)])
                for half in range(2):
                    z = nxt2("pg", 4)
                    for j in range(4):
                        kc = half * 4 + j
                        em.tr(PG[z][:, j * n:(j + 1) * n], Ht[0:n, kc * 128:(kc + 1) * 128], ID[0:n, 0:n],
                              r=[("H", hs_), "ID"], w=[("PG", z)])
                    em.copy("scalar", HT[:, half * 4:(half + 1) * 4, 0:n],
                            PG[z][:, 0:4 * n].rearrange("p (j t) -> p j t", j=4), r=[("PG", z)], w=["HT"])
                yield
                for c4 in range(4):
                    if c4 > 0:
                        yield
                    z = nxt2("pg", 4)
                    for cj in range(4):
                        c = c4 * 4 + cj
                        for kc in range(8):
                            em.mm(PG[z][:, cj * n:(cj + 1) * n], WQ[:, kc, c * 128:(c + 1) * 128], HT[:, kc, 0:n],
                                  kc == 0, kc == 7, r=WQK + ["HT"], w=[("PG", z)])
                    em.copy("scalar", QYT[:, c4 * 4:(c4 + 1) * 4, 0:n],
                            PG[z][:, 0:4 * n].rearrange("p (j t) -> p j t", j=4), r=[("PG", z)], w=[("QYT", c4)])
                yield
                for c4 in range(4):
                    z = nxt2("pg", 4)
                    for cj in range(4):
                        c = c4 * 4 + cj
                        em.mm(PG[z][0:n, cj * 128:(cj + 1) * 128], QYT[:, c, 0:n], KEYB[:, c % 2, :], True, True,
                              r=[("QYT", c4), "KEYB"], w=[("PG", z)])
                    em.copy("vector", SC[0:n, c4 * 4:(c4 + 1) * 4, :],
                            PG[z][0:n, :].rearrange("p (j k) -> p j k", j=4), r=[("PG", z)], w=[("SC", c4)])
                for c in range(16):
                    if c % 2 == 0:
                        yield
                    ck = ("SC", c // 4)
                    em.vmax(MX[0:n, c, 0:8], SC[0:n, c, :], r=[ck], w=[("MX", c)])
                    em.vmaxidx(MI[0:n, c, 0:8], MX[0:n, c, 0:8], SC[0:n, c, :], r=[ck, ("MX", c)], w=[("MI", c)])
                    em.vmatchrep(SC2[0:n, c, :], MX[0:n, c, 0:8], SC[0:n, c, :], -1e30, r=[ck, ("MX", c)], w=[("SC2", c)])
                    em.vmax(MX[0:n, c, 8:16], SC2[0:n, c, :], r=[("SC2", c)], w=[("MXb", c)])
                    em.vmaxidx(MI[0:n, c, 8:16], MX[0:n, c, 8:16], SC2[0:n, c, :], r=[("SC2", c), ("MXb", c)], w=[("MIb", c)])
                MXK = [("MX", c) for c in range(16)] + [("MXb", c) for c in range(16)]
                MIK = [("MI", c) for c in range(16)] + [("MIb", c) for c in range(16)]
                em.copy("vector", MIF[0:n], MI[0:n], r=MIK, w=["MIF"])
                yield
                MXv = MX[0:n].rearrange("p (h t) k -> p h t k", t=2)
                em.tt("vector", CAND[0:n].rearrange("p h (a b) -> p h a b", a=16),
                      MXv[:, :, 0, :].unsqueeze(3).to_broadcast([n, 8, 16, 16]),
                      MXv[:, :, 1, :].unsqueeze(2).to_broadcast([n, 8, 16, 16]), ALU.add, r=MXK, w=["CAND"])
                for h in range(8):
                    yield
                    em.vmax(TOPV[0:n, h, 0:8], CAND[0:n, h, :], r=["CAND"], w=[("TV", h)])
                    em.vmaxidx(SEL[0:n, h, 0:8], TOPV[0:n, h, 0:8], CAND[0:n, h, :], r=["CAND", ("TV", h)], w=[("SEL", h)])
                    em.vmatchrep(CAND2[0:n, h, :], TOPV[0:n, h, 0:8], CAND[0:n, h, :], -1e30, r=["CAND", ("TV", h)], w=[("C2", h)])
                    em.vmax(TOPV[0:n, h, 8:16], CAND2[0:n, h, :], r=[("C2", h)], w=[("TVb", h)])
                    em.vmaxidx(SEL[0:n, h, 8:16], TOPV[0:n, h, 8:16], CAND2[0:n, h, :], r=[("C2", h), ("TVb", h)], w=[("SELb", h)])
                TVK = [("TV", h) for h in range(8)] + [("TVb", h) for h in range(8)]
                SELK = [("SEL", h) for h in range(8)] + [("SELb", h) for h in range(8)]
                yield
                em.tss("vector", SELB[0:n], SEL[0:n], 15, ALU.bitwise_and, r=SELK, w=["SELB"])
                em.tss("vector", SELA[0:n], SEL[0:n], 4, ALU.logical_shift_right, r=SELK, w=["SELA"])
                em.copy("vector", AF_[0:n], SELA[0:n], r=["SELA"], w=["AFl"])
                em.copy("vector", BF_[0:n], SELB[0:n], r=["SELB"], w=["BFl"])
                yield
                MIFv = MIF[0:n].rearrange("p (h t) k -> p h t k", t=2)
                iota_b = IOTA[0:n, :].unsqueeze(1).unsqueeze(1).to_broadcast([n, 8, 16, 16])
                for (sel_f, tsel, dst, dk) in ((AF_, 0, I1S, "I1S"), (BF_, 1, I2S, "I2S")):
                    em.tt("vector", EQ[0:n], iota_b, sel_f[0:n].unsqueeze(3).to_broadcast([n, 8, 16, 16]), ALU.is_equal,
                          r=["IOTA", "AFl", "BFl"], w=["EQ"])
                    yield
                    em.tt("vector", EQ[0:n], EQ[0:n], MIFv[:, :, tsel, :].unsqueeze(2).to_broadcast([n, 8, 16, 16]), ALU.mult,
                          r=["EQ", "MIF"], w=["EQ"])
                    yield
                    em.reduce("vector", dst[0:n], EQ[0:n], ALU.add, r=["EQ"], w=[dk])
                    yield
                ix = nxt2("idx", 2)
                em.stt("vector", IDXF[0:n, :], I1S[0:n].rearrange("p h k -> p (h k)"), 128.0,
                       I2S[0:n].rearrange("p h k -> p (h k)"), ALU.mult, ALU.add, r=["I1S", "I2S"], w=["IDXF"])
                em.copy("vector", IDX[ix][0:n, :], IDXF[0:n, :], r=["IDXF"], w=[("IDX", ix)])
                yield
                GWt = GW[ix]
                gwk = ("GW", ix)
                em.tt("vector", GWt[0:n], TOPV[0:n], TOPV[0:n, :, 0:1].to_broadcast([n, 8, 16]), ALU.subtract, r=TVK, w=[gwk])
                em.act(GWt[0:n], GWt[0:n], AF.Exp, r=[gwk], w=[gwk])
                em.reduce("vector", GS[0:n], GWt[0:n], ALU.add, r=[gwk], w=["GS"])
                em.recip(GS[0:n], GS[0:n], r=["GS"], w=["GS"])
                em.tt("vector", GWt[0:n], GWt[0:n], GS[0:n].unsqueeze(2).to_broadcast([n, 8, 16]), ALU.mult, r=[gwk, "GS"], w=[gwk])
                ctx.update(hs_=hs_, ix=ix, gwk=gwk, GWt=GWt)

            grp_bufs = {}

            def emit_dots_dve(t, n, ctx, grp):
                hs_, ix = ctx["hs_"], ctx["ix"]
                Hb = HB[hs_]
                ga = grp % NGA
                A_ = ACT[ga]
                bufs = []
                pend = []
                for k in range(GSZ):
                    s = grp * GSZ + k
                    g = nxt2("uv", NUV)
                    bufs.append(g)
                    em.gather(UV[g][0:n, :], uv_b, IDX[ix][0:n, s:s + 1], r=[("IDX", ix)], w=[("UV", g)])
                    if k % 4 == 0:
                        em.stt("vector", JUNKB[0:n, :], UV[g][0:n, 0:1024], 1.0, Hb[0:n, :], ALU.mult, ALU.mult,
                               r=[("UV", g), ("HB", hs_)], w=["JUNKB", ("ACT", ga, k)], accum=A_[0:n, k:k + 1])
                    else:
                        pj = nxt2("prd", NPRD)
                        em.tt("vector", PRD[pj][0:n, :], UV[g][0:n, 0:1024], Hb[0:n, :], ALU.mult,
                              r=[("UV", g), ("HB", hs_)], w=[("PRD", pj)])
                        pend.append((pj, k))
                grp_bufs[(t, grp)] = (bufs, pend)

            def emit_dots_act(t, n, ctx, grp):
                ga = grp % NGA
                A_ = ACT[ga]
                for (pj, k) in grp_bufs[(t, grp)][1]:
                    em.act(JUNKC[0:n, :], PRD[pj][0:n, :], AF.Identity, accum=A_[0:n, k:k + 1],
                           r=[("PRD", pj)], w=["JUNKC", ("ACT", ga, k)])

            def emit_tail_a(t, n, ctx, grp):
                gwk, GWt = ctx["gwk"], ctx["GWt"]
                GWf = GWt[0:n].rearrange("p h k -> p (h k)")
                ga = grp % NGA
                A_, A2_, W_ = ACT[ga], ACT2[ga], WGT[ga]
                AK = [("ACT", ga, k) for k in range(GSZ)]
                em.act(A2_[0:n], A_[0:n], AF.Gelu_apprx_tanh, r=AK, w=[("ACT2", ga)])
                em.tt("vector", W_[0:n], A2_[0:n], GWf[:, grp * GSZ:(grp + 1) * GSZ], ALU.mult,
                      r=[("ACT2", ga), gwk], w=[("WGT", ga)])

            def emit_tail_b(t, n, ctx, grp):
                ga = grp % NGA
                W_ = WGT[ga]
                bufs = grp_bufs.pop((t, grp))[0]
                po = (t % 2) * 2
                for k in range(GSZ):
                    s = grp * GSZ + k
                    g = bufs[k]
                    d = nxt2("dg", NDG)
                    em.act(DG[d][0:n, 0:n], IDB[0:n, 0:n], AF.Copy, scale=W_[0:n, k:k + 1],
                           r=["IDB", ("WGT", ga)], w=[("DG", d)])
                    for hf in range(2):
                        em.mm(POUT[po + hf][0:n, :], DG[d][0:n, 0:n], UV[g][0:n, 1024 + hf * 512:1024 + (hf + 1) * 512],
                              s == 0, s == 127, r=[("DG", d), ("UV", g)], w=[("POUT", po + hf)])

            def epilogue(t, n, ctx, y_dst):
                po = (t % 2) * 2
                hs_ = ctx["hs_"]
                Ht = H[hs_]
                q = nxt2("pre", 2)
                for hf in range(2):
                    em.stt("vector", PRE[q][0:n, hf * 512:(hf + 1) * 512], Ht[0:n, hf * 512:(hf + 1) * 512],
                           ALPHA, POUT[po + hf][0:n, :], ALU.mult, ALU.add, r=[("H", hs_), ("POUT", po + hf)], w=[("PRE", q, hf)])
                layer_norm(em, PRE[q], YO[q], None, STAT[q], LNG, LNB, n,
                           [("PRE", q, 0), ("PRE", q, 1)], ("YO", q), ("STAT", q), "LNG", "LNB", gb_eng="vector")
                em.dma("sync", y_dst, YO[q][0:n, :], r=[("YO", q)])

            if do_peer:
                tiles = [(ti * 128, 128, y_p[ti * 128:(ti + 1) * 128, :]) for ti in range(T // 128)]
                if NSR > 0:
                    tiles.append((T, NSR, y_s[:, :]))
                ctxs = [dict() for _ in tiles]
                for _ in prologue(tiles[0][0], tiles[0][1], ctxs[0]):
                    pass
                items = [(t, grp) for t in range(len(tiles)) for grp in range(NGRP)]
                gens = {}
                emit_dots_dve(0, tiles[0][1], ctxs[0], 0)
                emit_dots_act(0, tiles[0][1], ctxs[0], 0)
                for i, (t, grp) in enumerate(items):
                    row0, n, y_dst = tiles[t]
                    if grp == 0 and t + 1 < len(tiles):
                        gens[t + 1] = prologue(tiles[t + 1][0], tiles[t + 1][1], ctxs[t + 1])
                    nx = items[i + 1] if i + 1 < len(items) else None
                    if nx is not None and nx[0] != t and (nx[0] in gens):
                        for _ in gens.pop(nx[0]):
                            pass
                    emit_tail_a(t, n, ctxs[t], grp)
                    if nx is not None:
                        emit_dots_dve(nx[0], tiles[nx[0]][1], ctxs[nx[0]], nx[1])
                    emit_tail_b(t, n, ctxs[t], grp)
                    if nx is not None:
                        emit_dots_act(nx[0], tiles[nx[0]][1], ctxs[nx[0]], nx[1])
                    if (t + 1) in gens:
                        try:
                            next(gens[t + 1])
                        except StopIteration:
                            gens.pop(t + 1)
                    if grp == NGRP - 1:
                        epilogue(t, n, ctxs[t], y_dst)
            p.finish(semstack, "b")
    return nc


def layer_norm(em, X, Y, JUNK, ST, G, B, n, xkeys, ykey, stkey, gk, bk, gb_eng="gpsimd"):
    jk = "JUNK"
    if JUNK is None:
        JUNK, jk = Y, ykey
    em.act(JUNK[0:n, :], X[0:n, :], AF.Identity, accum=ST[0:n, 0:1], r=xkeys, w=[jk, (stkey, 0)])
    em.act(JUNK[0:n, :], X[0:n, :], AF.Square, accum=ST[0:n, 1:2], r=xkeys, w=[jk, (stkey, 1)])
    em.ts("vector", ST[0:n, 2:3], ST[0:n, 0:1], 1.0 / D_MODEL, None, ALU.mult, r=[(stkey, 0)], w=[(stkey, 2)])
    em.ts("vector", ST[0:n, 3:4], ST[0:n, 1:2], 1.0 / D_MODEL, None, ALU.mult, r=[(stkey, 1)], w=[(stkey, 3)])
    em.tt("vector", ST[0:n, 4:5], ST[0:n, 2:3], ST[0:n, 2:3], ALU.mult, r=[(stkey, 2)], w=[(stkey, 4)])
    em.tt("vector", ST[0:n, 5:6], ST[0:n, 3:4], ST[0:n, 4:5], ALU.subtract, r=[(stkey, 3), (stkey, 4)], w=[(stkey, 5)])
    em.ts("vector", ST[0:n, 6:7], ST[0:n, 5:6], LN_EPS, None, ALU.add, r=[(stkey, 5)], w=[(stkey, 6)])
    em.act(ST[0:n, 6:7], ST[0:n, 6:7], AF.Ln, r=[(stkey, 6)], w=[(stkey, 6)])
    em.act(ST[0:n, 7:8], ST[0:n, 6:7], AF.Exp, scale=-0.5, r=[(stkey, 6)], w=[(stkey, 7)])
    em.ts("vector", Y[0:n, :], X[0:n, :], ST[0:n, 2:3], ST[0:n, 7:8], ALU.subtract, ALU.mult,
          r=xkeys + [(stkey, 2), (stkey, 7)], w=[ykey])
    em.tt(gb_eng, Y[0:n, :], Y[0:n, :], G[0:n, :], ALU.mult, r=[ykey, gk], w=[ykey])
    em.tt(gb_eng, Y[0:n, :], Y[0:n, :], B[0:n, :], ALU.add, r=[ykey, bk], w=[ykey])


def _fm(v):
    return np.ascontiguousarray(np.asarray(v, np.float32).reshape(4, 128).T)


def _blockdiag(w):
    out = np.zeros((128, 4, 128), np.float32)
    for cc in range(4):
        out[0:64, cc, 0:64] = w[2 * cc]
        out[64:128, cc, 64:128] = w[2 * cc + 1]
    return out


def make_consts(NB):
    NJ = max(1, NB // 128)
    ident = np.eye(128, dtype=np.float32)
    tri = np.triu(np.ones((128, 128), np.float32))
    k = np.arange(128)[:, None]
    q = np.arange(NB)[None, :]
    mask = np.stack([(q >= j * 128 + k).astype(np.float32) for j in range(NJ)], axis=1)
    iota = np.broadcast_to(np.arange(16, dtype=np.float32), (128, 16)).copy()
    return dict(c_ident=ident, c_tri=tri, c_mask=np.ascontiguousarray(mask), c_iota=iota)


def shared_maps(inp, NB):
    f = lambda a: np.ascontiguousarray(np.asarray(a, np.float32))
    rep = lambda v: np.ascontiguousarray(np.broadcast_to(np.asarray(v, np.float32).reshape(1, -1), (128, np.asarray(v).size)))
    m = dict(
        w_in=f(inp["w_in"][0]), bF=rep(inp["b_forget"][0]),
        cw=np.ascontiguousarray(np.asarray(inp["conv_w"][0], np.float32).reshape(4, 4, 128).transpose(2, 1, 0)),
        cb=_fm(inp["conv_b"][0]), wa=_blockdiag(np.asarray(inp["w_rg_a"][0], np.float32)), ba=_fm(inp["b_rg_a"][0]),
        wx=_blockdiag(np.asarray(inp["w_rg_x"][0], np.float32)), bx=_fm(inp["b_rg_x"][0]), lam=_fm(inp["lru_lambda"][0]),
        w_au=f(inp["w_attn_up"][0]), w_ru=f(inp["w_rnn_up"][0]), w_out=f(inp["w_out"][0]),
        ln1g=rep(inp["ln1_g"][0]), ln1b=rep(inp["ln1_b"][0]), wq=f(inp["peer_w_query"][0]),
        k1T=f(np.asarray(inp["peer_keys_1"][0]).T), k2T=f(np.asarray(inp["peer_keys_2"][0]).T),
        pu=f(inp["peer_u"][0]), pv=f(inp["peer_v"][0]), ln2g=rep(inp["ln2_g"][0]), ln2b=rep(inp["ln2_b"][0]),
    )
    m.update(make_consts(NB))
    return m


def core_map(inp, c, NS, shared):
    f = lambda a: np.ascontiguousarray(np.asarray(a, np.float32))
    sl = slice(c * NS, (c + 1) * NS)
    ck = np.asarray(inp["cache_k"][0][sl], np.float32)
    ns, past = ck.shape[0], ck.shape[1]
    ckT = ck.reshape(ns, past, 4, 2, 64).transpose(0, 3, 4, 2, 1).reshape(ns, 128, 4, past)
    sconv = np.asarray(inp["state_conv"][0][sl], np.float32)
    sconvT = sconv.reshape(ns, 3, 4, 128).transpose(0, 3, 2, 1)
    srnn = np.asarray(inp["state_rnn"][0][sl], np.float32)
    srnnT = srnn.reshape(ns, 4, 128).transpose(0, 2, 1)
    xs = np.asarray(inp["x_sample"][sl], np.float32)
    m = dict(shared)
    m.update(
        xp=f(inp["x_prompt"][c]), xs=f(xs.reshape(-1, D_MODEL)), ckT=f(ckT),
        cv=f(np.asarray(inp["cache_v"][0][sl]).reshape(ns, past, 512)),
        clf=f(inp["cache_logf"][0][sl]), sconvT=f(sconvT), srnnT=f(srnnT),
    )
    return m


def kernel(**inp):
    NCORES = 8
    T = inp["x_prompt"].shape[1]
    NS = inp["x_sample"].shape[0] // NCORES
    TS = inp["x_sample"].shape[1]
    PAST = inp["cache_k"].shape[2]
    NB = 256
    nc = build(T=T, NS=NS, TS=TS, PAST=PAST, NB=NB)
    shared = shared_maps(inp, NB)
    in_maps = [core_map(inp, c, NS, shared) for c in range(NCORES)]
    res = run_bass_kernel_spmd(nc, in_maps, core_ids=list(range(NCORES)))
    R = res.results
    B = NCORES
    cat = lambda k: np.stack([np.asarray(R[c][k], np.float32) for c in range(B)], 0)
    y_p = cat("y_p")
    y_s = cat("y_s").reshape(B * NS, TS, D_MODEL)
    k_p = cat("k_p").reshape(1, B, T, 8, 64)
    v_p = cat("v_p").reshape(1, B, T, 8, 64)
    lf_p = cat("lf_p").reshape(1, B, T, 8)
    conv_p = cat("conv_p").reshape(1, B, 3, 512)
    rnn_p = cat("rnn_p").reshape(1, B, 512)
    k_s = cat("k_s").reshape(1, B * NS, TS, 8, 64)
    v_s = cat("v_s").reshape(1, B * NS, TS, 8, 64)
    lf_s = cat("lf_s").reshape(1, B * NS, TS, 8)
    conv_s = cat("conv_s").reshape(1, B * NS, 3, 512)
    rnn_s = cat("rnn_s").reshape(1, B * NS, 512)
    return (y_p, y_s, k_p, v_p, lf_p, conv_p, rnn_p, k_s, v_s, lf_s, conv_s, rnn_s)
```

```python
import contextlib
import os
import numpy as np
import concourse.bass as bass
import concourse.mybir as mybir
from concourse.bass_utils import run_bass_kernel_spmd

F32 = mybir.dt.float32
BF16 = mybir.dt.bfloat16
I32 = mybir.dt.int32
U32 = mybir.dt.uint32
AF = mybir.ActivationFunctionType
ALU = mybir.AluOpType
AX = mybir.AxisListType

ENGS = ("sync", "scalar", "vector", "gpsimd", "tensor")
NPOOL = 16

D_MODEL = 1024
D_IN = 4616
N_EXP = 16384
ALPHA = 2.0 ** 0.25
LN_EPS = 1e-5
GELU_C = 1.5957691216057308


class Op:
    __slots__ = ("eng", "fn", "dma", "deps", "needed", "sem", "val", "pre")

    def __init__(self, eng, fn, dma):
        self.eng = eng
        self.fn = fn
        self.dma = dma
        self.deps = ()
        self.needed = False
        self.sem = None
        self.val = 0
        self.pre = None


class Prog:
    def __init__(self, nc):
        self.nc = nc
        self.ops = {e: [] for e in ENGS}
        self.last_writer = {}
        self.readers = {}

    def op(self, eng, fn, reads=(), writes=(), dma=False):
        o = Op(eng, fn, dma)
        deps = set()
        for k in reads:
            w = self.last_writer.get(k)
            if w is not None:
                deps.add(w)
        for k in writes:
            w = self.last_writer.get(k)
            if w is not None:
                deps.add(w)
            for r in self.readers.get(k, ()):
                deps.add(r)
        if eng == "tensor" and not dma:
            deps = {d for d in deps if not (d.eng == "tensor" and not d.dma)}
        o.deps = deps
        for d in deps:
            d.needed = True
        if dma:
            o.needed = True
        for k in reads:
            self.readers.setdefault(k, []).append(o)
        for k in writes:
            self.last_writer[k] = o
            self.readers[k] = []
        self.ops[eng].append(o)
        return o

    def finish(self, semstack, tag):
        nc = self.nc
        esem = {e: semstack.enter_context(nc.semaphore("s%s_%s" % (tag, e))) for e in ENGS}
        pools = {e: [semstack.enter_context(nc.semaphore("d%s_%s_%d" % (tag, e, i))) for i in range(NPOOL)]
                 for e in ("sync", "scalar", "gpsimd") if any(o.dma for o in self.ops[e])}
        final = {}
        for e in ENGS:
            cnt = 0
            ndma = 0
            for o in self.ops[e]:
                if o.dma:
                    o.sem = pools[e][ndma % NPOOL]
                    o.val = 16 * (ndma // NPOOL + 1)
                    if ndma >= NPOOL:
                        o.pre = (o.sem, 16 * (ndma // NPOOL))
                    ndma += 1
                    final[id(o.sem)] = (o.sem, o.val)
                elif o.needed:
                    cnt += 1
                    o.sem = esem[e]
                    o.val = cnt
                    final[id(o.sem)] = (o.sem, o.val)
        for e in ENGS:
            for o in reversed(self.ops[e]):
                if not o.dma:
                    if not o.needed:
                        o.needed = True
                        o.sem = esem[e]
                        o.val = final.get(id(esem[e]), (None, 0))[1] + 1
                        final[id(o.sem)] = (o.sem, o.val)
                    break

        def make(e):
            def body(eng):
                waited = {}
                for o in self.ops[e]:
                    if o.pre is not None:
                        s, v = o.pre
                        if waited.get(id(s), 0) < v:
                            eng.wait_ge(s, v)
                            waited[id(s)] = v
                    for d in o.deps:
                        s, v = d.sem, d.val
                        if waited.get(id(s), 0) < v:
                            eng.wait_ge(s, v)
                            waited[id(s)] = v
                    ins = o.fn(eng)
                    if o.dma:
                        ins.then_inc(o.sem, 16)
                    elif o.needed:
                        ins.then_inc(o.sem, 1)
                for s, v in final.values():
                    if waited.get(id(s), 0) < v:
                        eng.wait_ge(s, v)
            return body

        with nc.Block() as block:
            block.sync(make("sync"))
            block.scalar(make("scalar"))
            block.vector(make("vector"))
            block.gpsimd(make("gpsimd"))
            block.tensor(make("tensor"))


class Em:
    def __init__(self, p):
        self.p = p

    def dma(self, eng, out, in_, r=(), w=(), **kw):
        return self.p.op(eng, lambda e: e.dma_start(out=out, in_=in_, **kw), r, w, dma=True)

    def gather(self, out, table, idx, r=(), w=()):
        return self.p.op("gpsimd", lambda e: e.indirect_dma_start(
            out=out, out_offset=None, in_=table,
            in_offset=bass.IndirectOffsetOnAxis(ap=idx, axis=0)), r, w, dma=True)

    def mm(self, out, lhsT, rhs, start, stop, r=(), w=()):
        return self.p.op("tensor", lambda e: e.matmul(out, lhsT=lhsT, rhs=rhs, start=start, stop=stop), r, w)

    def tr(self, out, in_, ident, r=(), w=()):
        return self.p.op("tensor", lambda e: e.transpose(out=out, in_=in_, identity=ident), r, w)

    def act(self, out, in_, func, r=(), w=(), bias=None, scale=None, accum=None, eng="scalar"):
        kw = {}
        if bias is not None:
            kw["bias"] = bias
        if scale is not None:
            kw["scale"] = scale
        if accum is not None:
            kw["accum_out"] = accum
        return self.p.op(eng, lambda e: e.activation(out=out, in_=in_, func=func, **kw), r, w)

    def copy(self, eng, out, in_, r=(), w=()):
        if eng == "scalar":
            return self.p.op(eng, lambda e: e.copy(out=out, in_=in_), r, w)
        return self.p.op(eng, lambda e: e.tensor_copy(out=out, in_=in_), r, w)

    def tt(self, eng, out, in0, in1, op, r=(), w=()):
        return self.p.op(eng, lambda e: e.tensor_tensor(out=out, in0=in0, in1=in1, op=op), r, w)

    def ts(self, eng, out, in0, s1, s2, op0, op1=None, r=(), w=()):
        if op1 is None:
            return self.p.op(eng, lambda e: e.tensor_scalar(out=out, in0=in0, scalar1=s1, scalar2=None, op0=op0), r, w)
        return self.p.op(eng, lambda e: e.tensor_scalar(out=out, in0=in0, scalar1=s1, scalar2=s2, op0=op0, op1=op1), r, w)

    def tss(self, eng, out, in_, scalar, op, r=(), w=()):
        return self.p.op(eng, lambda e: e.tensor_single_scalar(out=out, in_=in_, scalar=scalar, op=op), r, w)

    def stt(self, eng, out, in0, scalar, in1, op0, op1, r=(), w=(), accum=None):
        if accum is None:
            return self.p.op(eng, lambda e: e.scalar_tensor_tensor(out=out, in0=in0, scalar=scalar, in1=in1, op0=op0, op1=op1), r, w)
        return self.p.op(eng, lambda e: e.scalar_tensor_tensor(out=out, in0=in0, scalar=scalar, in1=in1, op0=op0, op1=op1, accum_out=accum), r, w)

    def memset(self, eng, ap, val, r=(), w=()):
        return self.p.op(eng, lambda e: e.memset(ap, val), r, w)

    def recip(self, out, in_, r=(), w=()):
        return self.p.op("vector", lambda e: e.reciprocal(out=out, in_=in_), r, w)

    def scan(self, out, d0, d1, init, r=(), w=()):
        return self.p.op("vector", lambda e: e.tensor_tensor_scan(out=out, data0=d0, data1=d1, initial=init,
                                                                  op0=ALU.mult, op1=ALU.add), r, w)

    def vmax(self, out, in_, r=(), w=()):
        return self.p.op("vector", lambda e: e.max(out=out, in_=in_), r, w)

    def vmaxidx(self, out, mx, vals, r=(), w=()):
        return self.p.op("vector", lambda e: e.max_index(out=out, in_max=mx, in_values=vals), r, w)

    def vmatchrep(self, out, mx, vals, imm, r=(), w=()):
        return self.p.op("vector", lambda e: e.match_replace(out=out, in_to_replace=mx, in_values=vals, imm_value=imm), r, w)

    def reduce(self, eng, out, in_, op, r=(), w=()):
        return self.p.op(eng, lambda e: e.tensor_reduce(out=out, in_=in_, axis=AX.X, op=op), r, w)


def build(T=4096, NS=2, TS=32, PAST=1024, NB=256, do_peer=True, debug_h=False, stop=99):
    nc = bass.Bass("TRN2", target_bir_lowering=False)

    def din(name, shape, dt=F32):
        return nc.dram_tensor(name, list(shape), dt, kind="ExternalInput").ap()

    def dout(name, shape, dt=F32):
        return nc.dram_tensor(name, list(shape), dt, kind="ExternalOutput").ap()

    def dscr(name, shape, dt):
        return nc.dram_tensor(name, list(shape), dt, kind="Internal").ap()

    NSR = NS * TS
    NROW = T + NSR
    KMAX = max(T, PAST + TS)
    NTK = (KMAX + 127) // 128
    NJ = max(1, NB // 128)

    xp = din("xp", [T, D_MODEL])
    xs = din("xs", [NSR, D_MODEL])
    ckT = din("ckT", [NS, 128, 4, PAST])
    cv = din("cv", [NS, PAST, 512])
    clf = din("clf", [NS, PAST, 8])
    sconvT = din("sconvT", [NS, 128, 4, 3])
    srnnT = din("srnnT", [NS, 128, 4])
    w_in = din("w_in", [D_MODEL, D_IN])
    bF = din("bF", [128, 8])
    cw = din("cw", [128, 4, 4])
    cb = din("cb", [128, 4])
    wa = din("wa", [128, 4, 128])
    ba = din("ba", [128, 4])
    wx = din("wx", [128, 4, 128])
    bx = din("bx", [128, 4])
    lam = din("lam", [128, 4])
    w_au = din("w_au", [512, D_MODEL])
    w_ru = din("w_ru", [512, D_MODEL])
    w_out = din("w_out", [D_MODEL, D_MODEL])
    ln1g = din("ln1g", [128, D_MODEL])
    ln1b = din("ln1b", [128, D_MODEL])
    wq = din("wq", [D_MODEL, 2048])
    k1T = din("k1T", [128, 128])
    k2T = din("k2T", [128, 128])
    pu = din("pu", [N_EXP, D_MODEL])
    pv = din("pv", [N_EXP, D_MODEL])
    ln2g = din("ln2g", [128, D_MODEL])
    ln2b = din("ln2b", [128, D_MODEL])
    c_ident = din("c_ident", [128, 128])
    c_tri = din("c_tri", [128, 128])
    c_mask = din("c_mask", [128, NJ, NB])
    c_iota = din("c_iota", [128, 16])

    y_p = dout("y_p", [T, D_MODEL])
    y_s = dout("y_s", [NSR, D_MODEL])
    k_p = dout("k_p", [T, 512])
    v_p = dout("v_p", [T, 512])
    lf_p = dout("lf_p", [T, 8])
    conv_p = dout("conv_p", [3, 512])
    rnn_p = dout("rnn_p", [1, 512])
    k_s = dout("k_s", [NSR, 512])
    v_s = dout("v_s", [NSR, 512])
    lf_s = dout("lf_s", [NSR, 8])
    conv_s = dout("conv_s", [NS, 3, 512])
    rnn_s = dout("rnn_s", [NS, 512])

    w_in_b = dscr("w_in_b", [D_MODEL, 4672], BF16)
    w_au_b = dscr("w_au_b", [512, D_MODEL], BF16)
    w_ru_b = dscr("w_ru_b", [512, D_MODEL], BF16)
    w_out_b = dscr("w_out_b", [D_MODEL, D_MODEL], BF16)
    wq_b = dscr("wq_b", [D_MODEL, 2048], BF16)
    h_scr = dout("h_scr", [NROW, D_MODEL]) if debug_h else dscr("h_scr", [NROW, D_MODEL], F32)
    uv_b = dscr("uv_b", [N_EXP, 2048], BF16)

    semstack = contextlib.ExitStack()
    with semstack:
        with contextlib.ExitStack() as st:
            def sb(name, shape, dt=F32):
                return st.enter_context(nc.sbuf_tensor(name, list(shape), dt))

            def ps(name, shape, dt=F32):
                return st.enter_context(nc.psum_tensor(name, list(shape), dt))

            p = Prog(nc)
            em = Em(p)

            ID = sb("ID", [128, 128])
            TRI = sb("TRI", [128, 128])
            ONESF = sb("ONESF", [128, 128])
            MASKF = sb("MASKF", [128, NJ, NB])
            MASK = sb("MASK", [128, NJ, NB], BF16)
            BFt = sb("BFt", [128, 8])
            CW = sb("CW", [128, 4, 4])
            CB = sb("CB", [128, 4])
            WA = sb("WA", [128, 4, 128])
            WX = sb("WX", [128, 4, 128])
            BA = sb("BA", [128, 4])
            BX = sb("BX", [128, 4])
            LAM = sb("LAM", [128, 4])
            C8 = sb("C8", [128, 4])
            C16 = sb("C16", [128, 4])
            LNG = sb("LNG", [128, D_MODEL])
            LNB = sb("LNB", [128, D_MODEL])
            WF = sb("WF", [128, 8, 8], BF16)

            KT = sb("KT", [128, 4, KMAX], BF16)
            VA = sb("VA", [128, NTK, 4, 192], BF16)
            LF = sb("LF", [128, NTK, 8])
            FF = sb("FF", [128, NTK, 8])
            BIAS = sb("BIAS", [128, NTK, 8])
            RT = sb("RT", [128, 8])
            FREF = sb("FREF", [128, 8])
            HIST = sb("HIST", [128, 4, 3])
            HST = sb("HST", [128, 4])

            NWS = 5
            WS = [sb("WS%d" % i, [128, 8, 512], BF16) for i in range(NWS)]
            XS = [sb("XS%d" % i, [128, D_MODEL]) for i in range(2)]
            XT = sb("XT", [128, 8, NB], BF16)
            QTP = sb("QTP", [128, 8, NB], BF16)
            KTOK = [sb("KTOK%d" % i, [128, 512]) for i in range(2)]
            VTOK = [sb("VTOK%d" % i, [128, 512]) for i in range(2)]
            LT = sb("LT", [128, 8])
            XRs = [sb("XR%d" % i, [128, NB + 3]) for i in range(2)]
            GGs = [sb("GG%d" % i, [128, NB]) for i in range(2)]
            XCs = [sb("XC%d" % i, [128, NB]) for i in range(2)]
            RR = sb("RR", [128, NB])
            II = sb("II", [128, NB])
            AA = sb("AA", [128, NB])
            BB = sb("BB", [128, NB])
            HS = sb("HS", [128, NB])
            T1 = sb("T1", [128, NB])
            T2 = sb("T2", [128, NB])
            ROT = sb("ROT", [128, 4, NB], BF16)
            NPB = 3
            PB = [sb("PB%d" % i, [128, NB], BF16) for i in range(NPB)]
            RECS = [sb("RECS%d" % i, [128, NB]) for i in range(1)]
            OT = sb("OT", [128, 4, NB], BF16)
            SA = sb("SA", [128, NB])
            SR = sb("SR", [128, NB])
            M1 = sb("M1", [128, NB])
            M2 = sb("M2", [128, NB])
            MT = sb("MT", [128, 8, NB], BF16)
            PRE = [sb("PRE%d" % i, [128, D_MODEL]) for i in range(2)]
            HO = [sb("HO%d" % i, [128, D_MODEL]) for i in range(2)]
            STAT = [sb("STAT%d" % i, [128, 8]) for i in range(2)]

            PZ = [ps("PZ%d" % i, [128, 512]) for i in range(2)]
            PSS = [ps("PSS%d" % i, [128, 512]) for i in range(3)]
            PO = [ps("PO%d" % i, [128, 512]) for i in range(2)]
            PX = ps("PX", [128, 512])
            if os.environ.get("KDEBUG"):
                print("phase1 sbuf remaining", nc.sbuf_bytes_remaining)

            for r0 in range(0, D_MODEL, 128):
                em.dma("gpsimd", w_in_b[r0:r0 + 128, 0:1536], w_in[r0:r0 + 128, 0:1536], w=[("w_in_b", r0 // 128, 0)])
                em.dma("gpsimd", w_in_b[r0:r0 + 128, 1536:4608], w_in[r0:r0 + 128, 1544:4616], w=[("w_in_b", r0 // 128, 1)])
                em.dma("gpsimd", w_in_b[r0:r0 + 128, 4608:4616], w_in[r0:r0 + 128, 1536:1544], w=[("w_in_b", r0 // 128, 2)])
            for r0 in range(0, 512, 128):
                em.dma("gpsimd", w_au_b[r0:r0 + 128, :], w_au[r0:r0 + 128, :], w=[("w_au_b", r0 // 128)])
                em.dma("gpsimd", w_ru_b[r0:r0 + 128, :], w_ru[r0:r0 + 128, :], w=[("w_ru_b", r0 // 128)])
            for r0 in range(0, D_MODEL, 128):
                em.dma("gpsimd", w_out_b[r0:r0 + 128, :], w_out[r0:r0 + 128, :], w=[("w_out_b", r0 // 128)])
            for r0 in range(0, D_MODEL, 128):
                em.dma("gpsimd", wq_b[r0:r0 + 128, :], wq[r0:r0 + 128, :], w=[("wq_b", r0 // 128)])
            W_IN_KEYS = [("w_in_b", i, j) for i in range(8) for j in range(3)]
            W_AU_KEYS = [("w_au_b", i) for i in range(4)]
            W_RU_KEYS = [("w_ru_b", i) for i in range(4)]
            W_OUT_KEYS = [("w_out_b", i) for i in range(8)]

            for (t_, d_, k_) in ((ID, c_ident, "ID"), (TRI, c_tri, "TRI"), (MASKF, c_mask, "MASKF"), (BFt, bF, "BFt"),
                                 (CW, cw, "CW"), (CB, cb, "CB"), (WA, wa, "WA"), (WX, wx, "WX"), (BA, ba, "BA"),
                                 (BX, bx, "BX"), (LAM, lam, "LAM"), (LNG, ln1g, "LNG"), (LNB, ln1b, "LNB")):
                em.dma("sync", t_[:], d_, w=[k_])
            em.memset("vector", ONESF[:], 1.0, w=["ONESF"])
            em.ts("vector", BA[:], BA[:], -1.0, None, ALU.mult, r=["BA"], w=["BA"])
            em.ts("vector", BX[:], BX[:], -1.0, None, ALU.mult, r=["BX"], w=["BX"])
            em.memset("vector", FF[:], 0.0, w=[("FF", kt) for kt in range(NTK)])
            em.memset("vector", LF[:], 0.0, w=[("LF", kt) for kt in range(NTK)])
            em.copy("vector", MASK[:], MASKF[:], r=["MASKF"], w=["MASK"])
            em.memset("vector", VA[:, :, :, 64:128], 1.0, w=["VA1"])
            em.memset("vector", QTP[:], 0.0, w=[("QTP", h_) for h_ in range(8)])
            em.act(C8[:], LAM[:], AF.Exp, scale=-1.0, r=["LAM"], w=["C8"])
            em.act(C8[:], C8[:], AF.Ln, bias=1.0, r=["C8"], w=["C8"])
            em.ts("vector", C16[:], C8[:], -16.0, None, ALU.mult, r=["C8"], w=["C16"])
            em.ts("vector", C8[:], C8[:], -8.0, None, ALU.mult, r=["C8", "C16"], w=["C8"])
            em.dma("sync", WF[:], w_in_b.rearrange("(kc p) c -> p kc c", p=128)[:, :, 4608:4616],
                   r=W_IN_KEYS, w=["WF"])

            ws_rr = [0]

            def load_w(src_ap, shape_part, key_reads):
                i = ws_rr[0] % NWS
                ws_rr[0] += 1
                dst = WS[i]
                npart, nk, ncol = shape_part
                em.dma("sync", dst[0:npart, 0:nk, 0:ncol], src_ap, r=key_reads, w=[("WS", i)])
                return dst, ("WS", i)

            w_in_v = w_in_b.rearrange("(kc p) c -> p kc c", p=128)
            w_out_v = w_out_b.rearrange("(kc p) c -> p kc c", p=128)
            w_ru_v = w_ru_b.rearrange("(kc p) c -> p kc c", p=128)
            w_au_v = w_au_b.rearrange("(kc p) c -> p kc c", p=128)

            rr = {"xs": 0, "pz": 0, "kt": 0, "pss": 0, "po": 0, "pb": 0, "bc": 0, "pre": 0, "rg": 0}

            def nxt(name, n):
                v = rr[name] % n
                rr[name] += 1
                return v

            def cumsum_tile(kt, rows):
                if os.environ.get("SKIP_CUMSUM"):
                    return
                em.mm(PX[0:rows, 8:16], TRI[0:rows, 0:rows], LF[0:rows, kt, :], True, True,
                      r=["TRI", ("LF", kt)], w=["PX"])
                em.mm(PX[:, 16:24], ONESF[0:rows, :], LF[0:rows, kt, :], True, True,
                      r=["ONESF", ("LF", kt)], w=["PX"])
                em.tt("vector", FF[0:rows, kt, :], PX[0:rows, 8:16], RT[0:rows, :], ALU.add,
                      r=["PX", "RT"], w=[("FF", kt)])
                em.tt("vector", RT[:], PX[:, 16:24], RT[:], ALU.add, r=["PX", "RT"], w=["RT"])

            def process_block(xsrc, t0, NBs, kbase, outs, is_last):
                k_out, v_out, lf_out, h_row0 = outs
                TP = min(128, NBs)
                ntile = NBs // TP
                kpos0 = kbase + t0
                kt_first = kpos0 // 128
                n_kt = (kpos0 + NBs + 127) // 128
                new_kts = list(range(kt_first, n_kt))

                for i in range(ntile):
                    s = nxt("xs", 2)
                    em.dma("sync", XS[s][0:TP, :], xsrc[t0 + i * TP: t0 + (i + 1) * TP, :], w=[("XS", s)])
                    for half in range(2):
                        z = nxt("pz", 2)
                        for j in range(4):
                            kc = half * 4 + j
                            em.tr(PZ[z][:, j * TP:(j + 1) * TP], XS[s][0:TP, kc * 128:(kc + 1) * 128], ID[0:TP, 0:TP],
                                  r=[("XS", s), "ID"], w=[("PZ", z)])
                        em.copy("scalar", XT[:, half * 4:(half + 1) * 4, i * TP:(i + 1) * TP],
                                PZ[z][:, 0:4 * TP].rearrange("p (j t) -> p j t", j=4), r=[("PZ", z)], w=["XT"])
                yield "A"
                if stop <= 1:
                    return

                wq_t, wq_k = load_w(w_in_v[:, :, 0:512], (128, 8, 512), W_IN_KEYS)
                for pc in range(4):
                    z = nxt("pz", 2)
                    for kc in range(8):
                        em.mm(PZ[z][:, 0:NBs], wq_t[:, kc, pc * 128:(pc + 1) * 128], XT[:, kc, 0:NBs], kc == 0, kc == 7,
                              r=[wq_k, "XT"], w=[("PZ", z)])
                    em.copy("scalar", QTP[0:64, 2 * pc, 0:NBs], PZ[z][0:64, 0:NBs], r=[("PZ", z)], w=[("QTP", 2 * pc)])
                    em.copy("scalar", QTP[64:128, 2 * pc + 1, 0:NBs], PZ[z][64:128, 0:NBs], r=[("PZ", z)], w=[("QTP", 2 * pc + 1)])
                wk_t, wk_k = load_w(w_in_v[:, :, 512:1024], (128, 8, 512), W_IN_KEYS)
                for pc in range(4):
                    z = nxt("pz", 2)
                    for kc in range(8):
                        em.mm(PZ[z][:, 0:NBs], wk_t[:, kc, pc * 128:(pc + 1) * 128], XT[:, kc, 0:NBs], kc == 0, kc == 7,
                              r=[wk_k, "XT"], w=[("PZ", z)])
                    em.copy("scalar", KT[:, pc, kpos0:kpos0 + NBs], PZ[z][:, 0:NBs], r=[("PZ", z)],
                            w=[("KT", kt) for kt in new_kts])
                wv_t, wv_k = load_w(w_in_v[:, :, 1024:1536], (128, 8, 512), W_IN_KEYS)
                tile_kts = []
                for i in range(ntile):
                    kt = (kpos0 + i * TP) // 128
                    r0 = (kpos0 + i * TP) % 128
                    assert r0 == 0
                    tile_kts.append(kt)
                    tok = slice(i * TP, (i + 1) * TP)
                    s = nxt("kt", 2)
                    z = nxt("pz", 2)
                    for kc in range(8):
                        em.mm(PZ[z][0:TP, :], XT[:, kc, tok], wk_t[:, kc, :], kc == 0, kc == 7,
                              r=[wk_k, "XT"], w=[("PZ", z)])
                    em.copy("vector", KTOK[s][0:TP, :], PZ[z][0:TP, :], r=[("PZ", z)], w=[("KTOK", s)])
                    em.dma("gpsimd", k_out[t0 + i * TP: t0 + (i + 1) * TP, :], KTOK[s][0:TP, :], r=[("KTOK", s)])
                    z = nxt("pz", 2)
                    for kc in range(8):
                        em.mm(PZ[z][0:TP, :], XT[:, kc, tok], wv_t[:, kc, :], kc == 0, kc == 7,
                              r=[wv_k, "XT"], w=[("PZ", z)])
                    em.copy("scalar", VTOK[s][0:TP, :], PZ[z][0:TP, :], r=[("PZ", z)], w=[("VTOK", s)])
                    vt4 = VTOK[s][0:TP, :].rearrange("p (c t d) -> p c t d", c=4, t=2)
                    em.copy("gpsimd", VA[0:TP, kt, :, 0:64], vt4[:, :, 0, :], r=[("VTOK", s)], w=[("VA", kt)])
                    em.copy("gpsimd", VA[0:TP, kt, :, 128:192], vt4[:, :, 1, :], r=[("VTOK", s)], w=[("VA", kt)])
                    em.dma("gpsimd", v_out[t0 + i * TP: t0 + (i + 1) * TP, :], VTOK[s][0:TP, :], r=[("VTOK", s)])
                    for kc in range(8):
                        em.mm(PX[0:TP, 0:8], XT[:, kc, tok], WF[:, kc, :], kc == 0, kc == 7, r=["WF", "XT"], w=["PX"])
                    em.tt("vector", LT[0:TP, :], PX[0:TP, 0:8], BFt[0:TP, :], ALU.add, r=["PX", "BFt"], w=["LT"])
                    em.act(LT[0:TP, :], LT[0:TP, :], AF.Exp, scale=-1.0, r=["LT"], w=["LT"])
                    em.act(LT[0:TP, :], LT[0:TP, :], AF.Ln, bias=1.0, r=["LT"], w=["LT"])
                    em.ts("vector", LF[0:TP, kt, :], LT[0:TP, :], -1.0, None, ALU.mult, r=["LT"], w=[("LF", kt)])
                    em.dma("gpsimd", lf_out[t0 + i * TP: t0 + (i + 1) * TP, :], LF[0:TP, kt, :], r=[("LF", kt)])
                if stop <= 2:
                    return

                rgbuf = {}
                rgw = {}

                def rg_inproj(cc):
                    b_ = nxt("rg", 2)
                    rgbuf[cc] = b_
                    XR, GG, XC = XRs[b_], GGs[b_], XCs[b_]
                    xck = ("XC", b_)
                    if "xr" not in rgw:
                        rgw["xr"] = load_w(w_in_v[:, :, 1536:2048], (128, 8, 512), W_IN_KEYS)
                        rgw["gate"] = load_w(w_in_v[:, :, 2048:2560], (128, 8, 512), W_IN_KEYS)
                    wr_t, wr_k = rgw["xr"]
                    wg_t, wg_k = rgw["gate"]
                    csl = slice(cc * 128, (cc + 1) * 128)
                    za = nxt("pz", 2)
                    for kc in range(8):
                        em.mm(PZ[za][:, 0:NBs], wr_t[:, kc, csl], XT[:, kc, 0:NBs], kc == 0, kc == 7,
                              r=[wr_k, "XT"], w=[("PZ", za)])
                    em.copy("vector", XR[:, 0:3], HIST[:, cc, :], r=[("HIST", cc)], w=[("XRh", b_)])
                    em.copy("scalar", XR[:, 3:3 + NBs], PZ[za][:, 0:NBs], r=[("PZ", za)], w=[("XRb", b_)])
                    zb = nxt("pz", 2)
                    for kc in range(8):
                        em.mm(PZ[zb][:, 0:NBs], wg_t[:, kc, csl], XT[:, kc, 0:NBs], kc == 0, kc == 7,
                              r=[wg_k, "XT"], w=[("PZ", zb)])
                    em.copy("scalar", GG[:, 0:NBs], PZ[zb][:, 0:NBs], r=[("PZ", zb)], w=[("GG", b_)])
                    xk = [("XRh", b_), ("XRb", b_)]
                    em.copy("vector", HIST[:, cc, :], XR[:, NBs:NBs + 3], r=xk, w=[("HIST", cc)])
                    em.ts("vector", XC[:, 0:NBs], XR[:, 0:NBs], CW[:, cc, 0:1], CB[:, cc:cc + 1], ALU.mult, ALU.add,
                          r=xk + ["CW", "CB"], w=[xck])
                    for wtap in range(1, 4):
                        em.stt("vector", XC[:, 0:NBs], XR[:, wtap:wtap + NBs], CW[:, cc, wtap:wtap + 1], XC[:, 0:NBs],
                               ALU.mult, ALU.add, r=xk + ["CW", xck], w=[xck])

                def rg_chain(cc):
                    b_ = rgbuf[cc]
                    GG, XC = GGs[b_], XCs[b_]
                    gk, xck = ("GG", b_), ("XC", b_)
                    N = slice(0, NBs)
                    z1 = nxt("pz", 2)
                    em.mm(PZ[z1][:, N], WA[:, cc, :], XC[:, N], True, True, r=["WA", xck], w=[("PZ", z1)])
                    z2 = nxt("pz", 2)
                    em.mm(PZ[z2][:, N], WX[:, cc, :], XC[:, N], True, True, r=["WX", xck], w=[("PZ", z2)])
                    em.tt("gpsimd", T2[:, N], GG[:, N], GG[:, N], ALU.mult, r=[gk], w=["T2"])
                    em.ts("gpsimd", T2[:, N], T2[:, N], 0.044715, 1.0, ALU.mult, ALU.add, r=["T2"], w=["T2"])
                    em.tt("gpsimd", T2[:, N], T2[:, N], GG[:, N], ALU.mult, r=["T2", gk], w=["T2"])
                    yield
                    em.act(RR[:, N], PZ[z1][:, N], AF.Exp, bias=BA[:, cc:cc + 1], scale=-1.0, r=[("PZ", z1), "BA"], w=["RR"])
                    em.act(II[:, N], PZ[z2][:, N], AF.Exp, bias=BX[:, cc:cc + 1], scale=-1.0, r=[("PZ", z2), "BX"], w=["II"])
                    em.act(T2[:, N], T2[:, N], AF.Exp, scale=-GELU_C, r=["T2"], w=["T2"])
                    yield
                    em.ts("vector", RR[:, N], RR[:, N], 1.0, None, ALU.add, r=["RR"], w=["RR"])
                    em.recip(RR[:, N], RR[:, N], r=["RR"], w=["RR"])
                    yield
                    em.act(AA[:, N], RR[:, N], AF.Exp, scale=C8[:, cc:cc + 1], r=["RR", "C8"], w=["AA"])
                    em.act(T1[:, N], RR[:, N], AF.Exp, scale=C16[:, cc:cc + 1], r=["RR", "C16"], w=["T1"])
                    em.ts("vector", II[:, N], II[:, N], 1.0, None, ALU.add, r=["II"], w=["II"])
                    em.recip(II[:, N], II[:, N], r=["II"], w=["II"])
                    yield
                    em.ts("vector", T1[:, N], T1[:, N], -1.0, 1.0, ALU.mult, ALU.add, r=["T1"], w=["T1"])
                    em.tt("vector", BB[:, N], II[:, N], XC[:, N], ALU.mult, r=["II", xck], w=["BB"])
                    em.ts("gpsimd", T2[:, N], T2[:, N], 1.0, None, ALU.add, r=["T2"], w=["T2"])
                    yield
                    em.act(T1[:, N], T1[:, N], AF.Ln, r=["T1"], w=["T1"])
                    em.recip(T2[:, N], T2[:, N], r=["T2"], w=["T2"])
                    yield
                    em.act(T1[:, N], T1[:, N], AF.Exp, scale=0.5, r=["T1"], w=["T1"])
                    em.tt("gpsimd", T2[:, N], T2[:, N], GG[:, N], ALU.mult, r=["T2", gk], w=["T2"])
                    yield
                    em.tt("vector", BB[:, N], BB[:, N], T1[:, N], ALU.mult, r=["BB", "T1"], w=["BB"])
                    yield
                    em.scan(HS[:, N], AA[:, N], BB[:, N], HST[:, cc:cc + 1], r=["AA", "BB", ("HST", cc)], w=["HS"])
                    yield
                    em.copy("vector", HST[:, cc:cc + 1], HS[:, NBs - 1:NBs], r=["HS"], w=[("HST", cc)])
                    em.tt("vector", ROT[:, cc, N], HS[:, N], T2[:, N], ALU.mult, r=["HS", "T2"], w=["ROT"])

                rg_state = {"next_cc": 0, "gen": None}

                def rg_step(flush=False):
                    while True:
                        if rg_state["gen"] is None:
                            cc = rg_state["next_cc"]
                            if cc >= 4:
                                return
                            rg_state["gen"] = rg_chain(cc)
                        try:
                            next(rg_state["gen"])
                        except StopIteration:
                            rg_state["gen"] = None
                            rg_state["next_cc"] += 1
                            if rg_state["next_cc"] < 4:
                                rg_inproj(rg_state["next_cc"])
                        if not flush:
                            return

                rg_inproj(0)
                if len(tile_kts) == 1:
                    em.copy("vector", FREF[:], RT[:], r=["RT"], w=["FREF"])
                for ti_, kt in enumerate(tile_kts):
                    cumsum_tile(kt, TP)
                    if len(tile_kts) > 1 and ti_ == len(tile_kts) // 2 - 1:
                        em.copy("vector", FREF[:], RT[:], r=["RT"], w=["FREF"])
                if stop <= 3:
                    rg_step(flush=True)
                    return

                em.tt("vector", BIAS[:, 0:n_kt, :], FREF[:, :].unsqueeze(1).to_broadcast([128, n_kt, 8]),
                      FF[:, 0:n_kt, :], ALU.subtract, r=["FREF"] + [("FF", kt) for kt in range(n_kt)], w=["BIAS"])
                items = [(h, kt) for h in range(8) for kt in range(n_kt)]
                po_of = {}

                def kp_of(kt):
                    return min(128, kpos0 + NBs - kt * 128)

                def emit_qk(i):
                    h, kt = items[i]
                    kp = kp_of(kt)
                    s = i % 3
                    em.mm(PSS[s][0:kp, 0:NBs], KT[:, h // 2, kt * 128:kt * 128 + kp], QTP[:, h, 0:NBs],
                          True, True, r=[("KT", kt), ("QTP", h)], w=[("PSS", s)])

                def emit_exp(i):
                    h, kt = items[i]
                    kp = kp_of(kt)
                    s = i % 3
                    b = i % NPB
                    em.act(PB[b][0:kp, 0:NBs], PSS[s][0:kp, 0:NBs], AF.Exp, bias=BIAS[0:kp, kt, h:h + 1], scale=0.125,
                           r=[("PSS", s), "BIAS"], w=[("PB", b)])
                    if kt >= kt_first:
                        j = kt - kt_first
                        em.tt("gpsimd", PB[b][0:kp, 0:NBs], PB[b][0:kp, 0:NBs], MASK[0:kp, j, 0:NBs], ALU.mult,
                              r=[("PB", b), "MASK"], w=[("PB", b)])

                def emit_pv(i):
                    h, kt = items[i]
                    kp = kp_of(kt)
                    b = i % NPB
                    if kt == 0:
                        po_of[h] = nxt("po", 2)
                    o = po_of[h]
                    pc, odd = h // 2, h % 2
                    lhsT = VA[0:kp, kt, pc, 64:192] if odd else VA[0:kp, kt, pc, 0:128]
                    em.mm(PO[o][:, 0:NBs], lhsT, PB[b][0:kp, 0:NBs], kt == 0, kt == n_kt - 1,
                          r=[("VA", kt), "VA1", ("PB", b)], w=[("PO", o)])
                    if kt == n_kt - 1:
                        c = 0
                        osl = slice(64, 128) if odd else slice(0, 64)
                        dsl = slice(0, 64) if odd else slice(64, 128)
                        em.recip(RECS[c][osl, 0:NBs], PO[o][dsl, 0:NBs], r=[("PO", o)], w=[("RECS", c)])
                        em.tt("vector", OT[osl, pc, 0:NBs], PO[o][osl, 0:NBs], RECS[c][osl, 0:NBs], ALU.mult,
                              r=[("PO", o), ("RECS", c)], w=["OT"])

                emit_qk(0)
                if len(items) > 1:
                    emit_qk(1)
                for i in range(len(items)):
                    if i + 2 < len(items):
                        emit_qk(i + 2)
                    emit_exp(i)
                    emit_pv(i)
                    rg_step()
                rg_step(flush=True)
                if stop <= 4:
                    return

                for hf in range(2):
                    cs = slice(hf * 512, (hf + 1) * 512)
                    ga_t, ga_k = load_w(w_in_v[:, :, 2560 + hf * 512:2560 + (hf + 1) * 512], (128, 8, 512), W_IN_KEYS)
                    gr_t, gr_k = load_w(w_in_v[:, :, 3584 + hf * 512:3584 + (hf + 1) * 512], (128, 8, 512), W_IN_KEYS)
                    i_ = ws_rr[0] % NWS
                    ws_rr[0] += 1
                    em.dma("sync", WS[i_][:, 0:4, :], w_au_v[:, :, cs], r=W_AU_KEYS, w=[("WS", i_)])
                    em.dma("sync", WS[i_][:, 4:8, :], w_ru_v[:, :, cs], r=W_RU_KEYS, w=[("WS", i_)])
                    au_t, au_k = WS[i_], ("WS", i_)
                    ru_k = au_k
                    for fc in range(4):
                        fs = slice(fc * 128, (fc + 1) * 128)
                        for kc in range(8):
                            em.mm(PZ[0][:, 0:NBs], ga_t[:, kc, fs], XT[:, kc, 0:NBs], kc == 0, kc == 7,
                                  r=[ga_k, "XT"], w=[("PZ", 0)])
                        em.act(SA[:, 0:NBs], PZ[0][:, 0:NBs], AF.Sigmoid, r=[("PZ", 0)], w=["SA"])
                        for kc in range(8):
                            em.mm(PZ[1][:, 0:NBs], gr_t[:, kc, fs], XT[:, kc, 0:NBs], kc == 0, kc == 7,
                                  r=[gr_k, "XT"], w=[("PZ", 1)])
                        em.act(SR[:, 0:NBs], PZ[1][:, 0:NBs], AF.Sigmoid, r=[("PZ", 1)], w=["SR"])
                        for kc in range(4):
                            em.mm(PSS[0][:, 0:NBs], au_t[:, kc, fs], OT[:, kc, 0:NBs], kc == 0, kc == 3,
                                  r=[au_k, "OT"], w=[("PSS", 0)])
                        em.tt("vector", M1[:, 0:NBs], PSS[0][:, 0:NBs], SA[:, 0:NBs], ALU.mult, r=[("PSS", 0), "SA"], w=["M1"])
                        for kc in range(4):
                            em.mm(PSS[1][:, 0:NBs], au_t[:, 4 + kc, fs], ROT[:, kc, 0:NBs], kc == 0, kc == 3,
                                  r=[ru_k, "ROT"], w=[("PSS", 1)])
                        em.tt("vector", M2[:, 0:NBs], PSS[1][:, 0:NBs], SR[:, 0:NBs], ALU.mult, r=[("PSS", 1), "SR"], w=["M2"])
                        em.tt("gpsimd", MT[:, hf * 4 + fc, 0:NBs], M1[:, 0:NBs], M2[:, 0:NBs], ALU.add,
                              r=["M1", "M2"], w=["MT"])
                yield "E"
                if stop <= 5:
                    return

                wo0_t, wo0_k = load_w(w_out_v[:, :, 0:512], (128, 8, 512), W_OUT_KEYS)
                wo1_t, wo1_k = load_w(w_out_v[:, :, 512:1024], (128, 8, 512), W_OUT_KEYS)
                for i in range(ntile):
                    tok = slice(i * TP, (i + 1) * TP)
                    s = nxt("xs", 2)
                    em.dma("sync", XS[s][0:TP, :], xsrc[t0 + i * TP: t0 + (i + 1) * TP, :], w=[("XS", s)])
                    q = nxt("pre", 2)
                    for hf, (wt, wk) in enumerate(((wo0_t, wo0_k), (wo1_t, wo1_k))):
                        for kc in range(8):
                            em.mm(PO[hf][0:TP, :], MT[:, kc, tok], wt[:, kc, :], kc == 0, kc == 7,
                                  r=[wk, "MT"], w=[("PO", hf)])
                        em.stt("vector", PRE[q][0:TP, hf * 512:(hf + 1) * 512], XS[s][0:TP, hf * 512:(hf + 1) * 512],
                               ALPHA, PO[hf][0:TP, :], ALU.mult, ALU.add, r=[("XS", s), ("PO", hf)], w=[("PRE", q, hf)])
                    layer_norm(em, PRE[q], HO[q], None, STAT[q], LNG, LNB, TP,
                               [("PRE", q, 0), ("PRE", q, 1)], ("HO", q), ("STAT", q), "LNG", "LNB")
                    em.dma("gpsimd", h_scr[h_row0 + t0 + i * TP: h_row0 + t0 + (i + 1) * TP, :], HO[q][0:TP, :],
                           r=[("HO", q)], w=[("h_scr", (h_row0 + t0 + i * TP) // 32)])

            em.memset("vector", RT[:], 0.0, w=["RT"])
            em.memset("vector", HIST[:], 0.0, w=[("HIST", c) for c in range(4)])
            em.memset("vector", HST[:], 0.0, w=[("HST", c) for c in range(4)])
            UVCH = 512
            uv_jobs = []
            for r0 in range(0, N_EXP, UVCH):
                uv_jobs.append((uv_b[r0:r0 + UVCH, 0:1024], pu[r0:r0 + UVCH, :]))
                uv_jobs.append((uv_b[r0:r0 + UVCH, 1024:2048], pv[r0:r0 + UVCH, :]))
            nblk = T // NB if stop >= 1 else 0
            per_blk = (len(uv_jobs) + max(nblk, 1) - 1) // max(nblk, 1)
            def run_to(gen, tag):
                for t_ in gen:
                    if t_ == tag:
                        return
            gens_p = [process_block(xp, blk * NB, NB, 0, (k_p, v_p, lf_p, 0), blk == T // NB - 1) for blk in range(nblk)]
            if nblk:
                run_to(gens_p[0], "A")
            for blk in range(nblk):
                if do_peer:
                    for (o_, i_) in uv_jobs[blk * per_blk:(blk + 1) * per_blk]:
                        em.dma("gpsimd", o_, i_)
                run_to(gens_p[blk], "E")
                if blk + 1 < nblk:
                    run_to(gens_p[blk + 1], "A")
                run_to(gens_p[blk], None)
            if do_peer and nblk == 0:
                for (o_, i_) in uv_jobs:
                    em.dma("gpsimd", o_, i_)
            for c_ in range(4):
                em.dma("sync", conv_p.rearrange("w (cc p) -> p cc w", p=128)[:, c_, :], HIST[:, c_, :],
                       r=[("HIST", c_)], allow_slow_non_contiguous=True)
            em.dma("sync", rnn_p.rearrange("o (cc p) -> p (o cc)", p=128), HST[:],
                   r=[("HST", c) for c in range(4)], allow_slow_non_contiguous=True)

            NPT = PAST // 128
            for s_ in range(NS if stop >= 7 else 0):
                em.dma("gpsimd", KT[:, :, 0:PAST], ckT[s_], w=[("KT", kt) for kt in range(NPT)])
                for kt in range(NPT):
                    s = nxt("kt", 2)
                    em.dma("sync", VTOK[s][:, :], cv[s_, kt * 128:(kt + 1) * 128, :], w=[("VTOK", s)])
                    vt4 = VTOK[s][:, :].rearrange("p (c t d) -> p c t d", c=4, t=2)
                    em.copy("vector", VA[:, kt, :, 0:64], vt4[:, :, 0, :], r=[("VTOK", s)], w=[("VA", kt)])
                    em.copy("vector", VA[:, kt, :, 128:192], vt4[:, :, 1, :], r=[("VTOK", s)], w=[("VA", kt)])
                em.dma("sync", LF[:, 0:NPT, :], clf[s_].rearrange("(kt p) h -> p kt h", p=128),
                       w=[("LF", kt) for kt in range(NPT)])
                em.memset("vector", RT[:], 0.0, w=["RT"])
                for kt in range(NPT):
                    cumsum_tile(kt, 128)
                em.dma("sync", HIST[:], sconvT[s_], w=[("HIST", c) for c in range(4)])
                em.dma("sync", HST[:], srnnT[s_], w=[("HST", c) for c in range(4)])
                for _ in process_block(xs[s_ * TS:(s_ + 1) * TS, :], 0, TS, PAST,
                                       (k_s[s_ * TS:(s_ + 1) * TS, :], v_s[s_ * TS:(s_ + 1) * TS, :],
                                        lf_s[s_ * TS:(s_ + 1) * TS, :], T + s_ * TS), True):
                    pass
                for c_ in range(4):
                    em.dma("sync", conv_s[s_].rearrange("w (cc p) -> p cc w", p=128)[:, c_, :], HIST[:, c_, :],
                           r=[("HIST", c_)], allow_slow_non_contiguous=True)
                em.dma("sync", rnn_s[s_:s_ + 1, :].rearrange("o (cc p) -> p (o cc)", p=128), HST[:],
                       r=[("HST", c) for c in range(4)], allow_slow_non_contiguous=True)

            p.finish(semstack, "a")

        with contextlib.ExitStack() as st:
            def sb(name, shape, dt=F32):
                return st.enter_context(nc.sbuf_tensor(name, list(shape), dt))

            def ps(name, shape, dt=F32):
                return st.enter_context(nc.psum_tensor(name, list(shape), dt))

            p = Prog(nc)
            em = Em(p)
            ID = sb("ID2", [128, 128])
            IDB = sb("IDB", [128, 128], BF16)
            IOTA = sb("IOTA", [128, 16])
            LNG = sb("LNG2", [128, D_MODEL])
            LNB = sb("LNB2", [128, D_MODEL])
            WQ = sb("WQ", [128, 8, 2048], BF16)
            KEYF = sb("KEYF", [128, 2, 128])
            KEYB = sb("KEYB", [128, 2, 128], BF16)
            H = [sb("H%d" % i, [128, D_MODEL]) for i in range(2)]
            HB = [sb("HB%d" % i, [128, D_MODEL], BF16) for i in range(2)]
            HT = sb("HT", [128, 8, 128], BF16)
            QYT = sb("QYT", [128, 16, 128], BF16)
            SC = sb("SC", [128, 16, 128])
            SC2 = sb("SC2", [128, 16, 128])
            MX = sb("MX", [128, 16, 16])
            MI = sb("MI", [128, 16, 16], U32)
            MIF = sb("MIF", [128, 16, 16])
            CAND = sb("CAND", [128, 8, 256])
            CAND2 = sb("CAND2", [128, 8, 256])
            TOPV = sb("TOPV", [128, 8, 16])
            SEL = sb("SEL", [128, 8, 16], U32)
            SELA = sb("SELA", [128, 8, 16], U32)
            SELB = sb("SELB", [128, 8, 16], U32)
            AF_ = sb("AFl", [128, 8, 16])
            BF_ = sb("BFl", [128, 8, 16])
            EQ = sb("EQ", [128, 8, 16, 16])
            I1S = sb("I1S", [128, 8, 16])
            I2S = sb("I2S", [128, 8, 16])
            IDXF = sb("IDXF", [128, 128])
            IDX = [sb("IDX%d" % i, [128, 128], I32) for i in range(2)]
            GW = [sb("GW%d" % i, [128, 8, 16]) for i in range(2)]
            GS = sb("GS", [128, 8])
            NGRP = 32
            NGA = 4
            GSZ = 128 // NGRP
            ACT = [sb("ACT%d" % i, [128, GSZ]) for i in range(NGA)]
            ACT2 = [sb("ACTb%d" % i, [128, GSZ]) for i in range(NGA)]
            WGT = [sb("WGT%d" % i, [128, GSZ]) for i in range(NGA)]
            NUV = 16
            UV = [sb("UV%d" % i, [128, 2048], BF16) for i in range(NUV)]
            NDG = 4
            DG = [sb("DG%d" % i, [128, 128], BF16) for i in range(NDG)]
            JUNKB = sb("JUNKB", [128, D_MODEL], BF16)
            JUNKC = sb("JUNKC", [128, D_MODEL], BF16)
            NPRD = 4
            PRD = [sb("PRD%d" % i, [128, D_MODEL], BF16) for i in range(NPRD)]
            PRE = [sb("PREb%d" % i, [128, D_MODEL]) for i in range(2)]
            YO = [sb("YO%d" % i, [128, D_MODEL]) for i in range(2)]
            STAT = [sb("STATb%d" % i, [128, 8]) for i in range(2)]

            PG = [ps("PG%d" % i, [128, 512]) for i in range(4)]
            POUT = [ps("POUT%d" % i, [128, 512]) for i in range(4)]
            if os.environ.get("KDEBUG"):
                print("phase2 sbuf remaining", nc.sbuf_bytes_remaining)

            em.dma("sync", ID[:], c_ident, w=["ID"])
            em.copy("vector", IDB[:], ID[:], r=["ID"], w=["IDB"])
            em.dma("sync", IOTA[:], c_iota, w=["IOTA"])
            em.dma("sync", LNG[:], ln2g, w=["LNG"])
            em.dma("sync", LNB[:], ln2b, w=["LNB"])
            em.dma("sync", KEYF[:, 0, :], k1T, w=["KEYF0"])
            em.dma("sync", KEYF[:, 1, :], k2T, w=["KEYF1"])
            em.copy("vector", KEYB[:], KEYF[:], r=["KEYF0", "KEYF1"], w=["KEYB"])
            wq_v = wq_b.rearrange("(kc p) c -> p kc c", p=128)
            for kc in range(8):
                em.dma("sync", WQ[:, kc, :], wq_v[:, kc, :], w=[("WQ", kc)])
            WQK = [("WQ", kc) for kc in range(8)]

            rr2 = {"h": 0, "pq": 0, "pg": 0, "uv": 0, "dg": 0, "pre": 0, "idx": 0, "grp": 0, "prd": 0}

            def nxt2(name, n):
                v = rr2[name] % n
                rr2[name] += 1
                return v

            def prologue(row0, n, ctx):
                hs_ = nxt2("h", 2)
                Ht = H[hs_]
                Hb = HB[hs_]
                em.dma("sync", Ht[0:n, :], h_scr[row0:row0 + n, :], w=[("H", hs_)])
                em.copy("scalar", Hb[0:n, :], Ht[0:n, :], r=[("H", hs_)], w=[("HB", hs_)])
                for half in range(2):
                    z = nxt2("pg", 4)
                    for j in range(4):
                        kc = half * 4 + j
                        em.tr(PG[z][:, j * n:(j + 1) * n], Ht[0:n, kc * 128:(kc + 1) * 128], ID[0:n, 0:n],
                              r=[("H", hs_), "ID"], w=[("PG", z)])
                    em.copy("scalar", HT[:, half * 4:(half + 1) * 4, 0:n],
                            PG[z][:, 0:4 * n].rearrange("p (j t) -> p j t", j=4), r=[("PG", z)], w=["HT"])
                yield
                for c4 in range(4):
                    if c4 > 0:
                        yield
                    z = nxt2("pg", 4)
                    for cj in range(4):
                        c = c4 * 4 + cj
                        for kc in range(8):
                            em.mm(PG[z][:, cj * n:(cj + 1) * n], WQ[:, kc, c * 128:(c + 1) * 128], HT[:, kc, 0:n],
                                  kc == 0, kc == 7, r=WQK + ["HT"], w=[("PG", z)])
                    em.copy("scalar", QYT[:, c4 * 4:(c4 + 1) * 4, 0:n],
                            PG[z][:, 0:4 * n].rearrange("p (j t) -> p j t", j=4), r=[("PG", z)], w=[("QYT", c4)])
                yield
                for c4 in range(4):
                    z = nxt2("pg", 4)
                    for cj in range(4):
                        c = c4 * 4 + cj
                        em.mm(PG[z][0:n, cj * 128:(cj + 1) * 128], QYT[:, c, 0:n], KEYB[:, c % 2, :], True, True,
                              r=[("QYT", c4), "KEYB"], w=[("PG", z)])
                    em.copy("vector", SC[0:n, c4 * 4:(c4 + 1) * 4, :],
                            PG[z][0:n, :].rearrange("p (j k) -> p j k", j=4), r=[("PG", z)], w=[("SC", c4)])
                for c in range(16):
                    if c % 2 == 0:
                        yield
                    ck = ("SC", c // 4)
                    em.vmax(MX[0:n, c, 0:8], SC[0:n, c, :], r=[ck], w=[("MX", c)])
                    em.vmaxidx(MI[0:n, c, 0:8], MX[0:n, c, 0:8], SC[0:n, c, :], r=[ck, ("MX", c)], w=[("MI", c)])
                    em.vmatchrep(SC2[0:n, c, :], MX[0:n, c, 0:8], SC[0:n, c, :], -1e30, r=[ck, ("MX", c)], w=[("SC2", c)])
                    em.vmax(MX[0:n, c, 8:16], SC2[0:n, c, :], r=[("SC2", c)], w=[("MXb", c)])
                    em.vmaxidx(MI[0:n, c, 8:16], MX[0:n, c, 8:16], SC2[0:n, c, :], r=[("SC2", c), ("MXb", c)], w=[("MIb", c)])
                MXK = [("MX", c) for c in range(16)] + [("MXb", c) for c in range(16)]
                MIK = [("MI", c) for c in range(16)] + [("MIb", c) for c in range(16)]
                em.copy("vector", MIF[0:n], MI[0:n], r=MIK, w=["MIF"])
                yield
                MXv = MX[0:n].rearrange("p (h t) k -> p h t k", t=2)
                em.tt("vector", CAND[0:n].rearrange("p h (a b) -> p h a b", a=16),
                      MXv[:, :, 0, :].unsqueeze(3).to_broadcast([n, 8, 16, 16]),
                      MXv[:, :, 1, :].unsqueeze(2).to_broadcast([n, 8, 16, 16]), ALU.add, r=MXK, w=["CAND"])
                for h in range(8):
                    yield
                    em.vmax(TOPV[0:n, h, 0:8], CAND[0:n, h, :], r=["CAND"], w=[("TV", h)])
                    em.vmaxidx(SEL[0:n, h, 0:8], TOPV[0:n, h, 0:8], CAND[0:n, h, :], r=["CAND", ("TV", h)], w=[("SEL", h)])
                    em.vmatchrep(CAND2[0:n, h, :], TOPV[0:n, h, 0:8], CAND[0:n, h, :], -1e30, r=["CAND", ("TV", h)], w=[("C2", h)])
                    em.vmax(TOPV[0:n, h, 8:16], CAND2[0:n, h, :], r=[("C2", h)], w=[("TVb", h)])
                    em.vmaxidx(SEL[0:n, h, 8:16], TOPV[0:n, h, 8:16], CAND2[0:n, h, :], r=[("C2", h), ("TVb", h)], w=[("SELb", h)])
                TVK = [("TV", h) for h in range(8)] + [("TVb", h) for h in range(8)]
                SELK = [("SEL", h) for h in range(8)] + [("SELb", h) for h in range(8)]
                yield
                em.tss("vector", SELB[0:n], SEL[0:n], 15, ALU.bitwise_and, r=SELK, w=["SELB"])
                em.tss("vector", SELA[0:n], SEL[0:n], 4, ALU.logical_shift_right, r=SELK, w=["SELA"])
                em.copy("vector", AF_[0:n], SELA[0:n], r=["SELA"], w=["AFl"])
                em.copy("vector", BF_[0:n], SELB[0:n], r=["SELB"], w=["BFl"])
                yield
                MIFv = MIF[0:n].rearrange("p (h t) k -> p h t k", t=2)
                iota_b = IOTA[0:n, :].unsqueeze(1).unsqueeze(1).to_broadcast([n, 8, 16, 16])
                for (sel_f, tsel, dst, dk) in ((AF_, 0, I1S, "I1S"), (BF_, 1, I2S, "I2S")):
                    em.tt("vector", EQ[0:n], iota_b, sel_f[0:n].unsqueeze(3).to_broadcast([n, 8, 16, 16]), ALU.is_equal,
                          r=["IOTA", "AFl", "BFl"], w=["EQ"])
                    yield
                    em.tt("vector", EQ[0:n], EQ[0:n], MIFv[:, :, tsel, :].unsqueeze(2).to_broadcast([n, 8, 16, 16]), ALU.mult,
                          r=["EQ", "MIF"], w=["EQ"])
                    yield
                    em.reduce("vector", dst[0:n], EQ[0:n], ALU.add, r=["EQ"], w=[dk])
                    yield
                ix = nxt2("idx", 2)
                em.stt("vector", IDXF[0:n, :], I1S[0:n].rearrange("p h k -> p (h k)"), 128.0,
                       I2S[0:n].rearrange("p h k -> p (h k)"), ALU.mult, ALU.add, r=["I1S", "I2S"], w=["IDXF"])
                em.copy("vector", IDX[ix][0:n, :], IDXF[0:n, :], r=["IDXF"], w=[("IDX", ix)])
                yield
                GWt = GW[ix]
                gwk = ("GW", ix)
                em.tt("vector", GWt[0:n], TOPV[0:n], TOPV[0:n, :, 0:1].to_broadcast([n, 8, 16]), ALU.subtract, r=TVK, w=[gwk])
                em.act(GWt[0:n], GWt[0:n], AF.Exp, r=[gwk], w=[gwk])
                em.reduce("vector", GS[0:n], GWt[0:n], ALU.add, r=[gwk], w=["GS"])
                em.recip(GS[0:n], GS[0:n], r=["GS"], w=["GS"])
                em.tt("vector", GWt[0:n], GWt[0:n], GS[0:n].unsqueeze(2).to_broadcast([n, 8, 16]), ALU.mult, r=[gwk, "GS"], w=[gwk])
                ctx.update(hs_=hs_, ix=ix, gwk=gwk, GWt=GWt)

            grp_bufs = {}

            def emit_dots_dve(t, n, ctx, grp):
                hs_, ix = ctx["hs_"], ctx["ix"]
                Hb = HB[hs_]
                ga = grp % NGA
                A_ = ACT[ga]
                bufs = []
                pend = []
                for k in range(GSZ):
                    s = grp * GSZ + k
                    g = nxt2("uv", NUV)
                    bufs.append(g)
                    em.gather(UV[g][0:n, :], uv_b, IDX[ix][0:n, s:s + 1], r=[("IDX", ix)], w=[("UV", g)])
                    if k % 4 == 0:
                        em.stt("vector", JUNKB[0:n, :], UV[g][0:n, 0:1024], 1.0, Hb[0:n, :], ALU.mult, ALU.mult,
                               r=[("UV", g), ("HB", hs_)], w=["JUNKB", ("ACT", ga, k)], accum=A_[0:n, k:k + 1])
                    else:
                        pj = nxt2("prd", NPRD)
                        em.tt("vector", PRD[pj][0:n, :], UV[g][0:n, 0:1024], Hb[0:n, :], ALU.mult,
                              r=[("UV", g), ("HB", hs_)], w=[("PRD", pj)])
                        pend.append((pj, k))
                grp_bufs[(t, grp)] = (bufs, pend)

            def emit_dots_act(t, n, ctx, grp):
                ga = grp % NGA
                A_ = ACT[ga]
                for (pj, k) in grp_bufs[(t, grp)][1]:
                    em.act(JUNKC[0:n, :], PRD[pj][0:n, :], AF.Identity, accum=A_[0:n, k:k + 1],
                           r=[("PRD", pj)], w=["JUNKC", ("ACT", ga, k)])

            def emit_tail_a(t, n, ctx, grp):
                gwk, GWt = ctx["gwk"], ctx["GWt"]
                GWf = GWt[0:n].rearrange("p h k -> p (h k)")
                ga = grp % NGA
                A_, A2_, W_ = ACT[ga], ACT2[ga], WGT[ga]
                AK = [("ACT", ga, k) for k in range(GSZ)]
                em.act(A2_[0:n], A_[0:n], AF.Gelu_apprx_tanh, r=AK, w=[("ACT2", ga)])
                em.tt("vector", W_[0:n], A2_[0:n], GWf[:, grp * GSZ:(grp + 1) * GSZ], ALU.mult,
                      r=[("ACT2", ga), gwk], w=[("WGT", ga)])

            def emit_tail_b(t, n, ctx, grp):
                ga = grp % NGA
                W_ = WGT[ga]
                bufs = grp_bufs.pop((t, grp))[0]
                po = (t % 2) * 2
                for k in range(GSZ):
                    s = grp * GSZ + k
                    g = bufs[k]
                    d = nxt2("dg", NDG)
                    em.act(DG[d][0:n, 0:n], IDB[0:n, 0:n], AF.Copy, scale=W_[0:n, k:k + 1],
                           r=["IDB", ("WGT", ga)], w=[("DG", d)])
                    for hf in range(2):
                        em.mm(POUT[po + hf][0:n, :], DG[d][0:n, 0:n], UV[g][0:n, 1024 + hf * 512:1024 + (hf + 1) * 512],
                              s == 0, s == 127, r=[("DG", d), ("UV", g)], w=[("POUT", po + hf)])

            def epilogue(t, n, ctx, y_dst):
                po = (t % 2) * 2
                hs_ = ctx["hs_"]
                Ht = H[hs_]
                q = nxt2("pre", 2)
                X, Y, ST = PRE[q], YO[q], STAT[q]
                xkeys = [("PRE", q, 0), ("PRE", q, 1)]
                ykey, stkey = ("YO", q), ("STAT", q)
                for hf in range(2):
                    em.stt("vector", X[0:n, hf * 512:(hf + 1) * 512], Ht[0:n, hf * 512:(hf + 1) * 512],
                           ALPHA, POUT[po + hf][0:n, :], ALU.mult, ALU.add, r=[("H", hs_), ("POUT", po + hf)], w=[("PRE", q, hf)])
                    yield
                em.act(Y[0:n, :], X[0:n, :], AF.Identity, accum=ST[0:n, 0:1], r=xkeys, w=[ykey, (stkey, 0)])
                yield
                em.act(Y[0:n, :], X[0:n, :], AF.Square, accum=ST[0:n, 1:2], r=xkeys, w=[ykey, (stkey, 1)])
                yield
                em.ts("vector", ST[0:n, 2:3], ST[0:n, 0:1], 1.0 / D_MODEL, None, ALU.mult, r=[(stkey, 0)], w=[(stkey, 2)])
                em.ts("vector", ST[0:n, 3:4], ST[0:n, 1:2], 1.0 / D_MODEL, None, ALU.mult, r=[(stkey, 1)], w=[(stkey, 3)])
                em.tt("vector", ST[0:n, 4:5], ST[0:n, 2:3], ST[0:n, 2:3], ALU.mult, r=[(stkey, 2)], w=[(stkey, 4)])
                em.tt("vector", ST[0:n, 5:6], ST[0:n, 3:4], ST[0:n, 4:5], ALU.subtract, r=[(stkey, 3), (stkey, 4)], w=[(stkey, 5)])
                em.ts("vector", ST[0:n, 6:7], ST[0:n, 5:6], LN_EPS, None, ALU.add, r=[(stkey, 5)], w=[(stkey, 6)])
                yield
                em.act(ST[0:n, 6:7], ST[0:n, 6:7], AF.Ln, r=[(stkey, 6)], w=[(stkey, 6)])
                em.act(ST[0:n, 7:8], ST[0:n, 6:7], AF.Exp, scale=-0.5, r=[(stkey, 6)], w=[(stkey, 7)])
                yield
                em.ts("vector", Y[0:n, :], X[0:n, :], ST[0:n, 2:3], ST[0:n, 7:8], ALU.subtract, ALU.mult,
                      r=xkeys + [(stkey, 2), (stkey, 7)], w=[ykey])
                yield
                em.tt("vector", Y[0:n, :], Y[0:n, :], LNG[0:n, :], ALU.mult, r=[ykey, "LNG"], w=[ykey])
                yield
                em.tt("vector", Y[0:n, :], Y[0:n, :], LNB[0:n, :], ALU.add, r=[ykey, "LNB"], w=[ykey])
                em.dma("sync", y_dst, Y[0:n, :], r=[ykey])

            if do_peer:
                tiles = [(ti * 128, 128, y_p[ti * 128:(ti + 1) * 128, :]) for ti in range(T // 128)]
                if NSR > 0:
                    tiles.append((T, NSR, y_s[:, :]))
                ctxs = [dict() for _ in tiles]
                for _ in prologue(tiles[0][0], tiles[0][1], ctxs[0]):
                    pass
                items = [(t, grp) for t in range(len(tiles)) for grp in range(NGRP)]
                gens = {}
                epi = {"gen": None}
                emit_dots_dve(0, tiles[0][1], ctxs[0], 0)
                emit_dots_act(0, tiles[0][1], ctxs[0], 0)
                for i, (t, grp) in enumerate(items):
                    row0, n, y_dst = tiles[t]
                    if grp == 0 and t + 1 < len(tiles):
                        gens[t + 1] = prologue(tiles[t + 1][0], tiles[t + 1][1], ctxs[t + 1])
                    nx = items[i + 1] if i + 1 < len(items) else None
                    if nx is not None and nx[0] != t and (nx[0] in gens):
                        for _ in gens.pop(nx[0]):
                            pass
                    emit_tail_a(t, n, ctxs[t], grp)
                    if nx is not None:
                        emit_dots_dve(nx[0], tiles[nx[0]][1], ctxs[nx[0]], nx[1])
                    emit_tail_b(t, n, ctxs[t], grp)
                    if nx is not None:
                        emit_dots_act(nx[0], tiles[nx[0]][1], ctxs[nx[0]], nx[1])
                    if (t + 1) in gens:
                        try:
                            next(gens[t + 1])
                        except StopIteration:
                            gens.pop(t + 1)
                    if epi["gen"] is not None:
                        try:
                            next(epi["gen"])
                        except StopIteration:
                            epi["gen"] = None
                    if grp == NGRP - 1:
                        if epi["gen"] is not None:
                            for _ in epi["gen"]:
                                pass
                        epi["gen"] = epilogue(t, n, ctxs[t], y_dst)
                        next(epi["gen"])
                        next(epi["gen"])
                if epi["gen"] is not None:
                    for _ in epi["gen"]:
                        pass
            p.finish(semstack, "b")
    return nc


def layer_norm(em, X, Y, JUNK, ST, G, B, n, xkeys, ykey, stkey, gk, bk, gb_eng="gpsimd"):
    jk = "JUNK"
    if JUNK is None:
        JUNK, jk = Y, ykey
    em.act(JUNK[0:n, :], X[0:n, :], AF.Identity, accum=ST[0:n, 0:1], r=xkeys, w=[jk, (stkey, 0)])
    em.act(JUNK[0:n, :], X[0:n, :], AF.Square, accum=ST[0:n, 1:2], r=xkeys, w=[jk, (stkey, 1)])
    em.ts("vector", ST[0:n, 2:3], ST[0:n, 0:1], 1.0 / D_MODEL, None, ALU.mult, r=[(stkey, 0)], w=[(stkey, 2)])
    em.ts("vector", ST[0:n, 3:4], ST[0:n, 1:2], 1.0 / D_MODEL, None, ALU.mult, r=[(stkey, 1)], w=[(stkey, 3)])
    em.tt("vector", ST[0:n, 4:5], ST[0:n, 2:3], ST[0:n, 2:3], ALU.mult, r=[(stkey, 2)], w=[(stkey, 4)])
    em.tt("vector", ST[0:n, 5:6], ST[0:n, 3:4], ST[0:n, 4:5], ALU.subtract, r=[(stkey, 3), (stkey, 4)], w=[(stkey, 5)])
    em.ts("vector", ST[0:n, 6:7], ST[0:n, 5:6], LN_EPS, None, ALU.add, r=[(stkey, 5)], w=[(stkey, 6)])
    em.act(ST[0:n, 6:7], ST[0:n, 6:7], AF.Ln, r=[(stkey, 6)], w=[(stkey, 6)])
    em.act(ST[0:n, 7:8], ST[0:n, 6:7], AF.Exp, scale=-0.5, r=[(stkey, 6)], w=[(stkey, 7)])
    em.ts("vector", Y[0:n, :], X[0:n, :], ST[0:n, 2:3], ST[0:n, 7:8], ALU.subtract, ALU.mult,
          r=xkeys + [(stkey, 2), (stkey, 7)], w=[ykey])
    em.tt(gb_eng, Y[0:n, :], Y[0:n, :], G[0:n, :], ALU.mult, r=[ykey, gk], w=[ykey])
    em.tt(gb_eng, Y[0:n, :], Y[0:n, :], B[0:n, :], ALU.add, r=[ykey, bk], w=[ykey])


def _fm(v):
    return np.ascontiguousarray(np.asarray(v, np.float32).reshape(4, 128).T)


def _blockdiag(w):
    out = np.zeros((128, 4, 128), np.float32)
    for cc in range(4):
        out[0:64, cc, 0:64] = w[2 * cc]
        out[64:128, cc, 64:128] = w[2 * cc + 1]
    return out


def make_consts(NB):
    NJ = max(1, NB // 128)
    ident = np.eye(128, dtype=np.float32)
    tri = np.triu(np.ones((128, 128), np.float32))
    k = np.arange(128)[:, None]
    q = np.arange(NB)[None, :]
    mask = np.stack([(q >= j * 128 + k).astype(np.float32) for j in range(NJ)], axis=1)
    iota = np.broadcast_to(np.arange(16, dtype=np.float32), (128, 16)).copy()
    return dict(c_ident=ident, c_tri=tri, c_mask=np.ascontiguousarray(mask), c_iota=iota)


def shared_maps(inp, NB):
    f = lambda a: np.ascontiguousarray(np.asarray(a, np.float32))
    rep = lambda v: np.ascontiguousarray(np.broadcast_to(np.asarray(v, np.float32).reshape(1, -1), (128, np.asarray(v).size)))
    m = dict(
        w_in=f(inp["w_in"][0]), bF=rep(inp["b_forget"][0]),
        cw=np.ascontiguousarray(np.asarray(inp["conv_w"][0], np.float32).reshape(4, 4, 128).transpose(2, 1, 0)),
        cb=_fm(inp["conv_b"][0]), wa=_blockdiag(np.asarray(inp["w_rg_a"][0], np.float32)), ba=_fm(inp["b_rg_a"][0]),
        wx=_blockdiag(np.asarray(inp["w_rg_x"][0], np.float32)), bx=_fm(inp["b_rg_x"][0]), lam=_fm(inp["lru_lambda"][0]),
        w_au=f(inp["w_attn_up"][0]), w_ru=f(inp["w_rnn_up"][0]), w_out=f(inp["w_out"][0]),
        ln1g=rep(inp["ln1_g"][0]), ln1b=rep(inp["ln1_b"][0]), wq=f(inp["peer_w_query"][0]),
        k1T=f(np.asarray(inp["peer_keys_1"][0]).T), k2T=f(np.asarray(inp["peer_keys_2"][0]).T),
        pu=f(inp["peer_u"][0]), pv=f(inp["peer_v"][0]), ln2g=rep(inp["ln2_g"][0]), ln2b=rep(inp["ln2_b"][0]),
    )
    m.update(make_consts(NB))
    return m


def core_map(inp, c, NS, shared):
    f = lambda a: np.ascontiguousarray(np.asarray(a, np.float32))
    sl = slice(c * NS, (c + 1) * NS)
    ck = np.asarray(inp["cache_k"][0][sl], np.float32)
    ns, past = ck.shape[0], ck.shape[1]
    ckT = ck.reshape(ns, past, 4, 2, 64).transpose(0, 3, 4, 2, 1).reshape(ns, 128, 4, past)
    sconv = np.asarray(inp["state_conv"][0][sl], np.float32)
    sconvT = sconv.reshape(ns, 3, 4, 128).transpose(0, 3, 2, 1)
    srnn = np.asarray(inp["state_rnn"][0][sl], np.float32)
    srnnT = srnn.reshape(ns, 4, 128).transpose(0, 2, 1)
    xs = np.asarray(inp["x_sample"][sl], np.float32)
    m = dict(shared)
    m.update(
        xp=f(inp["x_prompt"][c]), xs=f(xs.reshape(-1, D_MODEL)), ckT=f(ckT),
        cv=f(np.asarray(inp["cache_v"][0][sl]).reshape(ns, past, 512)),
        clf=f(inp["cache_logf"][0][sl]), sconvT=f(sconvT), srnnT=f(srnnT),
    )
    return m


def kernel(**inp):
    NCORES = 8
    T = inp["x_prompt"].shape[1]
    NS = inp["x_sample"].shape[0] // NCORES
    TS = inp["x_sample"].shape[1]
    PAST = inp["cache_k"].shape[2]
    NB = 256
    nc = build(T=T, NS=NS, TS=TS, PAST=PAST, NB=NB)
    shared = shared_maps(inp, NB)
    in_maps = [core_map(inp, c, NS, shared) for c in range(NCORES)]
    res = run_bass_kernel_spmd(nc, in_maps, core_ids=list(range(NCORES)))
    R = res.results
    B = NCORES
    cat = lambda k: np.stack([np.asarray(R[c][k], np.float32) for c in range(B)], 0)
    y_p = cat("y_p")
    y_s = cat("y_s").reshape(B * NS, TS, D_MODEL)
    k_p = cat("k_p").reshape(1, B, T, 8, 64)
    v_p = cat("v_p").reshape(1, B, T, 8, 64)
    lf_p = cat("lf_p").reshape(1, B, T, 8)
    conv_p = cat("conv_p").reshape(1, B, 3, 512)
    rnn_p = cat("rnn_p").reshape(1, B, 512)
    k_s = cat("k_s").reshape(1, B * NS, TS, 8, 64)
    v_s = cat("v_s").reshape(1, B * NS, TS, 8, 64)
    lf_s = cat("lf_s").reshape(1, B * NS, TS, 8)
    conv_s = cat("conv_s").reshape(1, B * NS, 3, 512)
    rnn_s = cat("rnn_s").reshape(1, B * NS, 512)
    return (y_p, y_s, k_p, v_p, lf_p, conv_p, rnn_p, k_s, v_s, lf_s, conv_s, rnn_s)
```

```python
import contextlib
import os
import numpy as np
import concourse.bass as bass
import concourse.mybir as mybir
from concourse.bass_utils import run_bass_kernel_spmd

F32 = mybir.dt.float32
BF16 = mybir.dt.bfloat16
I32 = mybir.dt.int32
U32 = mybir.dt.uint32
AF = mybir.ActivationFunctionType
ALU = mybir.AluOpType
AX = mybir.AxisListType

ENGS = ("sync", "scalar", "vector", "gpsimd", "tensor")
NPOOL = 16

D_MODEL = 1024
D_IN = 4616
N_EXP = 16384
ALPHA = 2.0 ** 0.25
LN_EPS = 1e-5
GELU_C = 1.5957691216057308


class Op:
    __slots__ = ("eng", "fn", "dma", "deps", "needed", "sem", "val", "pre")

    def __init__(self, eng, fn, dma):
        self.eng = eng
        self.fn = fn
        self.dma = dma
        self.deps = ()
        self.needed = False
        self.sem = None
        self.val = 0
        self.pre = None


class Prog:
    def __init__(self, nc):
        self.nc = nc
        self.ops = {e: [] for e in ENGS}
        self.last_writer = {}
        self.readers = {}

    def op(self, eng, fn, reads=(), writes=(), dma=False):
        o = Op(eng, fn, dma)
        deps = set()
        for k in reads:
            w = self.last_writer.get(k)
            if w is not None:
                deps.add(w)
        for k in writes:
            w = self.last_writer.get(k)
            if w is not None:
                deps.add(w)
            for r in self.readers.get(k, ()):
                deps.add(r)
        if eng == "tensor" and not dma:
            deps = {d for d in deps if not (d.eng == "tensor" and not d.dma)}
        o.deps = deps
        for d in deps:
            d.needed = True
        if dma:
            o.needed = True
        for k in reads:
            self.readers.setdefault(k, []).append(o)
        for k in writes:
            self.last_writer[k] = o
            self.readers[k] = []
        self.ops[eng].append(o)
        return o

    def finish(self, semstack, tag):
        nc = self.nc
        esem = {e: semstack.enter_context(nc.semaphore("s%s_%s" % (tag, e))) for e in ENGS}
        pools = {e: [semstack.enter_context(nc.semaphore("d%s_%s_%d" % (tag, e, i))) for i in range(NPOOL)]
                 for e in ("sync", "scalar", "gpsimd") if any(o.dma for o in self.ops[e])}
        final = {}
        for e in ENGS:
            cnt = 0
            ndma = 0
            for o in self.ops[e]:
                if o.dma:
                    o.sem = pools[e][ndma % NPOOL]
                    o.val = 16 * (ndma // NPOOL + 1)
                    if ndma >= NPOOL:
                        o.pre = (o.sem, 16 * (ndma // NPOOL))
                    ndma += 1
                    final[id(o.sem)] = (o.sem, o.val)
                elif o.needed:
                    cnt += 1
                    o.sem = esem[e]
                    o.val = cnt
                    final[id(o.sem)] = (o.sem, o.val)
        for e in ENGS:
            for o in reversed(self.ops[e]):
                if not o.dma:
                    if not o.needed:
                        o.needed = True
                        o.sem = esem[e]
                        o.val = final.get(id(esem[e]), (None, 0))[1] + 1
                        final[id(o.sem)] = (o.sem, o.val)
                    break

        def make(e):
            def body(eng):
                waited = {}
                for o in self.ops[e]:
                    if o.pre is not None:
                        s, v = o.pre
                        if waited.get(id(s), 0) < v:
                            eng.wait_ge(s, v)
                            waited[id(s)] = v
                    for d in o.deps:
                        s, v = d.sem, d.val
                        if waited.get(id(s), 0) < v:
                            eng.wait_ge(s, v)
                            waited[id(s)] = v
                    ins = o.fn(eng)
                    if o.dma:
                        ins.then_inc(o.sem, 16)
                    elif o.needed:
                        ins.then_inc(o.sem, 1)
                for s, v in final.values():
                    if waited.get(id(s), 0) < v:
                        eng.wait_ge(s, v)
            return body

        with nc.Block() as block:
            block.sync(make("sync"))
            block.scalar(make("scalar"))
            block.vector(make("vector"))
            block.gpsimd(make("gpsimd"))
            block.tensor(make("tensor"))


class Em:
    def __init__(self, p):
        self.p = p

    def dma(self, eng, out, in_, r=(), w=(), **kw):
        return self.p.op(eng, lambda e: e.dma_start(out=out, in_=in_, **kw), r, w, dma=True)

    def gather(self, out, table, idx, r=(), w=()):
        return self.p.op("gpsimd", lambda e: e.indirect_dma_start(
            out=out, out_offset=None, in_=table,
            in_offset=bass.IndirectOffsetOnAxis(ap=idx, axis=0)), r, w, dma=True)

    def mm(self, out, lhsT, rhs, start, stop, r=(), w=()):
        return self.p.op("tensor", lambda e: e.matmul(out, lhsT=lhsT, rhs=rhs, start=start, stop=stop), r, w)

    def tr(self, out, in_, ident, r=(), w=()):
        return self.p.op("tensor", lambda e: e.transpose(out=out, in_=in_, identity=ident), r, w)

    def act(self, out, in_, func, r=(), w=(), bias=None, scale=None, accum=None, eng="scalar"):
        kw = {}
        if bias is not None:
            kw["bias"] = bias
        if scale is not None:
            kw["scale"] = scale
        if accum is not None:
            kw["accum_out"] = accum
        return self.p.op(eng, lambda e: e.activation(out=out, in_=in_, func=func, **kw), r, w)

    def copy(self, eng, out, in_, r=(), w=()):
        if eng == "scalar":
            return self.p.op(eng, lambda e: e.copy(out=out, in_=in_), r, w)
        return self.p.op(eng, lambda e: e.tensor_copy(out=out, in_=in_), r, w)

    def tt(self, eng, out, in0, in1, op, r=(), w=()):
        return self.p.op(eng, lambda e: e.tensor_tensor(out=out, in0=in0, in1=in1, op=op), r, w)

    def ts(self, eng, out, in0, s1, s2, op0, op1=None, r=(), w=()):
        if op1 is None:
            return self.p.op(eng, lambda e: e.tensor_scalar(out=out, in0=in0, scalar1=s1, scalar2=None, op0=op0), r, w)
        return self.p.op(eng, lambda e: e.tensor_scalar(out=out, in0=in0, scalar1=s1, scalar2=s2, op0=op0, op1=op1), r, w)

    def tss(self, eng, out, in_, scalar, op, r=(), w=()):
        return self.p.op(eng, lambda e: e.tensor_single_scalar(out=out, in_=in_, scalar=scalar, op=op), r, w)

    def stt(self, eng, out, in0, scalar, in1, op0, op1, r=(), w=(), accum=None):
        if accum is None:
            return self.p.op(eng, lambda e: e.scalar_tensor_tensor(out=out, in0=in0, scalar=scalar, in1=in1, op0=op0, op1=op1), r, w)
        return self.p.op(eng, lambda e: e.scalar_tensor_tensor(out=out, in0=in0, scalar=scalar, in1=in1, op0=op0, op1=op1, accum_out=accum), r, w)

    def memset(self, eng, ap, val, r=(), w=()):
        return self.p.op(eng, lambda e: e.memset(ap, val), r, w)

    def recip(self, out, in_, r=(), w=()):
        return self.p.op("vector", lambda e: e.reciprocal(out=out, in_=in_), r, w)

    def scan(self, out, d0, d1, init, r=(), w=()):
        return self.p.op("vector", lambda e: e.tensor_tensor_scan(out=out, data0=d0, data1=d1, initial=init,
                                                                  op0=ALU.mult, op1=ALU.add), r, w)

    def vmax(self, out, in_, r=(), w=()):
        return self.p.op("vector", lambda e: e.max(out=out, in_=in_), r, w)

    def vmaxidx(self, out, mx, vals, r=(), w=()):
        return self.p.op("vector", lambda e: e.max_index(out=out, in_max=mx, in_values=vals), r, w)

    def vmatchrep(self, out, mx, vals, imm, r=(), w=()):
        return self.p.op("vector", lambda e: e.match_replace(out=out, in_to_replace=mx, in_values=vals, imm_value=imm), r, w)

    def reduce(self, eng, out, in_, op, r=(), w=()):
        return self.p.op(eng, lambda e: e.tensor_reduce(out=out, in_=in_, axis=AX.X, op=op), r, w)


def build(T=4096, NS=2, TS=32, PAST=1024, NB=256, do_peer=True, debug_h=False, stop=99):
    nc = bass.Bass("TRN2", target_bir_lowering=False)

    def din(name, shape, dt=F32):
        return nc.dram_tensor(name, list(shape), dt, kind="ExternalInput").ap()

    def dout(name, shape, dt=F32):
        return nc.dram_tensor(name, list(shape), dt, kind="ExternalOutput").ap()

    def dscr(name, shape, dt):
        return nc.dram_tensor(name, list(shape), dt, kind="Internal").ap()

    NSR = NS * TS
    NROW = T + NSR
    KMAX = max(T, PAST + TS)
    NTK = (KMAX + 127) // 128
    NJ = max(1, NB // 128)

    xp = din("xp", [T, D_MODEL])
    xs = din("xs", [NSR, D_MODEL])
    ckT = din("ckT", [NS, 128, 4, PAST])
    cv = din("cv", [NS, PAST, 512])
    clf = din("clf", [NS, PAST, 8])
    sconvT = din("sconvT", [NS, 128, 4, 3])
    srnnT = din("srnnT", [NS, 128, 4])
    w_in = din("w_in", [D_MODEL, D_IN])
    bF = din("bF", [128, 8])
    cw = din("cw", [128, 4, 4])
    cb = din("cb", [128, 4])
    wa = din("wa", [128, 4, 128])
    ba = din("ba", [128, 4])
    wx = din("wx", [128, 4, 128])
    bx = din("bx", [128, 4])
    lam = din("lam", [128, 4])
    w_au = din("w_au", [512, D_MODEL])
    w_ru = din("w_ru", [512, D_MODEL])
    w_out = din("w_out", [D_MODEL, D_MODEL])
    ln1g = din("ln1g", [128, D_MODEL])
    ln1b = din("ln1b", [128, D_MODEL])
    wq = din("wq", [D_MODEL, 2048])
    k1T = din("k1T", [128, 128])
    k2T = din("k2T", [128, 128])
    pu = din("pu", [N_EXP, D_MODEL])
    pv = din("pv", [N_EXP, D_MODEL])
    ln2g = din("ln2g", [128, D_MODEL])
    ln2b = din("ln2b", [128, D_MODEL])
    c_ident = din("c_ident", [128, 128])
    c_tri = din("c_tri", [128, 128])
    c_mask = din("c_mask", [128, NJ, NB])
    c_iota = din("c_iota", [128, 16])

    y_p = dout("y_p", [T, D_MODEL])
    y_s = dout("y_s", [NSR, D_MODEL])
    k_p = dout("k_p", [T, 512])
    v_p = dout("v_p", [T, 512])
    lf_p = dout("lf_p", [T, 8])
    conv_p = dout("conv_p", [3, 512])
    rnn_p = dout("rnn_p", [1, 512])
    k_s = dout("k_s", [NSR, 512])
    v_s = dout("v_s", [NSR, 512])
    lf_s = dout("lf_s", [NSR, 8])
    conv_s = dout("conv_s", [NS, 3, 512])
    rnn_s = dout("rnn_s", [NS, 512])

    w_in_b = dscr("w_in_b", [D_MODEL, 4672], BF16)
    w_au_b = dscr("w_au_b", [512, D_MODEL], BF16)
    w_ru_b = dscr("w_ru_b", [512, D_MODEL], BF16)
    w_out_b = dscr("w_out_b", [D_MODEL, D_MODEL], BF16)
    wq_b = dscr("wq_b", [D_MODEL, 2048], BF16)
    h_scr = dout("h_scr", [NROW, D_MODEL]) if debug_h else dscr("h_scr", [NROW, D_MODEL], F32)
    uv_b = dscr("uv_b", [N_EXP, 2048], BF16)

    semstack = contextlib.ExitStack()
    with semstack:
        with contextlib.ExitStack() as st:
            def sb(name, shape, dt=F32):
                return st.enter_context(nc.sbuf_tensor(name, list(shape), dt))

            def ps(name, shape, dt=F32):
                return st.enter_context(nc.psum_tensor(name, list(shape), dt))

            p = Prog(nc)
            em = Em(p)

            ID = sb("ID", [128, 128])
            TRI = sb("TRI", [128, 128])
            ONESF = sb("ONESF", [128, 128])
            MASKF = sb("MASKF", [128, NJ, NB])
            MASK = sb("MASK", [128, NJ, NB], BF16)
            BFt = sb("BFt", [128, 8])
            CW = sb("CW", [128, 4, 4])
            CB = sb("CB", [128, 4])
            WA = sb("WA", [128, 4, 128])
            WX = sb("WX", [128, 4, 128])
            BA = sb("BA", [128, 4])
            BX = sb("BX", [128, 4])
            LAM = sb("LAM", [128, 4])
            C8 = sb("C8", [128, 4])
            C16 = sb("C16", [128, 4])
            LNG = sb("LNG", [128, D_MODEL])
            LNB = sb("LNB", [128, D_MODEL])
            WF = sb("WF", [128, 8, 8], BF16)

            KT = sb("KT", [128, 4, KMAX], BF16)
            VA = sb("VA", [128, NTK, 4, 192], BF16)
            LF = sb("LF", [128, NTK, 8])
            FF = sb("FF", [128, NTK, 8])
            BIAS = sb("BIAS", [128, NTK, 8])
            RT = sb("RT", [128, 8])
            FREF = sb("FREF", [128, 8])
            HIST = sb("HIST", [128, 4, 3])
            HST = sb("HST", [128, 4])

            NWS = 5
            WS = [sb("WS%d" % i, [128, 8, 512], BF16) for i in range(NWS)]
            XS = [sb("XS%d" % i, [128, D_MODEL]) for i in range(2)]
            XT = sb("XT", [128, 8, NB], BF16)
            QTP = sb("QTP", [128, 8, NB], BF16)
            KTOK = [sb("KTOK%d" % i, [128, 512]) for i in range(2)]
            VTOK = [sb("VTOK%d" % i, [128, 512]) for i in range(2)]
            LT = sb("LT", [128, 8])
            XRs = [sb("XR%d" % i, [128, NB + 3]) for i in range(2)]
            GGs = [sb("GG%d" % i, [128, NB]) for i in range(2)]
            XCs = [sb("XC%d" % i, [128, NB]) for i in range(2)]
            RR = sb("RR", [128, NB])
            II = sb("II", [128, NB])
            AA = sb("AA", [128, NB])
            BB = sb("BB", [128, NB])
            HS = sb("HS", [128, NB])
            T1 = sb("T1", [128, NB])
            T2 = sb("T2", [128, NB])
            ROT = sb("ROT", [128, 4, NB], BF16)
            NPB = 3
            PB = [sb("PB%d" % i, [128, NB], BF16) for i in range(NPB)]
            RECS = [sb("RECS%d" % i, [128, NB]) for i in range(1)]
            OT = sb("OT", [128, 4, NB], BF16)
            SA = sb("SA", [128, NB])
            SR = sb("SR", [128, NB])
            M1 = sb("M1", [128, NB])
            M2 = sb("M2", [128, NB])
            MT = sb("MT", [128, 8, NB], BF16)
            PRE = [sb("PRE%d" % i, [128, D_MODEL]) for i in range(2)]
            HO = [sb("HO%d" % i, [128, D_MODEL]) for i in range(2)]
            STAT = [sb("STAT%d" % i, [128, 8]) for i in range(2)]

            PZ = [ps("PZ%d" % i, [128, 512]) for i in range(2)]
            PSS = [ps("PSS%d" % i, [128, 512]) for i in range(3)]
            PO = [ps("PO%d" % i, [128, 512]) for i in range(2)]
            PX = ps("PX", [128, 512])
            if os.environ.get("KDEBUG"):
                print("phase1 sbuf remaining", nc.sbuf_bytes_remaining)

            for r0 in range(0, D_MODEL, 128):
                em.dma("gpsimd", w_in_b[r0:r0 + 128, 0:1536], w_in[r0:r0 + 128, 0:1536], w=[("w_in_b", r0 // 128, 0)])
                em.dma("gpsimd", w_in_b[r0:r0 + 128, 4608:4616], w_in[r0:r0 + 128, 1536:1544], w=[("w_in_b", r0 // 128, 2)])
            for r0 in range(0, D_MODEL, 128):
                em.dma("gpsimd", w_in_b[r0:r0 + 128, 1536:4608], w_in[r0:r0 + 128, 1544:4616], w=[("w_in_b", r0 // 128, 1)])
            for r0 in range(0, 512, 128):
                em.dma("gpsimd", w_au_b[r0:r0 + 128, :], w_au[r0:r0 + 128, :], w=[("w_au_b", r0 // 128)])
                em.dma("gpsimd", w_ru_b[r0:r0 + 128, :], w_ru[r0:r0 + 128, :], w=[("w_ru_b", r0 // 128)])
            for r0 in range(0, D_MODEL, 128):
                em.dma("gpsimd", w_out_b[r0:r0 + 128, :], w_out[r0:r0 + 128, :], w=[("w_out_b", r0 // 128)])
            for r0 in range(0, D_MODEL, 128):
                em.dma("gpsimd", wq_b[r0:r0 + 128, :], wq[r0:r0 + 128, :], w=[("wq_b", r0 // 128)])
            W_IN_KEYS = [("w_in_b", i, 1) for i in range(8)]
            W_IN_KEYS0 = [("w_in_b", i, 0) for i in range(8)]
            W_IN_KEYS2 = [("w_in_b", i, 2) for i in range(8)]
            W_AU_KEYS = [("w_au_b", i) for i in range(4)]
            W_RU_KEYS = [("w_ru_b", i) for i in range(4)]
            W_OUT_KEYS = [("w_out_b", i) for i in range(8)]

            for (t_, d_, k_) in ((ID, c_ident, "ID"), (TRI, c_tri, "TRI"), (MASKF, c_mask, "MASKF"), (BFt, bF, "BFt"),
                                 (CW, cw, "CW"), (CB, cb, "CB"), (WA, wa, "WA"), (WX, wx, "WX"), (BA, ba, "BA"),
                                 (BX, bx, "BX"), (LAM, lam, "LAM"), (LNG, ln1g, "LNG"), (LNB, ln1b, "LNB")):
                em.dma("sync", t_[:], d_, w=[k_])
            em.memset("vector", ONESF[:], 1.0, w=["ONESF"])
            em.ts("vector", BA[:], BA[:], -1.0, None, ALU.mult, r=["BA"], w=["BA"])
            em.ts("vector", BX[:], BX[:], -1.0, None, ALU.mult, r=["BX"], w=["BX"])
            em.memset("vector", FF[:], 0.0, w=[("FF", kt) for kt in range(NTK)])
            em.memset("vector", LF[:], 0.0, w=[("LF", kt) for kt in range(NTK)])
            em.copy("vector", MASK[:], MASKF[:], r=["MASKF"], w=["MASK"])
            em.memset("vector", VA[:, :, :, 64:128], 1.0, w=["VA1"])
            em.memset("vector", QTP[:], 0.0, w=[("QTP", h_) for h_ in range(8)])
            em.act(C8[:], LAM[:], AF.Exp, scale=-1.0, r=["LAM"], w=["C8"])
            em.act(C8[:], C8[:], AF.Ln, bias=1.0, r=["C8"], w=["C8"])
            em.ts("vector", C16[:], C8[:], -16.0, None, ALU.mult, r=["C8"], w=["C16"])
            em.ts("vector", C8[:], C8[:], -8.0, None, ALU.mult, r=["C8", "C16"], w=["C8"])
            em.dma("sync", WF[:], w_in_b.rearrange("(kc p) c -> p kc c", p=128)[:, :, 4608:4616],
                   r=W_IN_KEYS2, w=["WF"])

            ws_rr = [0]

            def load_w(src_ap, shape_part, key_reads):
                i = ws_rr[0] % NWS
                ws_rr[0] += 1
                dst = WS[i]
                npart, nk, ncol = shape_part
                em.dma("sync", dst[0:npart, 0:nk, 0:ncol], src_ap, r=key_reads, w=[("WS", i)])
                return dst, ("WS", i)

            w_in_v = w_in_b.rearrange("(kc p) c -> p kc c", p=128)
            w_out_v = w_out_b.rearrange("(kc p) c -> p kc c", p=128)
            w_ru_v = w_ru_b.rearrange("(kc p) c -> p kc c", p=128)
            w_au_v = w_au_b.rearrange("(kc p) c -> p kc c", p=128)

            rr = {"xs": 0, "pz": 0, "kt": 0, "pss": 0, "po": 0, "pb": 0, "bc": 0, "pre": 0, "rg": 0}

            def nxt(name, n):
                v = rr[name] % n
                rr[name] += 1
                return v

            def cumsum_tile(kt, rows):
                if os.environ.get("SKIP_CUMSUM"):
                    return
                em.mm(PX[0:rows, 8:16], TRI[0:rows, 0:rows], LF[0:rows, kt, :], True, True,
                      r=["TRI", ("LF", kt)], w=["PX"])
                em.mm(PX[:, 16:24], ONESF[0:rows, :], LF[0:rows, kt, :], True, True,
                      r=["ONESF", ("LF", kt)], w=["PX"])
                em.tt("vector", FF[0:rows, kt, :], PX[0:rows, 8:16], RT[0:rows, :], ALU.add,
                      r=["PX", "RT"], w=[("FF", kt)])
                em.tt("vector", RT[:], PX[:, 16:24], RT[:], ALU.add, r=["PX", "RT"], w=["RT"])

            def process_block(xsrc, t0, NBs, kbase, outs, is_last):
                k_out, v_out, lf_out, h_row0 = outs
                TP = min(128, NBs)
                ntile = NBs // TP
                kpos0 = kbase + t0
                kt_first = kpos0 // 128
                n_kt = (kpos0 + NBs + 127) // 128
                new_kts = list(range(kt_first, n_kt))

                for i in range(ntile):
                    s = nxt("xs", 2)
                    em.dma("sync", XS[s][0:TP, :], xsrc[t0 + i * TP: t0 + (i + 1) * TP, :], w=[("XS", s)])
                    for half in range(2):
                        z = nxt("pz", 2)
                        for j in range(4):
                            kc = half * 4 + j
                            em.tr(PZ[z][:, j * TP:(j + 1) * TP], XS[s][0:TP, kc * 128:(kc + 1) * 128], ID[0:TP, 0:TP],
                                  r=[("XS", s), "ID"], w=[("PZ", z)])
                        em.copy("scalar", XT[:, half * 4:(half + 1) * 4, i * TP:(i + 1) * TP],
                                PZ[z][:, 0:4 * TP].rearrange("p (j t) -> p j t", j=4), r=[("PZ", z)], w=["XT"])
                yield "A"
                if stop <= 1:
                    return

                wq_t, wq_k = load_w(w_in_v[:, :, 0:512], (128, 8, 512), W_IN_KEYS0)
                for pc in range(4):
                    z = nxt("pz", 2)
                    for kc in range(8):
                        em.mm(PZ[z][:, 0:NBs], wq_t[:, kc, pc * 128:(pc + 1) * 128], XT[:, kc, 0:NBs], kc == 0, kc == 7,
                              r=[wq_k, "XT"], w=[("PZ", z)])
                    em.copy("scalar", QTP[0:64, 2 * pc, 0:NBs], PZ[z][0:64, 0:NBs], r=[("PZ", z)], w=[("QTP", 2 * pc)])
                    em.copy("scalar", QTP[64:128, 2 * pc + 1, 0:NBs], PZ[z][64:128, 0:NBs], r=[("PZ", z)], w=[("QTP", 2 * pc + 1)])
                wk_t, wk_k = load_w(w_in_v[:, :, 512:1024], (128, 8, 512), W_IN_KEYS0)
                for pc in range(4):
                    z = nxt("pz", 2)
                    for kc in range(8):
                        em.mm(PZ[z][:, 0:NBs], wk_t[:, kc, pc * 128:(pc + 1) * 128], XT[:, kc, 0:NBs], kc == 0, kc == 7,
                              r=[wk_k, "XT"], w=[("PZ", z)])
                    em.copy("scalar", KT[:, pc, kpos0:kpos0 + NBs], PZ[z][:, 0:NBs], r=[("PZ", z)],
                            w=[("KT", kt) for kt in new_kts])
                wv_t, wv_k = load_w(w_in_v[:, :, 1024:1536], (128, 8, 512), W_IN_KEYS0)
                tile_kts = []
                for i in range(ntile):
                    kt = (kpos0 + i * TP) // 128
                    r0 = (kpos0 + i * TP) % 128
                    assert r0 == 0
                    tile_kts.append(kt)
                    tok = slice(i * TP, (i + 1) * TP)
                    s = nxt("kt", 2)
                    z = nxt("pz", 2)
                    for kc in range(8):
                        em.mm(PZ[z][0:TP, :], XT[:, kc, tok], wk_t[:, kc, :], kc == 0, kc == 7,
                              r=[wk_k, "XT"], w=[("PZ", z)])
                    em.copy("vector", KTOK[s][0:TP, :], PZ[z][0:TP, :], r=[("PZ", z)], w=[("KTOK", s)])
                    em.dma("gpsimd", k_out[t0 + i * TP: t0 + (i + 1) * TP, :], KTOK[s][0:TP, :], r=[("KTOK", s)])
                    z = nxt("pz", 2)
                    for kc in range(8):
                        em.mm(PZ[z][0:TP, :], XT[:, kc, tok], wv_t[:, kc, :], kc == 0, kc == 7,
                              r=[wv_k, "XT"], w=[("PZ", z)])
                    em.copy("scalar", VTOK[s][0:TP, :], PZ[z][0:TP, :], r=[("PZ", z)], w=[("VTOK", s)])
                    vt4 = VTOK[s][0:TP, :].rearrange("p (c t d) -> p c t d", c=4, t=2)
                    em.copy("gpsimd", VA[0:TP, kt, :, 0:64], vt4[:, :, 0, :], r=[("VTOK", s)], w=[("VA", kt)])
                    em.copy("gpsimd", VA[0:TP, kt, :, 128:192], vt4[:, :, 1, :], r=[("VTOK", s)], w=[("VA", kt)])
                    em.dma("gpsimd", v_out[t0 + i * TP: t0 + (i + 1) * TP, :], VTOK[s][0:TP, :], r=[("VTOK", s)])
                    for kc in range(8):
                        em.mm(PX[0:TP, 0:8], XT[:, kc, tok], WF[:, kc, :], kc == 0, kc == 7, r=["WF", "XT"], w=["PX"])
                    em.tt("vector", LT[0:TP, :], PX[0:TP, 0:8], BFt[0:TP, :], ALU.add, r=["PX", "BFt"], w=["LT"])
                    em.act(LT[0:TP, :], LT[0:TP, :], AF.Exp, scale=-1.0, r=["LT"], w=["LT"])
                    em.act(LT[0:TP, :], LT[0:TP, :], AF.Ln, bias=1.0, r=["LT"], w=["LT"])
                    em.ts("vector", LF[0:TP, kt, :], LT[0:TP, :], -1.0, None, ALU.mult, r=["LT"], w=[("LF", kt)])
                    em.dma("gpsimd", lf_out[t0 + i * TP: t0 + (i + 1) * TP, :], LF[0:TP, kt, :], r=[("LF", kt)])
                if stop <= 2:
                    return

                rgbuf = {}
                rgw = {}

                def rg_inproj(cc):
                    b_ = nxt("rg", 2)
                    rgbuf[cc] = b_
                    XR, GG, XC = XRs[b_], GGs[b_], XCs[b_]
                    xck = ("XC", b_)
                    if "xr" not in rgw:
                        rgw["xr"] = load_w(w_in_v[:, :, 1536:2048], (128, 8, 512), W_IN_KEYS)
                        rgw["gate"] = load_w(w_in_v[:, :, 2048:2560], (128, 8, 512), W_IN_KEYS)
                    wr_t, wr_k = rgw["xr"]
                    wg_t, wg_k = rgw["gate"]
                    csl = slice(cc * 128, (cc + 1) * 128)
                    za = nxt("pz", 2)
                    for kc in range(8):
                        em.mm(PZ[za][:, 0:NBs], wr_t[:, kc, csl], XT[:, kc, 0:NBs], kc == 0, kc == 7,
                              r=[wr_k, "XT"], w=[("PZ", za)])
                    em.copy("vector", XR[:, 0:3], HIST[:, cc, :], r=[("HIST", cc)], w=[("XRh", b_)])
                    em.copy("scalar", XR[:, 3:3 + NBs], PZ[za][:, 0:NBs], r=[("PZ", za)], w=[("XRb", b_)])
                    zb = nxt("pz", 2)
                    for kc in range(8):
                        em.mm(PZ[zb][:, 0:NBs], wg_t[:, kc, csl], XT[:, kc, 0:NBs], kc == 0, kc == 7,
                              r=[wg_k, "XT"], w=[("PZ", zb)])
                    em.copy("scalar", GG[:, 0:NBs], PZ[zb][:, 0:NBs], r=[("PZ", zb)], w=[("GG", b_)])
                    xk = [("XRh", b_), ("XRb", b_)]
                    em.copy("vector", HIST[:, cc, :], XR[:, NBs:NBs + 3], r=xk, w=[("HIST", cc)])
                    em.ts("vector", XC[:, 0:NBs], XR[:, 0:NBs], CW[:, cc, 0:1], CB[:, cc:cc + 1], ALU.mult, ALU.add,
                          r=xk + ["CW", "CB"], w=[xck])
                    for wtap in range(1, 4):
                        em.stt("vector", XC[:, 0:NBs], XR[:, wtap:wtap + NBs], CW[:, cc, wtap:wtap + 1], XC[:, 0:NBs],
                               ALU.mult, ALU.add, r=xk + ["CW", xck], w=[xck])

                def rg_chain(cc):
                    b_ = rgbuf[cc]
                    GG, XC = GGs[b_], XCs[b_]
                    gk, xck = ("GG", b_), ("XC", b_)
                    N = slice(0, NBs)
                    z1 = nxt("pz", 2)
                    em.mm(PZ[z1][:, N], WA[:, cc, :], XC[:, N], True, True, r=["WA", xck], w=[("PZ", z1)])
                    z2 = nxt("pz", 2)
                    em.mm(PZ[z2][:, N], WX[:, cc, :], XC[:, N], True, True, r=["WX", xck], w=[("PZ", z2)])
                    em.tt("gpsimd", T2[:, N], GG[:, N], GG[:, N], ALU.mult, r=[gk], w=["T2"])
                    em.ts("gpsimd", T2[:, N], T2[:, N], 0.044715, 1.0, ALU.mult, ALU.add, r=["T2"], w=["T2"])
                    em.tt("gpsimd", T2[:, N], T2[:, N], GG[:, N], ALU.mult, r=["T2", gk], w=["T2"])
                    yield
                    em.act(RR[:, N], PZ[z1][:, N], AF.Exp, bias=BA[:, cc:cc + 1], scale=-1.0, r=[("PZ", z1), "BA"], w=["RR"])
                    em.act(II[:, N], PZ[z2][:, N], AF.Exp, bias=BX[:, cc:cc + 1], scale=-1.0, r=[("PZ", z2), "BX"], w=["II"])
                    em.act(T2[:, N], T2[:, N], AF.Exp, scale=-GELU_C, r=["T2"], w=["T2"])
                    yield
                    em.ts("vector", RR[:, N], RR[:, N], 1.0, None, ALU.add, r=["RR"], w=["RR"])
                    em.recip(RR[:, N], RR[:, N], r=["RR"], w=["RR"])
                    yield
                    em.act(AA[:, N], RR[:, N], AF.Exp, scale=C8[:, cc:cc + 1], r=["RR", "C8"], w=["AA"])
                    em.act(T1[:, N], RR[:, N], AF.Exp, scale=C16[:, cc:cc + 1], r=["RR", "C16"], w=["T1"])
                    em.ts("vector", II[:, N], II[:, N], 1.0, None, ALU.add, r=["II"], w=["II"])
                    em.recip(II[:, N], II[:, N], r=["II"], w=["II"])
                    yield
                    em.ts("vector", T1[:, N], T1[:, N], -1.0, 1.0, ALU.mult, ALU.add, r=["T1"], w=["T1"])
                    em.tt("vector", BB[:, N], II[:, N], XC[:, N], ALU.mult, r=["II", xck], w=["BB"])
                    em.ts("gpsimd", T2[:, N], T2[:, N], 1.0, None, ALU.add, r=["T2"], w=["T2"])
                    yield
                    em.act(T1[:, N], T1[:, N], AF.Ln, r=["T1"], w=["T1"])
                    em.recip(T2[:, N], T2[:, N], r=["T2"], w=["T2"])
                    yield
                    em.act(T1[:, N], T1[:, N], AF.Exp, scale=0.5, r=["T1"], w=["T1"])
                    em.tt("gpsimd", T2[:, N], T2[:, N], GG[:, N], ALU.mult, r=["T2", gk], w=["T2"])
                    yield
                    em.tt("vector", BB[:, N], BB[:, N], T1[:, N], ALU.mult, r=["BB", "T1"], w=["BB"])
                    yield
                    em.scan(HS[:, N], AA[:, N], BB[:, N], HST[:, cc:cc + 1], r=["AA", "BB", ("HST", cc)], w=["HS"])
                    yield
                    em.copy("vector", HST[:, cc:cc + 1], HS[:, NBs - 1:NBs], r=["HS"], w=[("HST", cc)])
                    em.tt("vector", ROT[:, cc, N], HS[:, N], T2[:, N], ALU.mult, r=["HS", "T2"], w=["ROT"])

                rg_state = {"next_cc": 0, "gen": None}

                def rg_step(flush=False):
                    while True:
                        if rg_state["gen"] is None:
                            cc = rg_state["next_cc"]
                            if cc >= 4:
                                return
                            rg_state["gen"] = rg_chain(cc)
                        try:
                            next(rg_state["gen"])
                        except StopIteration:
                            rg_state["gen"] = None
                            rg_state["next_cc"] += 1
                            if rg_state["next_cc"] < 4:
                                rg_inproj(rg_state["next_cc"])
                        if not flush:
                            return

                rg_inproj(0)
                if len(tile_kts) == 1:
                    em.copy("vector", FREF[:], RT[:], r=["RT"], w=["FREF"])
                for ti_, kt in enumerate(tile_kts):
                    cumsum_tile(kt, TP)
                    if len(tile_kts) > 1 and ti_ == len(tile_kts) // 2 - 1:
                        em.copy("vector", FREF[:], RT[:], r=["RT"], w=["FREF"])
                if stop <= 3:
                    rg_step(flush=True)
                    return

                em.tt("vector", BIAS[:, 0:n_kt, :], FREF[:, :].unsqueeze(1).to_broadcast([128, n_kt, 8]),
                      FF[:, 0:n_kt, :], ALU.subtract, r=["FREF"] + [("FF", kt) for kt in range(n_kt)], w=["BIAS"])
                items = [(h, kt) for h in range(8) for kt in range(n_kt)]
                po_of = {}

                def kp_of(kt):
                    return min(128, kpos0 + NBs - kt * 128)

                def emit_qk(i):
                    h, kt = items[i]
                    kp = kp_of(kt)
                    s = i % 3
                    em.mm(PSS[s][0:kp, 0:NBs], KT[:, h // 2, kt * 128:kt * 128 + kp], QTP[:, h, 0:NBs],
                          True, True, r=[("KT", kt), ("QTP", h)], w=[("PSS", s)])

                def emit_exp(i):
                    h, kt = items[i]
                    kp = kp_of(kt)
                    s = i % 3
                    b = i % NPB
                    em.act(PB[b][0:kp, 0:NBs], PSS[s][0:kp, 0:NBs], AF.Exp, bias=BIAS[0:kp, kt, h:h + 1], scale=0.125,
                           r=[("PSS", s), "BIAS"], w=[("PB", b)])
                    if kt >= kt_first:
                        j = kt - kt_first
                        em.tt("gpsimd", PB[b][0:kp, 0:NBs], PB[b][0:kp, 0:NBs], MASK[0:kp, j, 0:NBs], ALU.mult,
                              r=[("PB", b), "MASK"], w=[("PB", b)])

                def emit_pv(i):
                    h, kt = items[i]
                    kp = kp_of(kt)
                    b = i % NPB
                    if kt == 0:
                        po_of[h] = nxt("po", 2)
                    o = po_of[h]
                    pc, odd = h // 2, h % 2
                    lhsT = VA[0:kp, kt, pc, 64:192] if odd else VA[0:kp, kt, pc, 0:128]
                    em.mm(PO[o][:, 0:NBs], lhsT, PB[b][0:kp, 0:NBs], kt == 0, kt == n_kt - 1,
                          r=[("VA", kt), "VA1", ("PB", b)], w=[("PO", o)])
                    if kt == n_kt - 1:
                        c = 0
                        osl = slice(64, 128) if odd else slice(0, 64)
                        dsl = slice(0, 64) if odd else slice(64, 128)
                        em.recip(RECS[c][osl, 0:NBs], PO[o][dsl, 0:NBs], r=[("PO", o)], w=[("RECS", c)])
                        em.tt("vector", OT[osl, pc, 0:NBs], PO[o][osl, 0:NBs], RECS[c][osl, 0:NBs], ALU.mult,
                              r=[("PO", o), ("RECS", c)], w=["OT"])

                emit_qk(0)
                if len(items) > 1:
                    emit_qk(1)
                for i in range(len(items)):
                    if i + 2 < len(items):
                        emit_qk(i + 2)
                    emit_exp(i)
                    emit_pv(i)
                    rg_step()
                rg_step(flush=True)
                if stop <= 4:
                    return

                for hf in range(2):
                    cs = slice(hf * 512, (hf + 1) * 512)
                    ga_t, ga_k = load_w(w_in_v[:, :, 2560 + hf * 512:2560 + (hf + 1) * 512], (128, 8, 512), W_IN_KEYS)
                    gr_t, gr_k = load_w(w_in_v[:, :, 3584 + hf * 512:3584 + (hf + 1) * 512], (128, 8, 512), W_IN_KEYS)
                    i_ = ws_rr[0] % NWS
                    ws_rr[0] += 1
                    em.dma("sync", WS[i_][:, 0:4, :], w_au_v[:, :, cs], r=W_AU_KEYS, w=[("WS", i_)])
                    em.dma("sync", WS[i_][:, 4:8, :], w_ru_v[:, :, cs], r=W_RU_KEYS, w=[("WS", i_)])
                    au_t, au_k = WS[i_], ("WS", i_)
                    ru_k = au_k
                    for fc in range(4):
                        fs = slice(fc * 128, (fc + 1) * 128)
                        for kc in range(8):
                            em.mm(PZ[0][:, 0:NBs], ga_t[:, kc, fs], XT[:, kc, 0:NBs], kc == 0, kc == 7,
                                  r=[ga_k, "XT"], w=[("PZ", 0)])
                        em.act(SA[:, 0:NBs], PZ[0][:, 0:NBs], AF.Sigmoid, r=[("PZ", 0)], w=["SA"])
                        for kc in range(8):
                            em.mm(PZ[1][:, 0:NBs], gr_t[:, kc, fs], XT[:, kc, 0:NBs], kc == 0, kc == 7,
                                  r=[gr_k, "XT"], w=[("PZ", 1)])
                        em.act(SR[:, 0:NBs], PZ[1][:, 0:NBs], AF.Sigmoid, r=[("PZ", 1)], w=["SR"])
                        for kc in range(4):
                            em.mm(PSS[0][:, 0:NBs], au_t[:, kc, fs], OT[:, kc, 0:NBs], kc == 0, kc == 3,
                                  r=[au_k, "OT"], w=[("PSS", 0)])
                        em.tt("vector", M1[:, 0:NBs], PSS[0][:, 0:NBs], SA[:, 0:NBs], ALU.mult, r=[("PSS", 0), "SA"], w=["M1"])
                        for kc in range(4):
                            em.mm(PSS[1][:, 0:NBs], au_t[:, 4 + kc, fs], ROT[:, kc, 0:NBs], kc == 0, kc == 3,
                                  r=[ru_k, "ROT"], w=[("PSS", 1)])
                        em.tt("vector", M2[:, 0:NBs], PSS[1][:, 0:NBs], SR[:, 0:NBs], ALU.mult, r=[("PSS", 1), "SR"], w=["M2"])
                        em.tt("gpsimd", MT[:, hf * 4 + fc, 0:NBs], M1[:, 0:NBs], M2[:, 0:NBs], ALU.add,
                              r=["M1", "M2"], w=["MT"])
                yield "E"
                if stop <= 5:
                    return

                wo0_t, wo0_k = load_w(w_out_v[:, :, 0:512], (128, 8, 512), W_OUT_KEYS)
                wo1_t, wo1_k = load_w(w_out_v[:, :, 512:1024], (128, 8, 512), W_OUT_KEYS)
                for i in range(ntile):
                    tok = slice(i * TP, (i + 1) * TP)
                    s = nxt("xs", 2)
                    em.dma("sync", XS[s][0:TP, :], xsrc[t0 + i * TP: t0 + (i + 1) * TP, :], w=[("XS", s)])
                    q = nxt("pre", 2)
                    for hf, (wt, wk) in enumerate(((wo0_t, wo0_k), (wo1_t, wo1_k))):
                        for kc in range(8):
                            em.mm(PO[hf][0:TP, :], MT[:, kc, tok], wt[:, kc, :], kc == 0, kc == 7,
                                  r=[wk, "MT"], w=[("PO", hf)])
                        em.stt("vector", PRE[q][0:TP, hf * 512:(hf + 1) * 512], XS[s][0:TP, hf * 512:(hf + 1) * 512],
                               ALPHA, PO[hf][0:TP, :], ALU.mult, ALU.add, r=[("XS", s), ("PO", hf)], w=[("PRE", q, hf)])
                    layer_norm(em, PRE[q], HO[q], None, STAT[q], LNG, LNB, TP,
                               [("PRE", q, 0), ("PRE", q, 1)], ("HO", q), ("STAT", q), "LNG", "LNB")
                    em.dma("gpsimd", h_scr[h_row0 + t0 + i * TP: h_row0 + t0 + (i + 1) * TP, :], HO[q][0:TP, :],
                           r=[("HO", q)], w=[("h_scr", (h_row0 + t0 + i * TP) // 32)])

            em.memset("vector", RT[:], 0.0, w=["RT"])
            em.memset("vector", HIST[:], 0.0, w=[("HIST", c) for c in range(4)])
            em.memset("vector", HST[:], 0.0, w=[("HST", c) for c in range(4)])
            UVCH = 512
            uv_jobs = []
            for r0 in range(0, N_EXP, UVCH):
                uv_jobs.append((uv_b[r0:r0 + UVCH, 0:1024], pu[r0:r0 + UVCH, :]))
                uv_jobs.append((uv_b[r0:r0 + UVCH, 1024:2048], pv[r0:r0 + UVCH, :]))
            nblk = T // NB if stop >= 1 else 0
            per_blk = (len(uv_jobs) + max(nblk, 1) - 1) // max(nblk, 1)
            def run_to(gen, tag):
                for t_ in gen:
                    if t_ == tag:
                        return
            gens_p = [process_block(xp, blk * NB, NB, 0, (k_p, v_p, lf_p, 0), blk == T // NB - 1) for blk in range(nblk)]
            if nblk:
                run_to(gens_p[0], "A")
            for blk in range(nblk):
                if do_peer:
                    for (o_, i_) in uv_jobs[blk * per_blk:(blk + 1) * per_blk]:
                        em.dma("gpsimd", o_, i_)
                run_to(gens_p[blk], "E")
                if blk + 1 < nblk:
                    run_to(gens_p[blk + 1], "A")
                run_to(gens_p[blk], None)
            if do_peer and nblk == 0:
                for (o_, i_) in uv_jobs:
                    em.dma("gpsimd", o_, i_)
            for c_ in range(4):
                em.dma("sync", conv_p.rearrange("w (cc p) -> p cc w", p=128)[:, c_, :], HIST[:, c_, :],
                       r=[("HIST", c_)], allow_slow_non_contiguous=True)
            em.dma("sync", rnn_p.rearrange("o (cc p) -> p (o cc)", p=128), HST[:],
                   r=[("HST", c) for c in range(4)], allow_slow_non_contiguous=True)

            NPT = PAST // 128
            for s_ in range(NS if stop >= 7 else 0):
                em.dma("gpsimd", KT[:, :, 0:PAST], ckT[s_], w=[("KT", kt) for kt in range(NPT)])
                for kt in range(NPT):
                    s = nxt("kt", 2)
                    em.dma("sync", VTOK[s][:, :], cv[s_, kt * 128:(kt + 1) * 128, :], w=[("VTOK", s)])
                    vt4 = VTOK[s][:, :].rearrange("p (c t d) -> p c t d", c=4, t=2)
                    em.copy("vector", VA[:, kt, :, 0:64], vt4[:, :, 0, :], r=[("VTOK", s)], w=[("VA", kt)])
                    em.copy("vector", VA[:, kt, :, 128:192], vt4[:, :, 1, :], r=[("VTOK", s)], w=[("VA", kt)])
                em.dma("sync", LF[:, 0:NPT, :], clf[s_].rearrange("(kt p) h -> p kt h", p=128),
                       w=[("LF", kt) for kt in range(NPT)])
                em.memset("vector", RT[:], 0.0, w=["RT"])
                for kt in range(NPT):
                    cumsum_tile(kt, 128)
                em.dma("sync", HIST[:], sconvT[s_], w=[("HIST", c) for c in range(4)])
                em.dma("sync", HST[:], srnnT[s_], w=[("HST", c) for c in range(4)])
                for _ in process_block(xs[s_ * TS:(s_ + 1) * TS, :], 0, TS, PAST,
                                       (k_s[s_ * TS:(s_ + 1) * TS, :], v_s[s_ * TS:(s_ + 1) * TS, :],
                                        lf_s[s_ * TS:(s_ + 1) * TS, :], T + s_ * TS), True):
                    pass
                for c_ in range(4):
                    em.dma("sync", conv_s[s_].rearrange("w (cc p) -> p cc w", p=128)[:, c_, :], HIST[:, c_, :],
                           r=[("HIST", c_)], allow_slow_non_contiguous=True)
                em.dma("sync", rnn_s[s_:s_ + 1, :].rearrange("o (cc p) -> p (o cc)", p=128), HST[:],
                       r=[("HST", c) for c in range(4)], allow_slow_non_contiguous=True)

            p.finish(semstack, "a")

        with contextlib.ExitStack() as st:
            def sb(name, shape, dt=F32):
                return st.enter_context(nc.sbuf_tensor(name, list(shape), dt))

            def ps(name, shape, dt=F32):
                return st.enter_context(nc.psum_tensor(name, list(shape), dt))

            p = Prog(nc)
            em = Em(p)
            ID = sb("ID2", [128, 128])
            IDB = sb("IDB", [128, 128], BF16)
            IOTA = sb("IOTA", [128, 16])
            LNG = sb("LNG2", [128, D_MODEL])
            LNB = sb("LNB2", [128, D_MODEL])
            WQ = sb("WQ", [128, 8, 2048], BF16)
            KEYF = sb("KEYF", [128, 2, 128])
            KEYB = sb("KEYB", [128, 2, 128], BF16)
            H = [sb("H%d" % i, [128, D_MODEL]) for i in range(2)]
            HB = [sb("HB%d" % i, [128, D_MODEL], BF16) for i in range(2)]
            HT = sb("HT", [128, 8, 128], BF16)
            QYT = sb("QYT", [128, 16, 128], BF16)
            SC = sb("SC", [128, 16, 128])
            SC2 = sb("SC2", [128, 16, 128])
            MX = sb("MX", [128, 16, 16])
            MI = sb("MI", [128, 16, 16], U32)
            MIF = sb("MIF", [128, 16, 16])
            CAND = sb("CAND", [128, 8, 256])
            CAND2 = sb("CAND2", [128, 8, 256])
            TOPV = sb("TOPV", [128, 8, 16])
            SEL = sb("SEL", [128, 8, 16], U32)
            SELA = sb("SELA", [128, 8, 16], U32)
            SELB = sb("SELB", [128, 8, 16], U32)
            AF_ = sb("AFl", [128, 8, 16])
            BF_ = sb("BFl", [128, 8, 16])
            EQ = sb("EQ", [128, 8, 16, 16])
            I1S = sb("I1S", [128, 8, 16])
            I2S = sb("I2S", [128, 8, 16])
            IDXF = sb("IDXF", [128, 128])
            IDX = [sb("IDX%d" % i, [128, 128], I32) for i in range(2)]
            GW = [sb("GW%d" % i, [128, 8, 16]) for i in range(2)]
            GS = sb("GS", [128, 8])
            NGRP = 32
            NGA = 4
            GSZ = 128 // NGRP
            ACT = [sb("ACT%d" % i, [128, GSZ]) for i in range(NGA)]
            ACT2 = [sb("ACTb%d" % i, [128, GSZ]) for i in range(NGA)]
            WGT = [sb("WGT%d" % i, [128, GSZ]) for i in range(NGA)]
            NUV = 16
            UV = [sb("UV%d" % i, [128, 2048], BF16) for i in range(NUV)]
            NDG = 4
            DG = [sb("DG%d" % i, [128, 128], BF16) for i in range(NDG)]
            JUNKB = sb("JUNKB", [128, D_MODEL], BF16)
            JUNKC = sb("JUNKC", [128, D_MODEL], BF16)
            NPRD = 4
            PRD = [sb("PRD%d" % i, [128, D_MODEL], BF16) for i in range(NPRD)]
            PRE = [sb("PREb%d" % i, [128, D_MODEL]) for i in range(2)]
            YO = [sb("YO%d" % i, [128, D_MODEL]) for i in range(2)]
            STAT = [sb("STATb%d" % i, [128, 8]) for i in range(2)]

            PG = [ps("PG%d" % i, [128, 512]) for i in range(4)]
            POUT = [ps("POUT%d" % i, [128, 512]) for i in range(4)]
            if os.environ.get("KDEBUG"):
                print("phase2 sbuf remaining", nc.sbuf_bytes_remaining)

            em.dma("sync", ID[:], c_ident, w=["ID"])
            em.copy("vector", IDB[:], ID[:], r=["ID"], w=["IDB"])
            em.dma("sync", IOTA[:], c_iota, w=["IOTA"])
            em.dma("sync", LNG[:], ln2g, w=["LNG"])
            em.dma("sync", LNB[:], ln2b, w=["LNB"])
            em.dma("sync", KEYF[:, 0, :], k1T, w=["KEYF0"])
            em.dma("sync", KEYF[:, 1, :], k2T, w=["KEYF1"])
            em.copy("vector", KEYB[:], KEYF[:], r=["KEYF0", "KEYF1"], w=["KEYB"])
            wq_v = wq_b.rearrange("(kc p) c -> p kc c", p=128)
            for kc in range(8):
                em.dma("sync", WQ[:, kc, :], wq_v[:, kc, :], w=[("WQ", kc)])
            WQK = [("WQ", kc) for kc in range(8)]

            rr2 = {"h": 0, "pq": 0, "pg": 0, "uv": 0, "dg": 0, "pre": 0, "idx": 0, "grp": 0, "prd": 0}

            def nxt2(name, n):
                v = rr2[name] % n
                rr2[name] += 1
                return v

            def prologue(row0, n, ctx):
                hs_ = nxt2("h", 2)
                Ht = H[hs_]
                Hb = HB[hs_]
                em.dma("sync", Ht[0:n, :], h_scr[row0:row0 + n, :], w=[("H", hs_)])
                em.copy("scalar", Hb[0:n, :], Ht[0:n, :], r=[("H", hs_)], w=[("HB", hs_)])
                for half in range(2):
                    z = nxt2("pg", 4)
                    for j in range(4):
                        kc = half * 4 + j
                        em.tr(PG[z][:, j * n:(j + 1) * n], Ht[0:n, kc * 128:(kc + 1) * 128], ID[0:n, 0:n],
                              r=[("H", hs_), "ID"], w=[("PG", z)])
                    em.copy("scalar", HT[:, half * 4:(half + 1) * 4, 0:n],
                            PG[z][:, 0:4 * n].rearrange("p (j t) -> p j t", j=4), r=[("PG", z)], w=["HT"])
                yield
                for c4 in range(4):
                    if c4 > 0:
                        yield
                    z = nxt2("pg", 4)
                    for cj in range(4):
                        c = c4 * 4 + cj
                        for kc in range(8):
                            em.mm(PG[z][:, cj * n:(cj + 1) * n], WQ[:, kc, c * 128:(c + 1) * 128], HT[:, kc, 0:n],
                                  kc == 0, kc == 7, r=WQK + ["HT"], w=[("PG", z)])
                    em.copy("scalar", QYT[:, c4 * 4:(c4 + 1) * 4, 0:n],
                            PG[z][:, 0:4 * n].rearrange("p (j t) -> p j t", j=4), r=[("PG", z)], w=[("QYT", c4)])
                yield
                for c4 in range(4):
                    z = nxt2("pg", 4)
                    for cj in range(4):
                        c = c4 * 4 + cj
                        em.mm(PG[z][0:n, cj * 128:(cj + 1) * 128], QYT[:, c, 0:n], KEYB[:, c % 2, :], True, True,
                              r=[("QYT", c4), "KEYB"], w=[("PG", z)])
                    em.copy("vector", SC[0:n, c4 * 4:(c4 + 1) * 4, :],
                            PG[z][0:n, :].rearrange("p (j k) -> p j k", j=4), r=[("PG", z)], w=[("SC", c4)])
                for c in range(16):
                    if c % 2 == 0:
                        yield
                    ck = ("SC", c // 4)
                    em.vmax(MX[0:n, c, 0:8], SC[0:n, c, :], r=[ck], w=[("MX", c)])
                    em.vmaxidx(MI[0:n, c, 0:8], MX[0:n, c, 0:8], SC[0:n, c, :], r=[ck, ("MX", c)], w=[("MI", c)])
                    em.vmatchrep(SC2[0:n, c, :], MX[0:n, c, 0:8], SC[0:n, c, :], -1e30, r=[ck, ("MX", c)], w=[("SC2", c)])
                    em.vmax(MX[0:n, c, 8:16], SC2[0:n, c, :], r=[("SC2", c)], w=[("MXb", c)])
                    em.vmaxidx(MI[0:n, c, 8:16], MX[0:n, c, 8:16], SC2[0:n, c, :], r=[("SC2", c), ("MXb", c)], w=[("MIb", c)])
                MXK = [("MX", c) for c in range(16)] + [("MXb", c) for c in range(16)]
                MIK = [("MI", c) for c in range(16)] + [("MIb", c) for c in range(16)]
                em.copy("vector", MIF[0:n], MI[0:n], r=MIK, w=["MIF"])
                yield
                MXv = MX[0:n].rearrange("p (h t) k -> p h t k", t=2)
                em.tt("vector", CAND[0:n].rearrange("p h (a b) -> p h a b", a=16),
                      MXv[:, :, 0, :].unsqueeze(3).to_broadcast([n, 8, 16, 16]),
                      MXv[:, :, 1, :].unsqueeze(2).to_broadcast([n, 8, 16, 16]), ALU.add, r=MXK, w=["CAND"])
                for h in range(8):
                    yield
                    em.vmax(TOPV[0:n, h, 0:8], CAND[0:n, h, :], r=["CAND"], w=[("TV", h)])
                    em.vmaxidx(SEL[0:n, h, 0:8], TOPV[0:n, h, 0:8], CAND[0:n, h, :], r=["CAND", ("TV", h)], w=[("SEL", h)])
                    em.vmatchrep(CAND2[0:n, h, :], TOPV[0:n, h, 0:8], CAND[0:n, h, :], -1e30, r=["CAND", ("TV", h)], w=[("C2", h)])
                    em.vmax(TOPV[0:n, h, 8:16], CAND2[0:n, h, :], r=[("C2", h)], w=[("TVb", h)])
                    em.vmaxidx(SEL[0:n, h, 8:16], TOPV[0:n, h, 8:16], CAND2[0:n, h, :], r=[("C2", h), ("TVb", h)], w=[("SELb", h)])
                TVK = [("TV", h) for h in range(8)] + [("TVb", h) for h in range(8)]
                SELK = [("SEL", h) for h in range(8)] + [("SELb", h) for h in range(8)]
                yield
                em.tss("vector", SELB[0:n], SEL[0:n], 15, ALU.bitwise_and, r=SELK, w=["SELB"])
                em.tss("vector", SELA[0:n], SEL[0:n], 4, ALU.logical_shift_right, r=SELK, w=["SELA"])
                em.copy("vector", AF_[0:n], SELA[0:n], r=["SELA"], w=["AFl"])
                em.copy("vector", BF_[0:n], SELB[0:n], r=["SELB"], w=["BFl"])
                yield
                MIFv = MIF[0:n].rearrange("p (h t) k -> p h t k", t=2)
                iota_b = IOTA[0:n, :].unsqueeze(1).unsqueeze(1).to_broadcast([n, 8, 16, 16])
                for (sel_f, tsel, dst, dk) in ((AF_, 0, I1S, "I1S"), (BF_, 1, I2S, "I2S")):
                    em.tt("vector", EQ[0:n], iota_b, sel_f[0:n].unsqueeze(3).to_broadcast([n, 8, 16, 16]), ALU.is_equal,
                          r=["IOTA", "AFl", "BFl"], w=["EQ"])
                    yield
                    em.tt("vector", EQ[0:n], EQ[0:n], MIFv[:, :, tsel, :].unsqueeze(2).to_broadcast([n, 8, 16, 16]), ALU.mult,
                          r=["EQ", "MIF"], w=["EQ"])
                    yield
                    em.reduce("vector", dst[0:n], EQ[0:n], ALU.add, r=["EQ"], w=[dk])
                    yield
                ix = nxt2("idx", 2)
                em.stt("vector", IDXF[0:n, :], I1S[0:n].rearrange("p h k -> p (h k)"), 128.0,
                       I2S[0:n].rearrange("p h k -> p (h k)"), ALU.mult, ALU.add, r=["I1S", "I2S"], w=["IDXF"])
                em.copy("vector", IDX[ix][0:n, :], IDXF[0:n, :], r=["IDXF"], w=[("IDX", ix)])
                yield
                GWt = GW[ix]
                gwk = ("GW", ix)
                em.tt("vector", GWt[0:n], TOPV[0:n], TOPV[0:n, :, 0:1].to_broadcast([n, 8, 16]), ALU.subtract, r=TVK, w=[gwk])
                em.act(GWt[0:n], GWt[0:n], AF.Exp, r=[gwk], w=[gwk])
                em.reduce("vector", GS[0:n], GWt[0:n], ALU.add, r=[gwk], w=["GS"])
                em.recip(GS[0:n], GS[0:n], r=["GS"], w=["GS"])
                em.tt("vector", GWt[0:n], GWt[0:n], GS[0:n].unsqueeze(2).to_broadcast([n, 8, 16]), ALU.mult, r=[gwk, "GS"], w=[gwk])
                ctx.update(hs_=hs_, ix=ix, gwk=gwk, GWt=GWt)

            grp_bufs = {}

            def emit_dots_dve(t, n, ctx, grp):
                hs_, ix = ctx["hs_"], ctx["ix"]
                Hb = HB[hs_]
                ga = grp % NGA
                A_ = ACT[ga]
                bufs = []
                pend = []
                for k in range(GSZ):
                    s = grp * GSZ + k
                    g = nxt2("uv", NUV)
                    bufs.append(g)
                    em.gather(UV[g][0:n, :], uv_b, IDX[ix][0:n, s:s + 1], r=[("IDX", ix)], w=[("UV", g)])
                    if k % 4 == 0:
                        em.stt("vector", JUNKB[0:n, :], UV[g][0:n, 0:1024], 1.0, Hb[0:n, :], ALU.mult, ALU.mult,
                               r=[("UV", g), ("HB", hs_)], w=["JUNKB", ("ACT", ga, k)], accum=A_[0:n, k:k + 1])
                    else:
                        pj = nxt2("prd", NPRD)
                        em.tt("vector", PRD[pj][0:n, :], UV[g][0:n, 0:1024], Hb[0:n, :], ALU.mult,
                              r=[("UV", g), ("HB", hs_)], w=[("PRD", pj)])
                        pend.append((pj, k))
                grp_bufs[(t, grp)] = (bufs, pend)

            def emit_dots_act(t, n, ctx, grp):
                ga = grp % NGA
                A_ = ACT[ga]
                for (pj, k) in grp_bufs[(t, grp)][1]:
                    em.act(JUNKC[0:n, :], PRD[pj][0:n, :], AF.Identity, accum=A_[0:n, k:k + 1],
                           r=[("PRD", pj)], w=["JUNKC", ("ACT", ga, k)])

            def emit_tail_a(t, n, ctx, grp):
                gwk, GWt = ctx["gwk"], ctx["GWt"]
                GWf = GWt[0:n].rearrange("p h k -> p (h k)")
                ga = grp % NGA
                A_, A2_, W_ = ACT[ga], ACT2[ga], WGT[ga]
                AK = [("ACT", ga, k) for k in range(GSZ)]
                em.act(A2_[0:n], A_[0:n], AF.Gelu_apprx_tanh, r=AK, w=[("ACT2", ga)])
                em.tt("vector", W_[0:n], A2_[0:n], GWf[:, grp * GSZ:(grp + 1) * GSZ], ALU.mult,
                      r=[("ACT2", ga), gwk], w=[("WGT", ga)])

            def emit_tail_b(t, n, ctx, grp):
                ga = grp % NGA
                W_ = WGT[ga]
                bufs = grp_bufs.pop((t, grp))[0]
                po = (t % 2) * 2
                for k in range(GSZ):
                    s = grp * GSZ + k
                    g = bufs[k]
                    d = nxt2("dg", NDG)
                    em.act(DG[d][0:n, 0:n], IDB[0:n, 0:n], AF.Copy, scale=W_[0:n, k:k + 1],
                           r=["IDB", ("WGT", ga)], w=[("DG", d)])
                    for hf in range(2):
                        em.mm(POUT[po + hf][0:n, :], DG[d][0:n, 0:n], UV[g][0:n, 1024 + hf * 512:1024 + (hf + 1) * 512],
                              s == 0, s == 127, r=[("DG", d), ("UV", g)], w=[("POUT", po + hf)])

            def epilogue(t, n, ctx, y_dst):
                po = (t % 2) * 2
                hs_ = ctx["hs_"]
                Ht = H[hs_]
                q = nxt2("pre", 2)
                X, Y, ST = PRE[q], YO[q], STAT[q]
                xkeys = [("PRE", q, 0), ("PRE", q, 1)]
                ykey, stkey = ("YO", q), ("STAT", q)
                for hf in range(2):
                    em.stt("vector", X[0:n, hf * 512:(hf + 1) * 512], Ht[0:n, hf * 512:(hf + 1) * 512],
                           ALPHA, POUT[po + hf][0:n, :], ALU.mult, ALU.add, r=[("H", hs_), ("POUT", po + hf)], w=[("PRE", q, hf)])
                    yield
                em.act(Y[0:n, :], X[0:n, :], AF.Identity, accum=ST[0:n, 0:1], r=xkeys, w=[ykey, (stkey, 0)])
                yield
                em.act(Y[0:n, :], X[0:n, :], AF.Square, accum=ST[0:n, 1:2], r=xkeys, w=[ykey, (stkey, 1)])
                yield
                em.ts("vector", ST[0:n, 2:3], ST[0:n, 0:1], 1.0 / D_MODEL, None, ALU.mult, r=[(stkey, 0)], w=[(stkey, 2)])
                em.ts("vector", ST[0:n, 3:4], ST[0:n, 1:2], 1.0 / D_MODEL, None, ALU.mult, r=[(stkey, 1)], w=[(stkey, 3)])
                em.tt("vector", ST[0:n, 4:5], ST[0:n, 2:3], ST[0:n, 2:3], ALU.mult, r=[(stkey, 2)], w=[(stkey, 4)])
                em.tt("vector", ST[0:n, 5:6], ST[0:n, 3:4], ST[0:n, 4:5], ALU.subtract, r=[(stkey, 3), (stkey, 4)], w=[(stkey, 5)])
                em.ts("vector", ST[0:n, 6:7], ST[0:n, 5:6], LN_EPS, None, ALU.add, r=[(stkey, 5)], w=[(stkey, 6)])
                yield
                em.act(ST[0:n, 6:7], ST[0:n, 6:7], AF.Ln, r=[(stkey, 6)], w=[(stkey, 6)])
                em.act(ST[0:n, 7:8], ST[0:n, 6:7], AF.Exp, scale=-0.5, r=[(stkey, 6)], w=[(stkey, 7)])
                yield
                em.ts("vector", Y[0:n, :], X[0:n, :], ST[0:n, 2:3], ST[0:n, 7:8], ALU.subtract, ALU.mult,
                      r=xkeys + [(stkey, 2), (stkey, 7)], w=[ykey])
                yield
                em.tt("vector", Y[0:n, :], Y[0:n, :], LNG[0:n, :], ALU.mult, r=[ykey, "LNG"], w=[ykey])
                yield
                em.tt("vector", Y[0:n, :], Y[0:n, :], LNB[0:n, :], ALU.add, r=[ykey, "LNB"], w=[ykey])
                em.dma("sync", y_dst, Y[0:n, :], r=[ykey])

            if do_peer:
                tiles = [(ti * 128, 128, y_p[ti * 128:(ti + 1) * 128, :]) for ti in range(T // 128)]
                if NSR > 0:
                    tiles.append((T, NSR, y_s[:, :]))
                ctxs = [dict() for _ in tiles]
                for _ in prologue(tiles[0][0], tiles[0][1], ctxs[0]):
                    pass
                items = [(t, grp) for t in range(len(tiles)) for grp in range(NGRP)]
                gens = {}
                epi = {"gen": None}
                emit_dots_dve(0, tiles[0][1], ctxs[0], 0)
                emit_dots_act(0, tiles[0][1], ctxs[0], 0)
                for i, (t, grp) in enumerate(items):
                    row0, n, y_dst = tiles[t]
                    if grp == 0 and t + 1 < len(tiles):
                        gens[t + 1] = prologue(tiles[t + 1][0], tiles[t + 1][1], ctxs[t + 1])
                    nx = items[i + 1] if i + 1 < len(items) else None
                    if nx is not None and nx[0] != t and (nx[0] in gens):
                        for _ in gens.pop(nx[0]):
                            pass
                    emit_tail_a(t, n, ctxs[t], grp)
                    if nx is not None:
                        emit_dots_dve(nx[0], tiles[nx[0]][1], ctxs[nx[0]], nx[1])
                    emit_tail_b(t, n, ctxs[t], grp)
                    if nx is not None:
                        emit_dots_act(nx[0], tiles[nx[0]][1], ctxs[nx[0]], nx[1])
                    if (t + 1) in gens:
                        try:
                            next(gens[t + 1])
                        except StopIteration:
                            gens.pop(t + 1)
                    if epi["gen"] is not None:
                        try:
                            next(epi["gen"])
                        except StopIteration:
                            epi["gen"] = None
                    if grp == NGRP - 1:
                        if epi["gen"] is not None:
                            for _ in epi["gen"]:
                                pass
                        epi["gen"] = epilogue(t, n, ctxs[t], y_dst)
                        next(epi["gen"])
                        next(epi["gen"])
                if epi["gen"] is not None:
                    for _ in epi["gen"]:
                        pass
            p.finish(semstack, "b")
    return nc


def layer_norm(em, X, Y, JUNK, ST, G, B, n, xkeys, ykey, stkey, gk, bk, gb_eng="gpsimd"):
    jk = "JUNK"
    if JUNK is None:
        JUNK, jk = Y, ykey
    em.act(JUNK[0:n, :], X[0:n, :], AF.Identity, accum=ST[0:n, 0:1], r=xkeys, w=[jk, (stkey, 0)])
    em.act(JUNK[0:n, :], X[0:n, :], AF.Square, accum=ST[0:n, 1:2], r=xkeys, w=[jk, (stkey, 1)])
    em.ts("vector", ST[0:n, 2:3], ST[0:n, 0:1], 1.0 / D_MODEL, None, ALU.mult, r=[(stkey, 0)], w=[(stkey, 2)])
    em.ts("vector", ST[0:n, 3:4], ST[0:n, 1:2], 1.0 / D_MODEL, None, ALU.mult, r=[(stkey, 1)], w=[(stkey, 3)])
    em.tt("vector", ST[0:n, 4:5], ST[0:n, 2:3], ST[0:n, 2:3], ALU.mult, r=[(stkey, 2)], w=[(stkey, 4)])
    em.tt("vector", ST[0:n, 5:6], ST[0:n, 3:4], ST[0:n, 4:5], ALU.subtract, r=[(stkey, 3), (stkey, 4)], w=[(stkey, 5)])
    em.ts("vector", ST[0:n, 6:7], ST[0:n, 5:6], LN_EPS, None, ALU.add, r=[(stkey, 5)], w=[(stkey, 6)])
    em.act(ST[0:n, 6:7], ST[0:n, 6:7], AF.Ln, r=[(stkey, 6)], w=[(stkey, 6)])
    em.act(ST[0:n, 7:8], ST[0:n, 6:7], AF.Exp, scale=-0.5, r=[(stkey, 6)], w=[(stkey, 7)])
    em.ts("vector", Y[0:n, :], X[0:n, :], ST[0:n, 2:3], ST[0:n, 7:8], ALU.subtract, ALU.mult,
          r=xkeys + [(stkey, 2), (stkey, 7)], w=[ykey])
    em.tt(gb_eng, Y[0:n, :], Y[0:n, :], G[0:n, :], ALU.mult, r=[ykey, gk], w=[ykey])
    em.tt(gb_eng, Y[0:n, :], Y[0:n, :], B[0:n, :], ALU.add, r=[ykey, bk], w=[ykey])


def _fm(v):
    return np.ascontiguousarray(np.asarray(v, np.float32).reshape(4, 128).T)


def _blockdiag(w):
    out = np.zeros((128, 4, 128), np.float32)
    for cc in range(4):
        out[0:64, cc, 0:64] = w[2 * cc]
        out[64:128, cc, 64:128] = w[2 * cc + 1]
    return out


def make_consts(NB):
    NJ = max(1, NB // 128)
    ident = np.eye(128, dtype=np.float32)
    tri = np.triu(np.ones((128, 128), np.float32))
    k = np.arange(128)[:, None]
    q = np.arange(NB)[None, :]
    mask = np.stack([(q >= j * 128 + k).astype(np.float32) for j in range(NJ)], axis=1)
    iota = np.broadcast_to(np.arange(16, dtype=np.float32), (128, 16)).copy()
    return dict(c_ident=ident, c_tri=tri, c_mask=np.ascontiguousarray(mask), c_iota=iota)


def shared_maps(inp, NB):
    f = lambda a: np.ascontiguousarray(np.asarray(a, np.float32))
    rep = lambda v: np.ascontiguousarray(np.broadcast_to(np.asarray(v, np.float32).reshape(1, -1), (128, np.asarray(v).size)))
    m = dict(
        w_in=f(inp["w_in"][0]), bF=rep(inp["b_forget"][0]),
        cw=np.ascontiguousarray(np.asarray(inp["conv_w"][0], np.float32).reshape(4, 4, 128).transpose(2, 1, 0)),
        cb=_fm(inp["conv_b"][0]), wa=_blockdiag(np.asarray(inp["w_rg_a"][0], np.float32)), ba=_fm(inp["b_rg_a"][0]),
        wx=_blockdiag(np.asarray(inp["w_rg_x"][0], np.float32)), bx=_fm(inp["b_rg_x"][0]), lam=_fm(inp["lru_lambda"][0]),
        w_au=f(inp["w_attn_up"][0]), w_ru=f(inp["w_rnn_up"][0]), w_out=f(inp["w_out"][0]),
        ln1g=rep(inp["ln1_g"][0]), ln1b=rep(inp["ln1_b"][0]), wq=f(inp["peer_w_query"][0]),
        k1T=f(np.asarray(inp["peer_keys_1"][0]).T), k2T=f(np.asarray(inp["peer_keys_2"][0]).T),
        pu=f(inp["peer_u"][0]), pv=f(inp["peer_v"][0]), ln2g=rep(inp["ln2_g"][0]), ln2b=rep(inp["ln2_b"][0]),
    )
    m.update(make_consts(NB))
    return m


def core_map(inp, c, NS, shared):
    f = lambda a: np.ascontiguousarray(np.asarray(a, np.float32))
    sl = slice(c * NS, (c + 1) * NS)
    ck = np.asarray(inp["cache_k"][0][sl], np.float32)
    ns, past = ck.shape[0], ck.shape[1]
    ckT = ck.reshape(ns, past, 4, 2, 64).transpose(0, 3, 4, 2, 1).reshape(ns, 128, 4, past)
    sconv = np.asarray(inp["state_conv"][0][sl], np.float32)
    sconvT = sconv.reshape(ns, 3, 4, 128).transpose(0, 3, 2, 1)
    srnn = np.asarray(inp["state_rnn"][0][sl], np.float32)
    srnnT = srnn.reshape(ns, 4, 128).transpose(0, 2, 1)
    xs = np.asarray(inp["x_sample"][sl], np.float32)
    m = dict(shared)
    m.update(
        xp=f(inp["x_prompt"][c]), xs=f(xs.reshape(-1, D_MODEL)), ckT=f(ckT),
        cv=f(np.asarray(inp["cache_v"][0][sl]).reshape(ns, past, 512)),
        clf=f(inp["cache_logf"][0][sl]), sconvT=f(sconvT), srnnT=f(srnnT),
    )
    return m


def kernel(**inp):
    NCORES = 8
    T = inp["x_prompt"].shape[1]
    NS = inp["x_sample"].shape[0] // NCORES
    TS = inp["x_sample"].shape[1]
    PAST = inp["cache_k"].shape[2]
    NB = 256
    nc = build(T=T, NS=NS, TS=TS, PAST=PAST, NB=NB)
    shared = shared_maps(inp, NB)
    in_maps = [core_map(inp, c, NS, shared) for c in range(NCORES)]
    res = run_bass_kernel_spmd(nc, in_maps, core_ids=list(range(NCORES)))
    R = res.results
    B = NCORES
    cat = lambda k: np.stack([np.asarray(R[c][k], np.float32) for c in range(B)], 0)
    y_p = cat("y_p")
    y_s = cat("y_s").reshape(B * NS, TS, D_MODEL)
    k_p = cat("k_p").reshape(1, B, T, 8, 64)
    v_p = cat("v_p").reshape(1, B, T, 8, 64)
    lf_p = cat("lf_p").reshape(1, B, T, 8)
    conv_p = cat("conv_p").reshape(1, B, 3, 512)
    rnn_p = cat("rnn_p").reshape(1, B, 512)
    k_s = cat("k_s").reshape(1, B * NS, TS, 8, 64)
    v_s = cat("v_s").reshape(1, B * NS, TS, 8, 64)
    lf_s = cat("lf_s").reshape(1, B * NS, TS, 8)
    conv_s = cat("conv_s").reshape(1, B * NS, 3, 512)
    rnn_s = cat("rnn_s").reshape(1, B * NS, 512)
    return (y_p, y_s, k_p, v_p, lf_p, conv_p, rnn_p, k_s, v_s, lf_s, conv_s, rnn_s)
```
